# Optimizing a Trainium2 kernel written in Bass

```python
import jax, jax.numpy as jnp
from jax import lax
import numpy as np

D_MODEL = 1024
BATCH = 4
SEQ = 8192
DEPTH = 1

HEAD_DIM = 64
ATTN_HEADS_PER_GROUP = 8
DILATED_GROUPS = ((128, 1), (512, 4), (2048, 16))
N_DIL = len(DILATED_GROUPS)
ATTN_WIDTH = ATTN_HEADS_PER_GROUP * HEAD_DIM
ROPE_DIM = HEAD_DIM // 4
ROPE_THETA = 500000.0
BLK = 128
SGU_CHUNK = 128
SGU_GROUPS = 8
SGU_WIDTH = D_MODEL // 2
SGU_GROUP_DIM = SGU_WIDTH // SGU_GROUPS
D_FF = -(-8 * D_MODEL // (3 * 256)) * 256
QKV_COLS = 3 * N_DIL * ATTN_WIDTH
IN_COLS = QKV_COLS + 2 * SGU_WIDTH + 2 * D_MODEL
EPS = 1e-6

kernel_name = "hybrid_dilated_attn_gmlp_gated_block"


def rmsnorm(x, g):
    xf = x.astype(jnp.float32)
    y = xf * lax.rsqrt(jnp.mean(xf * xf, axis=-1, keepdims=True) + EPS)
    return (y * g.astype(jnp.float32)).astype(x.dtype)


def layernorm(x, g, b):
    xf = x.astype(jnp.float32)
    mu = jnp.mean(xf, axis=-1, keepdims=True)
    xc = xf - mu
    y = xc * lax.rsqrt(jnp.mean(xc * xc, axis=-1, keepdims=True) + EPS)
    return (y * g.astype(jnp.float32) + b.astype(jnp.float32)).astype(x.dtype)


def partial_rope(t, positions):
    half = ROPE_DIM // 2
    inv_freq = ROPE_THETA ** (-jnp.arange(0, ROPE_DIM, 2, dtype=jnp.float32) / ROPE_DIM)
    ang = positions.astype(jnp.float32)[..., None] * inv_freq
    cos = jnp.cos(ang)[:, :, None, :]
    sin = jnp.sin(ang)[:, :, None, :]
    tf = t.astype(jnp.float32)
    x1, x2 = tf[..., :half], tf[..., half:ROPE_DIM]
    rot = jnp.concatenate([x1 * cos - x2 * sin, x2 * cos + x1 * sin, tf[..., ROPE_DIM:]], axis=-1)
    return rot.astype(t.dtype)


def dilated_attention(q, k, v, window, dilation):
    B, S, H, Dh = q.shape
    span = window // dilation
    L = -(-S // dilation)
    L_pad = -(-L // BLK) * BLK
    S_pad = L_pad * dilation
    nb = L_pad // BLK
    pad = ((0, 0), (0, S_pad - S), (0, 0), (0, 0))

    def strided(t):
        t = jnp.pad(t, pad).reshape(B, L_pad, dilation, H, Dh).transpose(0, 2, 1, 3, 4)
        return t.reshape(B, dilation, nb, BLK, H, Dh)

    def with_prev(t):
        prev = jnp.pad(t, ((0, 0), (0, 0), (1, 0), (0, 0), (0, 0), (0, 0)))[:, :, :-1]
        return jnp.concatenate([prev, t], axis=3)

    qs = strided(q * (Dh ** -0.5))
    kb = with_prev(strided(k))
    vb = with_prev(strided(v))
    s = jnp.einsum('brnqhd,brnkhd->brnhqk', qs, kb, preferred_element_type=jnp.float32)

    i = jnp.arange(BLK)[:, None]
    j = jnp.arange(2 * BLK)[None, :]
    diff = BLK + i - j
    band = (diff >= 0) & (diff <= span)
    key_exists = (jnp.arange(nb)[:, None, None] > 0) | (j >= BLK)[None]
    mask = band[None] & key_exists
    s = jnp.where(mask[None, None, :, None], s, -jnp.inf)

    m = jnp.max(s, axis=-1, keepdims=True)
    p = jnp.exp(s - m)
    den = jnp.sum(p, axis=-1)
    lse = m[..., 0] + jnp.log(den)
    o = jnp.einsum('brnhqk,brnkhd->brnqhd', p, vb.astype(jnp.float32))
    o = o / jnp.swapaxes(den, -1, -2)[..., None]

    o = o.reshape(B, dilation, L_pad, H, Dh).transpose(0, 2, 1, 3, 4).reshape(B, S_pad, H, Dh)[:, :S]
    lse = jnp.swapaxes(lse, -1, -2).reshape(B, dilation, L_pad, H).transpose(0, 2, 1, 3)
    lse = lse.reshape(B, S_pad, H)[:, :S]
    return o.astype(q.dtype), lse


def spatial_gating(uv, ln_g, ln_b, w_s, b_s):
    B, S, _ = uv.shape
    z = jax.nn.gelu(uv, approximate=False)
    u, v = z[..., :SGU_WIDTH], z[..., SGU_WIDTH:]
    v = layernorm(v, ln_g, ln_b)
    vc = v.reshape(B, S // SGU_CHUNK, SGU_CHUNK, SGU_GROUPS, SGU_GROUP_DIM)
    causal = jnp.tril(jnp.ones((SGU_CHUNK, SGU_CHUNK), dtype=bool))
    w_causal = jnp.where(causal[None], w_s, jnp.zeros_like(w_s))
    mixed = jnp.einsum('gts,bnsgc->bntgc', w_causal, vc)
    mixed = mixed + jnp.transpose(b_s)[None, None, :, :, None]
    return u * mixed.reshape(B, S, SGU_WIDTH)


def setup_inputs(seed: int = 0) -> dict:
    key = jax.random.key(seed)
    ks = jax.random.split(key, 18)
    f32 = jnp.float32
    x = jax.random.normal(ks[0], (BATCH, SEQ, D_MODEL), f32)
    offset = jax.random.randint(ks[1], (BATCH, 1), 0, 4096, dtype=jnp.int32)
    positions = offset + jnp.arange(SEQ, dtype=jnp.int32)[None, :]
    nrm = lambda k, shape, fan_in: jax.random.normal(k, shape, f32) * (fan_in ** -0.5)
    return {
        "x": x,
        "positions": positions,
        "norm1_g": 1.0 + 0.02 * jax.random.normal(ks[2], (DEPTH, D_MODEL), f32),
        "w_in": nrm(ks[3], (DEPTH, D_MODEL, IN_COLS), D_MODEL),
        "sgu_ln_g": 1.0 + 0.02 * jax.random.normal(ks[4], (DEPTH, SGU_WIDTH), f32),
        "sgu_ln_b": 0.02 * jax.random.normal(ks[5], (DEPTH, SGU_WIDTH), f32),
        "w_spatial": nrm(ks[6], (DEPTH, SGU_GROUPS, SGU_CHUNK, SGU_CHUNK), SGU_CHUNK),
        "b_spatial": 1.0 + 0.1 * jax.random.normal(ks[7], (DEPTH, SGU_GROUPS, SGU_CHUNK), f32),
        "w_proj_attn": nrm(ks[8], (DEPTH, ATTN_WIDTH, D_MODEL), ATTN_WIDTH),
        "w_proj_sgu": nrm(ks[9], (DEPTH, SGU_WIDTH, D_MODEL), SGU_WIDTH),
        "w_out": nrm(ks[10], (DEPTH, D_MODEL, D_MODEL), D_MODEL),
        "norm2_g": 1.0 + 0.02 * jax.random.normal(ks[11], (DEPTH, D_MODEL), f32),
        "w_ffn_gate": nrm(ks[12], (DEPTH, D_MODEL, D_FF), D_MODEL),
        "w_ffn_up": nrm(ks[13], (DEPTH, D_MODEL, D_FF), D_MODEL),
        "w_ffn_down": nrm(ks[14], (DEPTH, D_FF, D_MODEL), D_FF),
        "final_g": 1.0 + 0.02 * jax.random.normal(ks[15], (D_MODEL,), f32),
    }


def reference(x, positions, norm1_g, w_in, sgu_ln_g, sgu_ln_b, w_spatial, b_spatial,
              w_proj_attn, w_proj_sgu, w_out, norm2_g, w_ffn_gate, w_ffn_up, w_ffn_down,
              final_g):
    B, S, _ = x.shape
    for l in range(DEPTH):
        h = rmsnorm(x, norm1_g[l])
        proj = h @ w_in[l]
        qkv = proj[..., :QKV_COLS].reshape(B, S, 3, N_DIL, ATTN_HEADS_PER_GROUP, HEAD_DIM)
        uv = proj[..., QKV_COLS:QKV_COLS + 2 * SGU_WIDTH]
        gate_a = jax.nn.sigmoid(proj[..., QKV_COLS + 2 * SGU_WIDTH:QKV_COLS + 2 * SGU_WIDTH + D_MODEL])
        gate_b = jax.nn.sigmoid(proj[..., QKV_COLS + 2 * SGU_WIDTH + D_MODEL:])

        outs, lses = [], []
        for g, (window, dilation) in enumerate(DILATED_GROUPS):
            q = partial_rope(qkv[:, :, 0, g], positions)
            k = partial_rope(qkv[:, :, 1, g], positions)
            o, lse = dilated_attention(q, k, qkv[:, :, 2, g], window, dilation)
            outs.append(o)
            lses.append(lse)
        alpha = jax.nn.softmax(jnp.stack(lses, axis=0), axis=0)
        attn = jnp.sum(alpha[..., None].astype(x.dtype) * jnp.stack(outs, axis=0), axis=0)
        attn = attn.reshape(B, S, ATTN_WIDTH)

        sgu = spatial_gating(uv, sgu_ln_g[l], sgu_ln_b[l], w_spatial[l], b_spatial[l])

        merged = gate_a * (attn @ w_proj_attn[l]) + gate_b * (sgu @ w_proj_sgu[l])
        x = x + merged @ w_out[l]

        h2 = rmsnorm(x, norm2_g[l])
        ff = jax.nn.silu(h2 @ w_ffn_gate[l]) * (h2 @ w_ffn_up[l])
        x = x + ff @ w_ffn_down[l]
    return rmsnorm(x, final_g)
```

```python
import contextlib
import os
import numpy as np
import concourse.bass as bass
import concourse.mybir as mybir
from concourse.bass_utils import run_bass_kernel_spmd

F32 = mybir.dt.float32
BF16 = mybir.dt.bfloat16
I32 = mybir.dt.int32
AF = mybir.ActivationFunctionType
ALU = mybir.AluOpType

D = 1024
SEQ = 8192
NCORE = 8
OWN = 4096
HALO = 2048
T = 512
NCH = (OWN + HALO) // T
FIRST_OWN = HALO // T
DFF = 2816
NFT = DFF // 128
EPS = 1e-6
ROPE_THETA = 500000.0
NSLAB = 37
NEG = -30000.0


class Res:
    __slots__ = ("name", "last_w", "readers")

    def __init__(self, name):
        self.name = name
        self.last_w = None
        self.readers = {}


class Sched:
    ENG = ("pe", "act", "dve", "pool", "sp")

    def __init__(self, nc, stack):
        self.nc = nc
        self.stack = stack
        self.prog = {e: [] for e in self.ENG}
        self.sem = {}
        self.cnt = {}
        self.waited = {e: {} for e in self.ENG}
        for e in self.ENG:
            self.new_sem("c_" + e)

    def new_sem(self, name):
        s = self.stack.enter_context(self.nc.semaphore(name))
        self.sem[name] = s
        self.cnt[name] = 0
        return name

    def _deps(self, eng, reads, writes, deps):
        toks = {}

        def add(t):
            if t is None:
                return
            s, v = t
            if toks.get(s, 0) < v:
                toks[s] = v
        for t in deps:
            add(t)
        for r in reads:
            add(r.last_w)
        for r in writes:
            add(r.last_w)
            for s, v in r.readers.items():
                add((s, v))
        own = "c_" + eng
        for s, v in toks.items():
            if s == own and eng == "pe":
                continue
            if self.waited[eng].get(s, 0) < v:
                self.prog[eng].append(("wait", s, v))
                self.waited[eng][s] = v

    def op(self, eng, fn, reads=(), writes=(), deps=(), sem=None, inc=1):
        self._deps(eng, reads, writes, deps)
        s = sem or ("c_" + eng)
        self.cnt[s] += inc
        tok = (s, self.cnt[s])
        self.prog[eng].append(("inst", fn, s, inc))
        for r in reads:
            if r.readers.get(s, 0) < tok[1]:
                r.readers[s] = tok[1]
        for r in writes:
            r.last_w = tok
            r.readers = {}
        return tok

    def dma(self, eng, sem, out, in_, reads=(), writes=(), deps=()):
        def fn(h):
            return h.dma_start(out=out, in_=in_)
        return self.op(eng, fn, reads=reads, writes=writes, deps=deps, sem=sem, inc=16)

    def wait_all(self, eng, toks):
        for t in toks:
            if t is None:
                continue
            s, v = t
            if self.waited[eng].get(s, 0) < v:
                self.prog[eng].append(("wait", s, v))
                self.waited[eng][s] = v

    def fence(self, src, dst):
        for d in dst:
            for r in src:
                if r.last_w is not None:
                    s, v = r.last_w
                    if d.readers.get(s, 0) < v:
                        d.readers[s] = v
                for s, v in r.readers.items():
                    if d.readers.get(s, 0) < v:
                        d.readers[s] = v

    def emit(self):
        nc = self.nc
        with nc.Block() as block:
            def run(e):
                def body(h):
                    for it in self.prog[e]:
                        if it[0] == "wait":
                            h.wait_ge(self.sem[it[1]], it[2])
                        else:
                            ins = it[1](h)
                            ins.then_inc(self.sem[it[2]], it[3])
                return body
            block.tensor(run("pe"))
            block.scalar(run("act"))
            block.vector(run("dve"))
            block.gpsimd(run("pool"))
            block.sync(run("sp"))


def slab_src(wd, s):
    if s < 15:
        return wd["w_in"][:, s * 512:(s + 1) * 512].rearrange("(k p) c -> p k c", p=128), 8, 512
    if s == 15:
        return wd["wpa"].rearrange("(k p) c -> p k c", p=128), 4, 1024
    if s == 16:
        return wd["wps"].rearrange("(k p) c -> p k c", p=128), 4, 1024
    if s < 19:
        h = s - 17
        return wd["wout"][:, h * 512:(h + 1) * 512].rearrange("(k p) c -> p k c", p=128), 8, 512
    if s < 31:
        nm = "wg" if s < 25 else "wu"
        j = (s - 19) % 6
        c = 512 if j < 5 else 256
        return wd[nm][:, j * 512:j * 512 + c].rearrange("(k p) c -> p k c", p=128), 8, c
    j = s - 31
    k = 4 if j < 5 else 2
    return wd["wd"][j * 512:j * 512 + k * 128, :].rearrange("(k p) c -> p k c", p=128), k, 1024


def build_program(taps=None, nchunks=NCH, stages=5):
    taps = taps or {}
    nc = bass.Bass("TRN2", target_bir_lowering=False)

    def din(name, shape, dt=F32):
        return nc.dram_tensor(name, shape, dt, kind="ExternalInput").ap()

    xin = din("xin", [OWN + HALO, D])
    pos = din("pos", [128, 48], I32)
    hbias_d = din("hbias", [128, 1])
    wd = {"w_in": din("w_in", [D, 7680]), "wpa": din("wpa", [512, D]), "wps": din("wps", [512, D]),
          "wout": din("wout", [D, D]), "wg": din("wg", [D, DFF]), "wu": din("wu", [D, DFF]),
          "wd": din("wd", [DFF, D])}
    g1c_d = din("g1c", [128, 8])
    g2c_d = din("g2c", [128, 8])
    gfin_d = din("gfin", [128, D])
    lng_d = din("lng", [128, 512])
    lnb_d = din("lnb", [128, 512])
    bsp_d = din("bsp", [128, 512])
    wsp_d = din("wsp", [128, 1024])
    y = nc.dram_tensor("y", [OWN, D], F32, kind="ExternalOutput").ap()
    wbf = nc.dram_tensor("wbf", [NSLAB, 128, 4096], BF16, kind="Internal").ap()
    tap_out = {}
    for nm, (shape, dt) in taps.items():
        tap_out[nm] = nc.dram_tensor("tap_" + nm, shape, dt, kind="ExternalOutput").ap()

    with contextlib.ExitStack() as st:
        S = Sched(nc, st)

        def sb(name, shape, dt):
            return st.enter_context(nc.sbuf_tensor("s_" + name, shape, dt))

        def ps(name, shape, dt):
            return st.enter_context(nc.psum_tensor("p_" + name, shape, dt))

        xbuf2 = sb("xbuf", [128, 2, 4, D], F32)
        hT = sb("hT", [128, 8, T], BF16)
        wring = sb("wring", [128, 3, 4096], BF16)
        kr = [sb("kr0", [128, 4, 8, 128], BF16), sb("kr1", [128, 4, 8, 128], BF16),
              sb("kr2", [128, 4, 20, 128], BF16)]
        vr = [sb("vr0", [128, 8, 512], BF16), sb("vr1", [128, 8, 512], BF16), sb("vr2", [128, 20, 512], BF16)]
        NSL = [8, 8, 20]
        sA = sb("sA", [128, 12288], BF16)
        qT = sA[:, 0:6144].rearrange("p (g f t) -> p g f t", g=3, f=4)
        uT = sA[:, 6144:8192].rearrange("p (f t) -> p f t", f=4)
        attnT = sA[:, 8192:10240].rearrange("p (f t) -> p f t", f=4)
        sguT = sA[:, 10240:12288].rearrange("p (f t) -> p f t", f=4)
        mergedT = sA[:, 0:4096].rearrange("p (f t) -> p f t", f=8)
        gbT = sA[:, 4096:8192].rearrange("p (f t) -> p f t", f=8)
        ffT = sA[:, 0:NFT * T].rearrange("p (f t) -> p f t", f=NFT)
        stg = sb("stg", [128, 2, 1024], BF16)
        sgl = stg
        xs = stg
        pbuf = sb("pbuf", [128, 4, 512], BF16)
        qz = sb("qz", [128, 2, 2, 512], BF16)
        accN = sb("accN", [128, 512], F32)
        accD = sb("accD", [128, 512], F32)
        sgm = sb("sgm", [128, 2048], F32)
        vz = sgm[:, 0:1024].bitcast(BF16).rearrange("p (j c) -> p j c", j=4)
        vt = sgm[:, 1024:1536]
        sgt = sgm[:, 1536:2048]
        t1 = sgm[:, 0:1024]
        ropet = sgm[:, 0:512].rearrange("p (j c) -> p j c", j=4)
        vnA = sb("vnA", [128, 2, 512], BF16)
        vnB = sb("vnB", [128, 2, 512], BF16)
        mtb = sb("mtb", [128, 512], BF16)
        m3 = sb("m3", [128, 3, 512], BF16)
        cosT = sb("cosT", [128, 48, 8], F32)
        sinT = sb("sinT", [128, 48, 8], F32)
        gfin = sb("gfin", [128, D], F32)
        lng = sb("lng", [128, 512], F32)
        lnb = sb("lnb", [128, 512], F32)
        bsp = sb("bsp", [128, 512], F32)
        WcT = sb("WcT", [128, 8, 128], BF16)
        ident = sb("ident", [128, 128], BF16)
        ones64 = sb("ones64", [128, 64], BF16)
        g1c = sb("g1c", [128, 8], F32)
        g2c = sb("g2c", [128, 8], F32)
        hbias = sb("hbias", [128, 1], F32)
        epsb = sb("epsb", [128, 1], F32)
        small = sb("small", [128, 64], F32)
        ss = small[:, 0:4]
        sd = small[:, 4:8]
        rstd = small[:, 8:12]
        bn6 = small[:, 16:28].rearrange("p (j s) -> p j s", j=2)
        mv = small[:, 28:32].rearrange("p (j s) -> p j s", j=2)
        lsd = small[:, 32:34]
        lrs = small[:, 34:36]
        nhalf = small[:, 40:44]

        mm = [ps("mm0", [128, 1024], F32), ps("mm1", [128, 1024], F32)]
        sps = [ps("sps0", [128, 512], F32), ps("sps1", [128, 512], F32)]
        pvN = ps("pvN", [128, 512], F32)
        pvD = ps("pvD", [128, 512], F32)

        R = lambda n: Res(n)
        r_xb = [[R(f"x{b}_{i}") for i in range(4)] for b in range(2)]
        r_x = r_xb[0]
        r_h = [R(f"h{i}") for i in range(4)]
        r_w = [R(f"w{i}") for i in range(3)]
        r_k = [[R(f"k{g}_{s}") for s in range(NSL[g])] for g in range(3)]
        r_v = [[R(f"v{g}_{s}") for s in range(NSL[g])] for g in range(3)]
        r_q = [R(f"q{g}") for g in range(3)]
        r_u = R("uT")
        r_at = [R(f"at{p}") for p in range(4)]
        r_sg = R("sguT")
        r_mg = [R(f"mg{i}") for i in range(4)]
        r_gb = [R(f"gb{i}") for i in range(4)]
        r_ff = [R(f"ff{i}") for i in range(NFT // 2)]
        r_stg = [R("stg0"), R("stg1")]
        r_xs = r_stg
        r_pb = [R(f"pb{i}") for i in range(4)]
        r_qz = [R("qz0"), R("qz1")]
        r_acc = R("acc")
        r_vz = R("vz")
        r_rt = r_vz
        r_vt = R("vt")
        r_sgt = R("sgt")
        r_t1 = R("t1")
        r_vn = [R("vn0"), R("vn1")]
        r_mm = [R("mm0"), R("mm1")]
        r_sps = [R("sps0"), R("sps1")]
        r_pv = R("pv")
        r_pvd = R("pvd")
        r_c = R("consts")
        r_small = [R(f"sm{i}") for i in range(4)]
        r_ln = R("lnsmall")
        r_cv = [R(f"cv{s}") for s in range(NSLAB)]
        for i in range(4):
            S.new_sem(f"xs0{i}")
            S.new_sem(f"xs1{i}")
            S.new_sem(f"st{i}")
        for i in range(3):
            S.new_sem(f"w{i}")
        for s in range(NSLAB):
            S.new_sem(f"cv{s}")
        S.new_sem("ld")
        S.new_sem("tap")

        state = {"mm": 0, "sps": 0, "w": 0, "pb": 0, "stg": 0, "xs": 0, "vn": 0, "qz": 0, "tr": 0}
        r_xs = r_stg
        trt = [t_[:, :].bitcast(BF16).rearrange("p (k t) -> p k t", k=8) for t_ in (sps[0], sps[1], pvN, pvD)]
        r_tr = [r_sps[0], r_sps[1], r_pv, r_pvd]

        def nxt(key, n):
            i = state[key]
            state[key] = (i + 1) % n
            return i

        def load_slab(s):
            _, k, c = slab_src(wd, s)
            slot = nxt("w", 3)
            S.dma("sp", f"w{slot}", wring[:, slot, 0:k * c], wbf[s][:, 0:k * c],
                  reads=[r_cv[s]], writes=[r_w[slot]])
            return wring[:, slot, 0:k * c].rearrange("p (k c) -> p k c", k=k), r_w[slot]

        def load_x(ci):
            b_ = ci % 2
            for tb in range(4):
                r0 = ci * T + tb * 128
                S.dma("sp", f"xs{b_}{tb}", xbuf2[:, b_, tb, :], xin[r0:r0 + 128, :], writes=[r_xb[b_][tb]])

        S.dma("sp", "ld", g1c[:, :], g1c_d, writes=[r_c])
        S.dma("sp", "ld", g2c[:, :], g2c_d, writes=[r_c])
        S.dma("sp", "ld", hbias[:, :], hbias_d, writes=[r_c])
        S.dma("sp", "ld", gfin[:, :], gfin_d, writes=[r_c])
        S.dma("sp", "ld", lng[:, :], lng_d, writes=[r_c])
        S.dma("sp", "ld", lnb[:, :], lnb_d, writes=[r_c])
        S.dma("sp", "ld", bsp[:, :], bsp_d, writes=[r_c])
        xf = xbuf2[:, 0, :, :].rearrange("p a b -> p (a b)")
        tmpA = xf[:, 0:1024]
        tmpI = xf[:, 1024:1536]
        S.dma("sp", "ld", tmpA, wsp_d, writes=r_x)
        S.dma("sp", "ld", tmpI[:, 0:48].bitcast(I32), pos, writes=r_x)
        ld_tok = ("ld", S.cnt["ld"])

        def setup_dve(h):
            onesf = xf[:, 1536:1664]
            trif = xf[:, 1664:1792]
            bandf = xf[:, 1792:1920]
            idf = xf[:, 1920:2048]
            modf = xf[:, 2048:2176]
            modi = xf[:, 2176:2304].bitcast(I32)
            h.memset(onesf, 1.0)
            h.memset(epsb[:, :], EPS)
            h.memset(nhalf, -0.5)
            h.memset(ones64[:, :], 1.0)
            h.memset(vnA[:, :, :], 0.0)
            h.memset(vnB[:, :, :], 0.0)
            h.memset(qz[:, :, :, :], 0.0)
            return h.memset(idf, 0.0)
        S.op("dve", setup_dve, writes=r_x + [r_c], deps=[ld_tok])

        def setup_pool(h):
            onesf = xf[:, 1536:1664]
            trif = xf[:, 1664:1792]
            bandf = xf[:, 1792:1920]
            idf = xf[:, 1920:2048]
            modi = xf[:, 2176:2304].bitcast(I32)
            h.affine_select(out=trif, in_=onesf, pattern=[[1, 128]], compare_op=ALU.is_ge, fill=0.0,
                            base=0, channel_multiplier=-1)
            h.affine_select(out=bandf, in_=onesf, pattern=[[-1, 128]], compare_op=ALU.is_ge, fill=0.0,
                            base=0, channel_multiplier=1)
            h.affine_select(out=idf, in_=onesf, pattern=[[-1, 128]], compare_op=ALU.is_equal, fill=0.0,
                            base=0, channel_multiplier=1)
            return h.iota(modi, pattern=[[1, 128]], base=128, channel_multiplier=-1)
        S.op("pool", setup_pool, reads=r_x, writes=r_x)

        INV2PI = float(1.0 / (2.0 * np.pi))
        C1 = 6.28125
        C2 = float(2.0 * np.pi - 6.28125)
        PI = float(np.pi)
        inv_freq = (ROPE_THETA ** (-np.arange(0, 16, 2, dtype=np.float32) / np.float32(16))).astype(np.float32)

        onesf = xf[:, 1536:1664]
        trif = xf[:, 1664:1792]
        bandf = xf[:, 1792:1920]
        idf = xf[:, 1920:2048]
        modf = xf[:, 2048:2176]
        modi = xf[:, 2176:2304].bitcast(I32)
        posf = xf[:, 2304:2352]
        ang = xf[:, 2432:2816].rearrange("p (b f) -> p b f", f=8)
        angf = xf[:, 2432:2816]
        ki = xf[:, 2816:3200].bitcast(I32)
        kf = xf[:, 3200:3584]
        fx = xf[:, 3584:3968]
        r_set = Res("setup")

        def SD(fn, eng="dve"):
            S.op(eng, fn, reads=r_x + [r_set], writes=r_x + [r_set, r_c])

        SD(lambda h: h.tensor_copy(out=ident[:, :], in_=idf))
        SD(lambda h: h.tensor_single_scalar(out=modi, in_=modi, scalar=3, op=ALU.bitwise_and))
        SD(lambda h: h.tensor_copy(out=modf, in_=modi))
        SD(lambda h: h.tensor_scalar(out=modf, in0=modf, scalar1=0.0, scalar2=None, op0=ALU.is_equal))

        def masks1(h):
            ins = None
            for j in range(2):
                h.tensor_copy(out=mtb[:, j * 128:(j + 1) * 128], in_=trif)
                h.tensor_copy(out=mtb[:, 256 + j * 128:256 + (j + 1) * 128], in_=bandf)
            for j in range(4):
                h.tensor_tensor(out=m3[:, 0, j * 128:(j + 1) * 128], in0=modf, in1=trif, op=ALU.mult)
                h.tensor_copy(out=m3[:, 1, j * 128:(j + 1) * 128], in_=modf)
                ins = h.tensor_tensor(out=m3[:, 2, j * 128:(j + 1) * 128], in0=modf, in1=bandf, op=ALU.mult)
            return ins
        SD(masks1)
        SD(lambda h: h.tensor_scalar(out=mtb[:, :], in0=mtb[:, :], scalar1=-1.0, scalar2=-NEG, op0=ALU.add, op1=ALU.mult))
        SD(lambda h: h.tensor_scalar(out=m3[:, :, :], in0=m3[:, :, :], scalar1=-1.0, scalar2=-NEG, op0=ALU.add, op1=ALU.mult))
        SD(lambda h: h.tensor_tensor(out=WcT[:, :, :], in0=tmpA.rearrange("p (g t) -> p g t", g=8),
                                     in1=trif.unsqueeze(1).to_broadcast([128, 8, 128]), op=ALU.mult))
        SD(lambda h: h.tensor_copy(out=posf, in_=tmpI[:, 0:48].bitcast(I32)))

        def angs(h):
            ins = None
            for f in range(8):
                ins = h.tensor_scalar(out=ang[:, :, f], in0=posf, scalar1=float(inv_freq[f]), scalar2=None, op0=ALU.mult)
            return ins
        SD(angs)
        SD(lambda h: h.tensor_scalar(out=ki, in0=angf, scalar1=INV2PI, scalar2=None, op0=ALU.mult))
        SD(lambda h: h.tensor_copy(out=kf, in_=ki))
        SD(lambda h: h.scalar_tensor_tensor(out=angf, in0=kf, scalar=-C1, in1=angf, op0=ALU.mult, op1=ALU.add))
        SD(lambda h: h.scalar_tensor_tensor(out=angf, in0=kf, scalar=-C2, in1=angf, op0=ALU.mult, op1=ALU.add))
        SD(lambda h: h.tensor_scalar(out=fx, in0=angf, scalar1=PI, scalar2=-2.0 * PI, op0=ALU.is_gt, op1=ALU.mult))
        SD(lambda h: h.tensor_tensor(out=angf, in0=angf, in1=fx, op=ALU.add))
        SD(lambda h: h.tensor_scalar(out=fx, in0=angf, scalar1=-PI, scalar2=2.0 * PI, op0=ALU.is_lt, op1=ALU.mult))
        SD(lambda h: h.tensor_tensor(out=angf, in0=angf, in1=fx, op=ALU.add))
        SD(lambda h: h.activation(out=sinT[:, :, :].rearrange("p b f -> p (b f)"), in_=angf, func=AF.Sin), eng="act")
        SD(lambda h: h.tensor_scalar(out=angf, in0=angf, scalar1=PI / 2.0, scalar2=None, op0=ALU.add))
        SD(lambda h: h.tensor_scalar(out=fx, in0=angf, scalar1=PI, scalar2=-2.0 * PI, op0=ALU.is_gt, op1=ALU.mult))
        SD(lambda h: h.tensor_tensor(out=angf, in0=angf, in1=fx, op=ALU.add))
        SD(lambda h: h.activation(out=cosT[:, :, :].rearrange("p b f -> p (b f)"), in_=angf, func=AF.Sin), eng="act")

        cv_order = [5, 8, 3, 6, 4, 7, 0, 1, 2, 9, 10, 11, 12, 15, 13, 14, 16, 17, 18]
        for j in range(6):
            cv_order += [19 + j, 25 + j]
        cv_order += list(range(31, 37))
        assert sorted(cv_order) == list(range(NSLAB))
        cv_state = {"i": 0, "toks": []}

        def emit_cv(n):
            for _ in range(n):
                i = cv_state["i"]
                if i >= NSLAB:
                    return
                sl_ = cv_order[i]
                if i >= 3:
                    S.wait_all("pool", [cv_state["toks"][i - 3]])
                src, k, c = slab_src(wd, sl_)
                dst = wbf[sl_][:, 0:k * c].rearrange("p (k c) -> p k c", k=k)
                cv_state["toks"].append(S.dma("pool", f"cv{sl_}", dst, src, writes=[r_cv[sl_]]))
                cv_state["i"] = i + 1

        def pe_groups(groups, reads, writes):
            def fn(h):
                ins = None
                for out, pairs, st0, sp0, tp in groups:
                    n = len(pairs)
                    for i, (l, r) in enumerate(pairs):
                        kw = {}
                        if tp is not None:
                            kw["tile_position"] = tp
                        ins = h.matmul(out, lhsT=l, rhs=r, start=(st0 and i == 0), stop=(sp0 and i == n - 1), **kw)
                return ins
            return S.op("pe", fn, reads=reads, writes=writes)

        def tap(name, ap, reads):
            if name in tap_out:
                S.dma("sp", "tap", tap_out[name], ap, reads=reads)

        def norm_gen(gcol, b_):
            xb_ = xbuf2[:, b_]
            rx_ = r_xb[b_]
            sls = {}

            def A(tb):
                sl = nxt("xs", 2)
                sls[tb] = sl
                S.op("act", lambda h, tb=tb, sl=sl: h.activation(out=xs[:, sl, :], in_=xb_[:, tb, :], func=AF.Square,
                                                                   accum_out=ss[:, tb:tb + 1]),
                     reads=[rx_[tb]], writes=[r_xs[sl], r_small[tb]])
                S.op("dve", lambda h, tb=tb: h.tensor_scalar(out=sd[:, tb:tb + 1], in0=ss[:, tb:tb + 1], scalar1=1.0 / D, scalar2=EPS,
                                                             op0=ALU.mult, op1=ALU.add),
                     reads=[r_small[tb]], writes=[r_small[tb]])
                S.op("pool", lambda h, tb=tb: h.tensor_tensor(out=rstd[:, tb:tb + 1], in0=sd[:, tb:tb + 1], in1=nhalf[:, 0:1], op=ALU.pow),
                     reads=[r_small[tb], r_c], writes=[r_small[tb]])
                S.op("act", lambda h, tb=tb, sl=sl: h.activation(out=xs[:, sl, :], in_=xb_[:, tb, :], func=AF.Copy,
                                                                   scale=rstd[:, tb:tb + 1]),
                     reads=[rx_[tb], r_small[tb]], writes=[r_xs[sl]])

            def Tr(tb):
                sl = sls[tb]
                ti = nxt("tr", 4)
                trp = trt[ti]

                def tr(h, sl=sl, trp=trp):
                    ins = None
                    for kc in range(8):
                        ins = h.transpose(out=trp[:, kc, :], in_=xs[:, sl, kc * 128:(kc + 1) * 128], identity=ident[:, :])
                    return ins
                S.op("pe", tr, reads=[r_xs[sl], r_c], writes=[r_tr[ti]])
                S.op("dve", lambda h, tb=tb, trp=trp: h.tensor_tensor(
                    out=hT[:, :, tb * 128:(tb + 1) * 128], in0=trp,
                    in1=gcol[:, :].unsqueeze(2).to_broadcast([128, 8, 128]), op=ALU.mult),
                    reads=[r_tr[ti], r_c], writes=[r_h[tb]])
            A(0); yield
            A(1); yield
            Tr(0); yield
            A(2); yield
            Tr(1); yield
            A(3); yield
            Tr(2); yield
            Tr(3)

        def norm_to_hT(gcol, b_):
            for _ in norm_gen(gcol, b_):
                pass

        def tok_sel(g, blk):
            if g == 0:
                return lambda kc: hT[:, kc, blk * 128:(blk + 1) * 128]
            return lambda kc: hT[:, kc, :].rearrange("p (i c) -> p c i", c=4)[:, blk, :]

        def qkv_stage(ci, g, with_q):
            slot0 = 4 * (ci % (NSL[g] // 4))
            kslab, r_ks = load_slab(3 + g)
            if with_q:
                qslab, r_qs = load_slab(g)
            nf = 8 if with_q else 4
            pend = []

            def emit_tr(tb, si):
                mj = nxt("tr", 4)
                trp = trt[mj]

                def tr(h, si=si, trp=trp, nf=nf):
                    ins = None
                    for f in range(nf):
                        ins = h.transpose(out=trp[:, f, :], in_=stg[:, si, f * 128:(f + 1) * 128], identity=ident[:, :])
                    return ins
                S.op("pe", tr, reads=[r_stg[si], r_c], writes=[r_tr[mj]])
                kf0 = 4 if with_q else 0
                if g == 0:
                    if with_q:
                        S.op("dve", lambda h, trp=trp, tb=tb: h.tensor_copy(out=qT[:, 0, :, tb * 128:(tb + 1) * 128], in_=trp[:, 0:4, :]),
                             reads=[r_tr[mj]], writes=[r_q[0]])
                    S.op("dve", lambda h, trp=trp, tb=tb, kf0=kf0: h.tensor_copy(out=kr[0][:, :, slot0 + tb, :], in_=trp[:, kf0:kf0 + 4, :]),
                         reads=[r_tr[mj]], writes=[r_k[0][slot0 + tb]])
                else:
                    if with_q:
                        S.op("dve", lambda h, trp=trp, tb=tb: h.tensor_copy(
                            out=qT[:, g, :, :].rearrange("p f (c j) -> p f c j", c=4)[:, :, :, tb * 32:(tb + 1) * 32],
                            in_=trp[:, 0:4, :].rearrange("p f (i c) -> p f c i", c=4)),
                            reads=[r_tr[mj]], writes=[r_q[g]])
                    S.op("dve", lambda h, trp=trp, tb=tb, kf0=kf0: h.tensor_copy(
                        out=kr[g][:, :, slot0:slot0 + 4, tb * 32:(tb + 1) * 32],
                        in_=trp[:, kf0:kf0 + 4, :].rearrange("p f (i c) -> p f c i", c=4)),
                        reads=[r_tr[mj]], writes=[r_k[g][slot0 + c] for c in range(4)])

            for tb in range(4):
                mi = nxt("mm", 2)
                groups = []
                rd = [r_h[tb], r_ks]
                if with_q:
                    groups.append((mm[mi][:, 0:512], [(hT[:, kc, tb * 128:(tb + 1) * 128], qslab[:, kc, :]) for kc in range(8)],
                                   True, True, None))
                    rd.append(r_qs)
                    kcol = 512
                else:
                    kcol = 0
                groups.append((mm[mi][:, kcol:kcol + 512], [(hT[:, kc, tb * 128:(tb + 1) * 128], kslab[:, kc, :]) for kc in range(8)],
                               True, True, None))
                pe_groups(groups, rd, [r_mm[mi]])
                W = kcol + 512
                si = nxt("stg", 2)
                S.op("act", lambda h, mi=mi, si=si, W=W: h.copy(out=stg[:, si, 0:W], in_=mm[mi][:, 0:W]),
                     reads=[r_mm[mi]], writes=[r_stg[si]])
                nh = W // 64
                blk = ci * 4 + tb

                def rope_views(mi, si, W, nh, blk):
                    mv_ = mm[mi][:, 0:W].rearrange("p (h d) -> p h d", d=64)
                    sv = stg[:, si, 0:W].rearrange("p (h d) -> p h d", d=64)
                    cb = cosT[:, blk, :].unsqueeze(1).to_broadcast([128, nh, 8])
                    sn = sinT[:, blk, :].unsqueeze(1).to_broadcast([128, nh, 8])
                    tt = [ropet[:, j, 0:nh * 8].rearrange("p (h d) -> p h d", d=8) for j in range(4)]
                    return mv_, sv, cb, sn, tt

                def rope1(h, a=(mi, si, W, nh, blk)):
                    mv_, sv, cb, sn, tt = rope_views(*a)
                    x1 = mv_[:, :, 0:8]
                    x2 = mv_[:, :, 8:16]
                    h.tensor_tensor(out=tt[0], in0=x1, in1=cb, op=ALU.mult)
                    h.tensor_tensor(out=tt[1], in0=x2, in1=sn, op=ALU.mult)
                    h.tensor_tensor(out=tt[2], in0=x2, in1=cb, op=ALU.mult)
                    return h.tensor_tensor(out=tt[3], in0=x1, in1=sn, op=ALU.mult)

                def rope2(h, a=(mi, si, W, nh, blk)):
                    mv_, sv, cb, sn, tt = rope_views(*a)
                    h.tensor_tensor(out=sv[:, :, 0:8], in0=tt[0], in1=tt[1], op=ALU.subtract)
                    return h.tensor_tensor(out=sv[:, :, 8:16], in0=tt[2], in1=tt[3], op=ALU.add)
                import os
                if "norope1" not in os.environ.get("ATT_DBG", ""):
                    S.op("dve", rope1, reads=[r_mm[mi], r_c, r_stg[si]], writes=[r_rt])
                if "norope2" not in os.environ.get("ATT_DBG", ""):
                    S.op("dve", rope2, reads=[r_rt], writes=[r_stg[si]])
                pend.append((tb, si))
                if len(pend) > 1:
                    emit_tr(*pend.pop(0))
            while pend:
                emit_tr(*pend.pop(0))
            vslab, r_vs = load_slab(6 + g)
            for bp in range(2):
                mi = nxt("mm", 2)
                groups = []
                for j in range(2):
                    sel = tok_sel(g, bp * 2 + j)
                    groups.append((mm[mi][:, j * 512:(j + 1) * 512], [(sel(kc), vslab[:, kc, :]) for kc in range(8)], True, True, None))
                pe_groups(groups, r_h + [r_vs], [r_mm[mi]])
                s0 = slot0 + bp * 2
                S.op("act", lambda h, mi=mi, s0=s0: h.copy(out=vr[g][:, s0:s0 + 2, :], in_=mm[mi][:, :].rearrange("p (j c) -> p j c", j=2)),
                     reads=[r_mm[mi]], writes=[r_v[g][s0], r_v[g][s0 + 1]])

        def attention(ci):
            tasks = []
            pvsets = [(pvN[:, :], pvD[:, :], [r_pv, r_pvd]), (mm[1][:, 0:512], mm[1][:, 512:1024], [r_mm[1]])]
            for p in range(4):
                hA, hB = 2 * p, 2 * p + 1
                for g in range(2):
                    pvN_, pvD_, rpv_ = pvsets[(p * 3 + g) % 2]
                    for qb in range(4):
                        n = 4 * ci + qb
                        own_s = n % 8
                        prv = n - 1 if g == 0 else n - 4
                        prv_s = prv % 8
                        halo_prev = prv < 4 * FIRST_OWN
                        qk = []
                        for j, (X, ks) in enumerate([(0, own_s), (1, own_s), (0, prv_s), (1, prv_s)]):
                            qk.append((j * 128, kr[g][:, p, ks, :], X, slice(qb * 128, (qb + 1) * 128)))
                        pv = []
                        qc = slice(qb * 128, (qb + 1) * 128)
                        pv.append((pvN_[0:64, qc], vr[g][:, own_s, hA * 64:(hA + 1) * 64], 0, True, False, (0, 0)))
                        pv.append((pvN_[0:64, qc], vr[g][:, prv_s, hA * 64:(hA + 1) * 64], 256, False, True, (0, 0)))
                        pv.append((pvN_[64:128, qc], vr[g][:, own_s, hB * 64:(hB + 1) * 64], 128, True, False, (0, 64)))
                        pv.append((pvN_[64:128, qc], vr[g][:, prv_s, hB * 64:(hB + 1) * 64], 384, False, True, (0, 64)))
                        pv.append((pvD_[0:64, qc], ones64[:, :], 0, True, False, (0, 0)))
                        pv.append((pvD_[0:64, qc], ones64[:, :], 256, False, True, (0, 0)))
                        pv.append((pvD_[64:128, qc], ones64[:, :], 128, True, False, (0, 64)))
                        pv.append((pvD_[64:128, qc], ones64[:, :], 384, False, True, (0, 64)))
                        tasks.append(dict(p=p, g=g, qk=qk, pv=pv, mask=mtb[:, :],
                                          bias=[(0, 256, False), (256, 512, True)] if halo_prev else [(0, 512, False)],
                                          rk=[r_k[g][own_s], r_k[g][prv_s]], rv=[r_v[g][own_s], r_v[g][prv_s]],
                                          first=(qb == 0), last=(qb == 3), pvs=(pvN_, pvD_, rpv_)))
                g = 2
                pvN_, pvD_, rpv_ = pvsets[(p * 3 + g) % 2]
                for X, (hh, tp, prow) in enumerate([(hA, (0, 0), slice(0, 64)), (hB, (0, 64), slice(64, 128))]):
                    for d in (4, 3, 2, 1, 0):
                        cj = ci - d
                        qk, pv, rk, rv = [], [], [], []
                        for m in range(4):
                            sl = (4 * cj + m) % 20
                            qk.append((m * 128, kr[2][:, p, sl, :], X, slice(m * 128, (m + 1) * 128)))
                            qc = slice(m * 128, (m + 1) * 128)
                            pv.append((pvN_[prow, qc], vr[2][:, sl, hh * 64:(hh + 1) * 64], m * 128, d == 4 and m == 0, d == 0 and m == 3, tp))
                            pv.append((pvD_[prow, qc], ones64[:, :], m * 128, d == 4 and m == 0, d == 0 and m == 3, tp))
                            rk.append(r_k[2][sl])
                            rv.append(r_v[2][sl])
                        mt = 0 if d == 0 else (2 if d == 4 else 1)
                        tasks.append(dict(p=p, g=2, qk=qk, pv=pv, mask=m3[:, mt, :],
                                          bias=[(0, 512, cj < FIRST_OWN)], rk=rk, rv=rv,
                                          first=(X == 0 and d == 4), last=(X == 1 and d == 0), pvs=(pvN_, pvD_, rpv_)))

            import os
            dbgm0 = os.environ.get("ATT_DBG", "")

            def emit_front(t):
                if t["first"]:
                    zs = nxt("qz", 2)
                    state["qz_cur"] = zs
                    g_, p_ = t["g"], t["p"]

                    def zc(h, zs=zs, g_=g_, p_=p_):
                        h.tensor_copy(out=qz[0:64, 0, zs, :], in_=qT[0:64, g_, p_, :])
                        return h.tensor_copy(out=qz[64:128, 1, zs, :], in_=qT[64:128, g_, p_, :])
                    S.op("pool", zc, reads=[r_q[g_]], writes=[r_qz[zs]])
                zs = state["qz_cur"]
                si = nxt("sps", 2)
                pi = nxt("pb", 4)
                t["pi"] = pi
                sp = sps[si]
                nq_ = len(t["qk"])
                pe_groups([(sp[:, :], [(ident[:, :], t["mask"])], True, False, None)] +
                          [(sp[:, c0:c0 + 128], [(l, qz[:, X, zs, qc])], False, j_ == nq_ - 1, None)
                           for j_, (c0, l, X, qc) in enumerate(t["qk"])],
                          t["rk"] + [r_qz[zs], r_c], [r_sps[si]])

                def ex(h, t=t, sp=sp, pi=pi):
                    ins = None
                    for (a, b, hb) in t["bias"]:
                        if hb:
                            ins = h.activation(out=pbuf[:, pi, a:b], in_=sp[:, a:b], func=AF.Exp, bias=hbias[:, 0:1], scale=0.125)
                        else:
                            ins = h.activation(out=pbuf[:, pi, a:b], in_=sp[:, a:b], func=AF.Exp, scale=0.125)
                    return ins
                if "noexp" in dbgm0:
                    return
                S.op("act", ex, reads=[r_sps[si], r_c], writes=[r_pb[pi]])
                if "nopool" in dbgm0:
                    return

            def emit_back(t):
                pi = t["pi"]
                pe_groups([(o, [(l, pbuf[:, pi, c0:c0 + 128])], st0, sp0, tp) for (o, l, c0, st0, sp0, tp) in t["pv"]],
                          t["rv"] + [r_pb[pi], r_c], t["pvs"][2])
                if t["last"]:
                    g, p = t["g"], t["p"]
                    pn_, pd_, rp_ = t["pvs"]
                    if g == 0:
                        def ev(h, pn_=pn_, pd_=pd_):
                            h.tensor_copy(out=accN[:, :], in_=pn_)
                            return h.tensor_copy(out=accD[:, :], in_=pd_)
                    else:
                        def ev(h, pn_=pn_, pd_=pd_):
                            av = accN[:, :].rearrange("p (j c) -> p c j", c=4)
                            dv = accD[:, :].rearrange("p (j c) -> p c j", c=4)
                            h.tensor_tensor(out=av, in0=av, in1=pn_.rearrange("p (c j) -> p c j", c=4), op=ALU.add)
                            return h.tensor_tensor(out=dv, in0=dv, in1=pd_.rearrange("p (c j) -> p c j", c=4), op=ALU.add)
                    S.op("dve", ev, reads=rp_, writes=[r_acc])
                    if g == 2:
                        S.op("act", lambda h: h.activation(out=accD[:, :], in_=accD[:, :], func=AF.Ln), reads=[r_acc], writes=[r_acc])
                        S.op("act", lambda h: h.activation(out=accD[:, :], in_=accD[:, :], func=AF.Exp, scale=-1.0), reads=[r_acc], writes=[r_acc])
                        S.op("dve", lambda h, p=p: h.tensor_tensor(out=attnT[:, p, :], in0=accN[:, :], in1=accD[:, :], op=ALU.mult),
                             reads=[r_acc], writes=[r_at[p]])

            import os
            dbgm = os.environ.get("ATT_DBG", "")
            if "g01" in dbgm:
                tasks = [t for t in tasks if t["g"] < 2]
            if "g2" in dbgm:
                tasks = [t for t in tasks if t["g"] == 2]
            if "p0" in dbgm:
                tasks = [t for t in tasks if t["p"] == 0]
            pend_ = []
            for t in tasks:
                emit_front(t)
                pend_.append(t)
                if len(pend_) > 2:
                    emit_back(pend_.pop(0))
                yield
            while pend_:
                emit_back(pend_.pop(0))

        def sgu_pre(ci):
            uslab, r_us = load_slab(9)
            for fp in range(2):
                mi = nxt("mm", 2)
                groups = [(mm[mi][:, j * 512:(j + 1) * 512],
                           [(uslab[:, kc, (fp * 2 + j) * 128:(fp * 2 + j + 1) * 128], hT[:, kc, :]) for kc in range(8)],
                           True, True, None) for j in range(2)]
                pe_groups(groups, r_h + [r_us], [r_mm[mi]])
                S.op("act", lambda h, mi=mi, fp=fp: h.activation(out=uT[:, fp * 2:fp * 2 + 2, :],
                                                                 in_=mm[mi][:, :].rearrange("p (j c) -> p j c", j=2), func=AF.Gelu),
                     reads=[r_mm[mi]], writes=[r_u])
            vslab, r_vs = load_slab(10)
            for tp_ in range(2):
                mi = nxt("mm", 2)
                groups = [(mm[mi][:, j * 512:(j + 1) * 512],
                           [(hT[:, kc, (tp_ * 2 + j) * 128:(tp_ * 2 + j + 1) * 128], vslab[:, kc, :]) for kc in range(8)],
                           True, True, None) for j in range(2)]
                pe_groups(groups, r_h + [r_vs], [r_mm[mi]])
                S.op("act", lambda h, mi=mi, tp_=tp_: h.activation(out=vz[:, tp_ * 2:tp_ * 2 + 2, :],
                                                                   in_=mm[mi][:, :].rearrange("p (j c) -> p j c", j=2), func=AF.Gelu),
                     reads=[r_mm[mi]], writes=[r_vz])

        def sgu_stage(ci):
            for tp_ in range(2):
                def stats1(h, tp_=tp_):
                    h.bn_stats(out=bn6[:, 0, :], in_=vz[:, tp_ * 2, :])
                    return h.bn_stats(out=bn6[:, 1, :], in_=vz[:, tp_ * 2 + 1, :])

                def stats2(h):
                    h.bn_aggr(out=mv[:, 0, :], in_=bn6[:, 0, :])
                    return h.bn_aggr(out=mv[:, 1, :], in_=bn6[:, 1, :])
                S.op("dve", stats1, reads=[r_vz], writes=[r_ln])
                S.op("dve", stats2, reads=[r_ln], writes=[r_ln])
                S.op("dve", lambda h: h.tensor_scalar(out=lsd[:, 0:2], in0=mv[:, :, 1], scalar1=EPS, scalar2=None, op0=ALU.add),
                     reads=[r_ln], writes=[r_ln])
                S.op("pool", lambda h: h.tensor_tensor(out=lrs[:, 0:2], in0=lsd[:, 0:2], in1=nhalf[:, 0:2], op=ALU.pow),
                     reads=[r_ln, r_c], writes=[r_ln])
                yield
                for j in range(2):
                    tb = tp_ * 2 + j
                    vi = nxt("vn", 2)
                    S.op("dve", lambda h, j=j, tb=tb: h.tensor_scalar(out=vt, in0=vz[:, tb, :], scalar1=mv[:, j, 0:1], scalar2=lrs[:, j:j + 1],
                                                                      op0=ALU.subtract, op1=ALU.mult),
                         reads=[r_vz, r_ln], writes=[r_vt])
                    S.op("dve", lambda h: h.tensor_tensor(out=vt, in0=vt, in1=lng[:, :], op=ALU.mult), reads=[r_vt, r_c], writes=[r_vt])

                    def nrm3(h, vi=vi):
                        v4 = vt.rearrange("p (g e c) -> p g e c", g=4, e=2)
                        b4 = lnb[:, :].rearrange("p (g e c) -> p g e c", g=4, e=2)
                        a4 = vnA[:, vi, :].rearrange("p (g e c) -> p g e c", g=4, e=2)
                        c4 = vnB[:, vi, :].rearrange("p (g e c) -> p g e c", g=4, e=2)
                        h.tensor_tensor(out=a4[:, :, 0, :], in0=v4[:, :, 0, :], in1=b4[:, :, 0, :], op=ALU.add)
                        return h.tensor_tensor(out=c4[:, :, 1, :], in0=v4[:, :, 1, :], in1=b4[:, :, 1, :], op=ALU.add)
                    S.op("dve", nrm3, reads=[r_vt, r_c], writes=[r_vn[vi]])
                    si = 0
                    yield
                    groups = [(mm[si][:, ft * 128:(ft + 1) * 128],
                               [(vnA[:, vi, ft * 128:(ft + 1) * 128], WcT[:, 2 * ft, :]),
                                (vnB[:, vi, ft * 128:(ft + 1) * 128], WcT[:, 2 * ft + 1, :])], True, True, None) for ft in range(4)]
                    pe_groups(groups, [r_vn[vi], r_c], [r_mm[si]])
                    S.op("dve", lambda h, si=si: h.tensor_tensor(out=sgt, in0=mm[si][:, 0:512], in1=bsp[:, :], op=ALU.add),
                         reads=[r_mm[si], r_c], writes=[r_sgt])
                    S.op("dve", lambda h, tb=tb: h.tensor_tensor(out=sguT[:, :, tb * 128:(tb + 1) * 128],
                                                                 in0=sgt.rearrange("p (f t) -> p f t", f=4),
                                                                 in1=uT[:, :, tb * 128:(tb + 1) * 128], op=ALU.mult),
                         reads=[r_sgt, r_u], writes=[r_sg])

        def merge_stage(ci):
            xbuf = xbuf2[:, ci % 2]
            r_x = r_xb[ci % 2]
            S.fence(r_q + [r_u], r_mg + r_gb)
            S.fence([r_vz, r_vt, r_sgt], [r_t1])
            for hs in range(2):
                gslab, r_gs = load_slab(11 + hs)
                for tq in range(2):
                    tp_ = hs * 2 + tq
                    mi = nxt("mm", 2)
                    groups = [(mm[mi][:, j * 512:(j + 1) * 512],
                               [(gslab[:, kc, (tq * 2 + j) * 128:(tq * 2 + j + 1) * 128], hT[:, kc, :]) for kc in range(8)],
                               True, True, None) for j in range(2)]
                    pe_groups(groups, r_h + [r_gs], [r_mm[mi]])
                    S.op("act", lambda h, mi=mi, tp_=tp_: h.activation(out=mergedT[:, tp_ * 2:tp_ * 2 + 2, :],
                                                                       in_=mm[mi][:, :].rearrange("p (j c) -> p j c", j=2), func=AF.Sigmoid),
                         reads=[r_mm[mi]], writes=[r_mg[tp_]])
            paslab, r_pa = load_slab(15)
            for tp_ in range(4):
                mi = nxt("mm", 2)
                groups = [(mm[mi][:, j * 512:(j + 1) * 512],
                           [(paslab[:, kc, (tp_ * 2 + j) * 128:(tp_ * 2 + j + 1) * 128], attnT[:, kc, :]) for kc in range(4)],
                           True, True, None) for j in range(2)]
                pe_groups(groups, r_at + [r_pa], [r_mm[mi]])
                S.op("dve", lambda h, mi=mi, tp_=tp_: h.tensor_tensor(out=mergedT[:, tp_ * 2:tp_ * 2 + 2, :],
                                                                      in0=mm[mi][:, :].rearrange("p (j c) -> p j c", j=2),
                                                                      in1=mergedT[:, tp_ * 2:tp_ * 2 + 2, :], op=ALU.mult),
                     reads=[r_mm[mi], r_mg[tp_]], writes=[r_mg[tp_]])
            for hs in range(2):
                gslab, r_gs = load_slab(13 + hs)
                for tq in range(2):
                    tp_ = hs * 2 + tq
                    mi = nxt("mm", 2)
                    groups = [(mm[mi][:, j * 512:(j + 1) * 512],
                               [(gslab[:, kc, (tq * 2 + j) * 128:(tq * 2 + j + 1) * 128], hT[:, kc, :]) for kc in range(8)],
                               True, True, None) for j in range(2)]
                    pe_groups(groups, r_h + [r_gs], [r_mm[mi]])
                    S.op("act", lambda h, mi=mi, tp_=tp_: h.activation(out=gbT[:, tp_ * 2:tp_ * 2 + 2, :],
                                                                       in_=mm[mi][:, :].rearrange("p (j c) -> p j c", j=2), func=AF.Sigmoid),
                         reads=[r_mm[mi]], writes=[r_gb[tp_]])
            pbslab, r_pbs = load_slab(16)
            for tp_ in range(4):
                mi = nxt("mm", 2)
                groups = [(mm[mi][:, j * 512:(j + 1) * 512],
                           [(pbslab[:, kc, (tp_ * 2 + j) * 128:(tp_ * 2 + j + 1) * 128], sguT[:, kc, :]) for kc in range(4)],
                           True, True, None) for j in range(2)]
                pe_groups(groups, [r_sg, r_pbs], [r_mm[mi]])

                S.op("dve", lambda h, mi=mi, tp_=tp_: h.tensor_tensor(
                    out=t1, in0=mm[mi][:, :], in1=gbT[:, tp_ * 2:tp_ * 2 + 2, :].rearrange("p a b -> p (a b)"), op=ALU.mult),
                    reads=[r_mm[mi], r_gb[tp_]], writes=[r_t1])
                S.op("dve", lambda h, tp_=tp_: h.tensor_tensor(
                    out=mergedT[:, tp_ * 2:tp_ * 2 + 2, :].rearrange("p a b -> p (a b)"), in0=t1,
                    in1=mergedT[:, tp_ * 2:tp_ * 2 + 2, :].rearrange("p a b -> p (a b)"), op=ALU.add),
                    reads=[r_t1, r_mg[tp_]], writes=[r_mg[tp_]])
            oslabs = [load_slab(17), load_slab(18)]
            for tp_ in range(2):
                for hf in range(2):
                    oslab, r_os = oslabs[hf]
                    mi = nxt("mm", 2)
                    groups = [(mm[mi][:, j * 512:(j + 1) * 512],
                               [(mergedT[:, kc, (tp_ * 2 + j) * 128:(tp_ * 2 + j + 1) * 128], oslab[:, kc, :]) for kc in range(8)],
                               True, True, None) for j in range(2)]
                    pe_groups(groups, r_mg + [r_os], [r_mm[mi]])
                    S.op("dve", lambda h, mi=mi, tp_=tp_, hf=hf: h.tensor_tensor(
                        out=xbuf[:, tp_ * 2:tp_ * 2 + 2, hf * 512:(hf + 1) * 512],
                        in0=xbuf[:, tp_ * 2:tp_ * 2 + 2, hf * 512:(hf + 1) * 512],
                        in1=mm[mi][:, :].rearrange("p (j c) -> p j c", j=2), op=ALU.add),
                        reads=[r_mm[mi]], writes=[r_x[tp_ * 2], r_x[tp_ * 2 + 1]])

        def ffn_stage(ci, nxt_norm):
            xbuf = xbuf2[:, ci % 2]
            r_x = r_xb[ci % 2]
            norm_to_hT(g2c, ci % 2)
            S.fence(r_q + [r_u, r_sg] + r_at + r_mg + r_gb, r_ff)
            S.fence(r_stg, r_stg)
            for s in range(6):
                gs, r_gs = load_slab(19 + s)
                us, r_us = load_slab(25 + s)
                ntp = 2 if s < 5 else 1
                for tq in range(ntp):
                    fi = s * 2 + tq
                    mg_ = nxt("mm", 2)
                    groups = [(mm[mg_][:, j * 512:(j + 1) * 512],
                               [(gs[:, kc, (tq * 2 + j) * 128:(tq * 2 + j + 1) * 128], hT[:, kc, :]) for kc in range(8)],
                               True, True, None) for j in range(2)]
                    pe_groups(groups, r_h + [r_gs], [r_mm[mg_]])
                    si = nxt("stg", 2)
                    S.op("act", lambda h, mg_=mg_, si=si: h.activation(out=sgl[:, si, :], in_=mm[mg_][:, :], func=AF.Silu),
                         reads=[r_mm[mg_]], writes=[r_stg[si]])
                    mu_ = nxt("mm", 2)
                    groups = [(mm[mu_][:, j * 512:(j + 1) * 512],
                               [(us[:, kc, (tq * 2 + j) * 128:(tq * 2 + j + 1) * 128], hT[:, kc, :]) for kc in range(8)],
                               True, True, None) for j in range(2)]
                    pe_groups(groups, r_h + [r_us], [r_mm[mu_]])
                    S.op("dve", lambda h, mu_=mu_, si=si, fi=fi: h.tensor_tensor(
                        out=ffT[:, fi * 2:fi * 2 + 2, :].rearrange("p a b -> p (a b)"), in0=sgl[:, si, :], in1=mm[mu_][:, :], op=ALU.mult),
                        reads=[r_mm[mu_], r_stg[si]], writes=[r_ff[fi]])
            for s in range(6):
                ds, r_ds = load_slab(31 + s)
                nk = 4 if s < 5 else 2
                for tb in range(4):
                    mi = nxt("mm", 2)
                    groups = [(mm[mi][:, hf * 512:(hf + 1) * 512],
                               [(ffT[:, s * 4 + kl, tb * 128:(tb + 1) * 128], ds[:, kl, hf * 512:(hf + 1) * 512]) for kl in range(nk)],
                               True, True, None) for hf in range(2)]
                    pe_groups(groups, [r_ff[s * 2 + kl // 2] for kl in range(0, nk, 2)] + [r_ds], [r_mm[mi]])
                    S.op("dve", lambda h, mi=mi, tb=tb: h.tensor_tensor(out=xbuf[:, tb, :], in0=xbuf[:, tb, :], in1=mm[mi][:, :], op=ALU.add),
                         reads=[r_mm[mi]], writes=[r_x[tb]])
                    if nxt_norm is not None and (s * 4 + tb) % 2 == 1:
                        next(nxt_norm, None)
            if nxt_norm is not None:
                for _ in nxt_norm:
                    pass
            for tb in range(4):
                sl = nxt("xs", 2)
                S.op("act", lambda h, tb=tb, sl=sl: h.activation(out=xs[:, sl, :], in_=xbuf[:, tb, :], func=AF.Square,
                                                                   accum_out=ss[:, tb:tb + 1]),
                     reads=[r_x[tb]], writes=[r_xs[sl], r_small[tb]])
                S.op("dve", lambda h, tb=tb: h.tensor_scalar(out=sd[:, tb:tb + 1], in0=ss[:, tb:tb + 1], scalar1=1.0 / D, scalar2=EPS,
                                                             op0=ALU.mult, op1=ALU.add),
                     reads=[r_small[tb]], writes=[r_small[tb]])
                S.op("pool", lambda h, tb=tb: h.tensor_tensor(out=rstd[:, tb:tb + 1], in0=sd[:, tb:tb + 1], in1=nhalf[:, 0:1], op=ALU.pow),
                     reads=[r_small[tb], r_c], writes=[r_small[tb]])
                S.op("dve", lambda h, tb=tb: h.scalar_tensor_tensor(out=xbuf[:, tb, :], in0=xbuf[:, tb, :], scalar=rstd[:, tb:tb + 1],
                                                                    in1=gfin[:, :], op0=ALU.mult, op1=ALU.mult),
                     reads=[r_x[tb], r_small[tb], r_c], writes=[r_x[tb]])
                r0 = (ci - FIRST_OWN) * T + tb * 128
                S.dma("pool", f"st{tb}", y[r0:r0 + 128, :], xbuf[:, tb, :], reads=[r_x[tb]])

        load_x(0)
        emit_cv(NSLAB)
        norm_to_hT(g1c, 0)
        for ci in range(nchunks):
            nn = None
            own = ci >= FIRST_OWN
            if ci + 1 < nchunks:
                if not own:
                    load_x(ci + 1)
                nn = norm_gen(g1c, (ci + 1) % 2)
            if ci == FIRST_OWN:
                tap("hT", hT[:, :, :], r_h)
            if not own:
                qkv_stage(ci, 2, False)
                if ci == FIRST_OWN - 1:
                    qkv_stage(ci, 0, False)
                    qkv_stage(ci, 1, False)
                if nn is not None:
                    for _ in nn:
                        pass
                continue
            S.fence(r_ff + r_mg + r_gb, r_q + [r_u, r_sg] + r_at)
            S.fence([r_t1], [r_vz, r_vt, r_sgt])
            for g in range(3):
                qkv_stage(ci, g, True)
            if ci == FIRST_OWN:
                tap("qT", qT, r_q)
                for g in range(3):
                    tap(f"kr{g}", kr[g][:, :, :, :], r_k[g])
                    tap(f"vr{g}", vr[g][:, :, :], r_v[g])
            sgu_pre(ci)
            ga_ = attention(ci)
            gs_ = sgu_stage(ci)
            k_ = 0
            for _ in ga_:
                k_ += 1
                if k_ % 9 == 4:
                    next(gs_, None)
            for _ in gs_:
                pass
            if ci == FIRST_OWN:
                tap("attnT", attnT, r_at)
                tap("sguT", sguT, [r_sg])
                tap("uT", uT, [r_u])
            if ci + 1 < nchunks:
                load_x(ci + 1)
            merge_stage(ci)
            if ci == FIRST_OWN:
                tap("mergedT", mergedT, r_mg)
                tap("x1", xbuf2[:, ci % 2], r_xb[ci % 2])
            ffn_stage(ci, nn)

        fin = [(f"st{i}", S.cnt[f"st{i}"]) for i in range(4)] + [("tap", S.cnt["tap"])]
        S.wait_all("sp", fin)
        S.emit()
    return nc


def make_in_maps(x, positions, norm1_g, w_in, sgu_ln_g, sgu_ln_b, w_spatial, b_spatial, w_proj_attn, w_proj_sgu,
                 w_out, norm2_g, w_ffn_gate, w_ffn_up, w_ffn_down, final_g):
    f32 = np.float32
    x = np.asarray(x, f32)
    positions = np.asarray(positions, np.int32)
    shared = {
        "w_in": np.ascontiguousarray(np.asarray(w_in, f32)[0]),
        "wpa": np.ascontiguousarray(np.asarray(w_proj_attn, f32)[0]),
        "wps": np.ascontiguousarray(np.asarray(w_proj_sgu, f32)[0]),
        "wout": np.ascontiguousarray(np.asarray(w_out, f32)[0]),
        "wg": np.ascontiguousarray(np.asarray(w_ffn_gate, f32)[0]),
        "wu": np.ascontiguousarray(np.asarray(w_ffn_up, f32)[0]),
        "wd": np.ascontiguousarray(np.asarray(w_ffn_down, f32)[0]),
        "g1c": np.ascontiguousarray(np.asarray(norm1_g, f32)[0].reshape(8, 128).T),
        "g2c": np.ascontiguousarray(np.asarray(norm2_g, f32)[0].reshape(8, 128).T),
        "gfin": np.ascontiguousarray(np.broadcast_to(np.asarray(final_g, f32)[None, :], (128, D))),
        "lng": np.ascontiguousarray(np.broadcast_to(np.asarray(sgu_ln_g, f32)[0][None, :], (128, 512))),
        "lnb": np.ascontiguousarray(np.broadcast_to(np.asarray(sgu_ln_b, f32)[0][None, :], (128, 512))),
        "bsp": np.ascontiguousarray(np.repeat(np.asarray(b_spatial, f32)[0].reshape(4, 2, 128), 64, axis=1)
                                    .transpose(1, 0, 2).reshape(128, 512)),
        "wsp": np.ascontiguousarray(np.asarray(w_spatial, f32)[0].transpose(2, 0, 1).reshape(128, 1024)),
    }
    in_maps = []
    for core in range(NCORE):
        b, half = core // 2, core % 2
        s0 = half * OWN
        xin = np.zeros((OWN + HALO, D), f32)
        pin = np.zeros((OWN + HALO,), np.int32)
        xin[HALO:] = x[b, s0:s0 + OWN]
        pin[HALO:] = positions[b, s0:s0 + OWN]
        if half == 1:
            xin[:HALO] = x[b, s0 - HALO:s0]
            pin[:HALO] = positions[b, s0 - HALO:s0]
        m = dict(shared)
        m["xin"] = xin
        m["pos"] = np.ascontiguousarray(pin.reshape(48, 128).T)
        m["hbias"] = np.full((128, 1), 0.0 if half == 1 else NEG, f32)
        in_maps.append(m)
    return in_maps


_NC_CACHE = {}


def kernel(x, positions, norm1_g, w_in, sgu_ln_g, sgu_ln_b, w_spatial, b_spatial, w_proj_attn, w_proj_sgu,
           w_out, norm2_g, w_ffn_gate, w_ffn_up, w_ffn_down, final_g):
    in_maps = make_in_maps(x, positions, norm1_g, w_in, sgu_ln_g, sgu_ln_b, w_spatial, b_spatial, w_proj_attn,
                           w_proj_sgu, w_out, norm2_g, w_ffn_gate, w_ffn_up, w_ffn_down, final_g)
    if "nc" not in _NC_CACHE:
        _NC_CACHE["nc"] = build_program()
    nc = _NC_CACHE["nc"]
    res = run_bass_kernel_spmd(nc, in_maps, core_ids=list(range(NCORE)))
    out = np.zeros((4, SEQ, D), np.float32)
    for core in range(NCORE):
        b, half = core // 2, core % 2
        out[b, half * OWN:(half + 1) * OWN] = np.asarray(res.results[core]["y"], np.float32)
    return out
```

```python
import contextlib
import os
import numpy as np
import concourse.bass as bass
import concourse.mybir as mybir
from concourse.bass_utils import run_bass_kernel_spmd

F32 = mybir.dt.float32
BF16 = mybir.dt.bfloat16
I32 = mybir.dt.int32
AF = mybir.ActivationFunctionType
ALU = mybir.AluOpType

D = 1024
SEQ = 8192
NCORE = 8
OWN = 4096
HALO = 2048
T = 512
NCH = (OWN + HALO) // T
FIRST_OWN = HALO // T
DFF = 2816
NFT = DFF // 128
EPS = 1e-6
ROPE_THETA = 500000.0
NSLAB = 37
NEG = -30000.0


class Res:
    __slots__ = ("name", "last_w", "readers")

    def __init__(self, name):
        self.name = name
        self.last_w = None
        self.readers = {}


class Sched:
    ENG = ("pe", "act", "dve", "pool", "sp")

    def __init__(self, nc, stack):
        self.nc = nc
        self.stack = stack
        self.prog = {e: [] for e in self.ENG}
        self.sem = {}
        self.cnt = {}
        self.waited = {e: {} for e in self.ENG}
        for e in self.ENG:
            self.new_sem("c_" + e)

    def new_sem(self, name):
        s = self.stack.enter_context(self.nc.semaphore(name))
        self.sem[name] = s
        self.cnt[name] = 0
        return name

    def _deps(self, eng, reads, writes, deps):
        toks = {}

        def add(t):
            if t is None:
                return
            s, v = t
            if toks.get(s, 0) < v:
                toks[s] = v
        for t in deps:
            add(t)
        for r in reads:
            add(r.last_w)
        for r in writes:
            add(r.last_w)
            for s, v in r.readers.items():
                add((s, v))
        own = "c_" + eng
        for s, v in toks.items():
            if s == own and eng == "pe":
                continue
            if self.waited[eng].get(s, 0) < v:
                self.prog[eng].append(("wait", s, v))
                self.waited[eng][s] = v

    def op(self, eng, fn, reads=(), writes=(), deps=(), sem=None, inc=1):
        self._deps(eng, reads, writes, deps)
        s = sem or ("c_" + eng)
        self.cnt[s] += inc
        tok = (s, self.cnt[s])
        self.prog[eng].append(("inst", fn, s, inc))
        for r in reads:
            if r.readers.get(s, 0) < tok[1]:
                r.readers[s] = tok[1]
        for r in writes:
            r.last_w = tok
            r.readers = {}
        return tok

    def dma(self, eng, sem, out, in_, reads=(), writes=(), deps=()):
        def fn(h):
            return h.dma_start(out=out, in_=in_)
        return self.op(eng, fn, reads=reads, writes=writes, deps=deps, sem=sem, inc=16)

    def wait_all(self, eng, toks):
        for t in toks:
            if t is None:
                continue
            s, v = t
            if self.waited[eng].get(s, 0) < v:
                self.prog[eng].append(("wait", s, v))
                self.waited[eng][s] = v

    def fence(self, src, dst):
        for d in dst:
            for r in src:
                if r.last_w is not None:
                    s, v = r.last_w
                    if d.readers.get(s, 0) < v:
                        d.readers[s] = v
                for s, v in r.readers.items():
                    if d.readers.get(s, 0) < v:
                        d.readers[s] = v

    def emit(self):
        nc = self.nc
        with nc.Block() as block:
            def run(e):
                def body(h):
                    for it in self.prog[e]:
                        if it[0] == "wait":
                            h.wait_ge(self.sem[it[1]], it[2])
                        else:
                            ins = it[1](h)
                            ins.then_inc(self.sem[it[2]], it[3])
                return body
            block.tensor(run("pe"))
            block.scalar(run("act"))
            block.vector(run("dve"))
            block.gpsimd(run("pool"))
            block.sync(run("sp"))


def slab_src(wd, s):
    if s < 15:
        return wd["w_in"][:, s * 512:(s + 1) * 512].rearrange("(k p) c -> p k c", p=128), 8, 512
    if s == 15:
        return wd["wpa"].rearrange("(k p) c -> p k c", p=128), 4, 1024
    if s == 16:
        return wd["wps"].rearrange("(k p) c -> p k c", p=128), 4, 1024
    if s < 19:
        h = s - 17
        return wd["wout"][:, h * 512:(h + 1) * 512].rearrange("(k p) c -> p k c", p=128), 8, 512
    if s < 31:
        nm = "wg" if s < 25 else "wu"
        j = (s - 19) % 6
        c = 512 if j < 5 else 256
        return wd[nm][:, j * 512:j * 512 + c].rearrange("(k p) c -> p k c", p=128), 8, c
    j = s - 31
    k = 4 if j < 5 else 2
    return wd["wd"][j * 512:j * 512 + k * 128, :].rearrange("(k p) c -> p k c", p=128), k, 1024


def build_program(taps=None, nchunks=NCH, stages=5):
    taps = taps or {}
    nc = bass.Bass("TRN2", target_bir_lowering=False)

    def din(name, shape, dt=F32):
        return nc.dram_tensor(name, shape, dt, kind="ExternalInput").ap()

    xin = din("xin", [OWN + HALO, D])
    pos = din("pos", [128, 48], I32)
    hbias_d = din("hbias", [128, 1])
    wd = {"w_in": din("w_in", [D, 7680]), "wpa": din("wpa", [512, D]), "wps": din("wps", [512, D]),
          "wout": din("wout", [D, D]), "wg": din("wg", [D, DFF]), "wu": din("wu", [D, DFF]),
          "wd": din("wd", [DFF, D])}
    g1c_d = din("g1c", [128, 8])
    g2c_d = din("g2c", [128, 8])
    gfin_d = din("gfin", [128, D])
    lng_d = din("lng", [128, 512])
    lnb_d = din("lnb", [128, 512])
    bsp_d = din("bsp", [128, 512])
    wsp_d = din("wsp", [128, 1024])
    y = nc.dram_tensor("y", [OWN, D], F32, kind="ExternalOutput").ap()
    wbf = nc.dram_tensor("wbf", [NSLAB, 128, 4096], BF16, kind="Internal").ap()
    tap_out = {}
    for nm, (shape, dt) in taps.items():
        tap_out[nm] = nc.dram_tensor("tap_" + nm, shape, dt, kind="ExternalOutput").ap()

    with contextlib.ExitStack() as st:
        S = Sched(nc, st)

        def sb(name, shape, dt):
            return st.enter_context(nc.sbuf_tensor("s_" + name, shape, dt))

        def ps(name, shape, dt):
            return st.enter_context(nc.psum_tensor("p_" + name, shape, dt))

        xbuf2 = sb("xbuf", [128, 2, 4, D], F32)
        hT = sb("hT", [128, 8, T], BF16)
        wring = sb("wring", [128, 3, 4096], BF16)
        kr = [sb("kr0", [128, 4, 8, 128], BF16), sb("kr1", [128, 4, 8, 128], BF16),
              sb("kr2", [128, 4, 20, 128], BF16)]
        vr = [sb("vr0", [128, 8, 512], BF16), sb("vr1", [128, 8, 512], BF16), sb("vr2", [128, 20, 512], BF16)]
        NSL = [8, 8, 20]
        sA = sb("sA", [128, 12288], BF16)
        qT = sA[:, 0:6144].rearrange("p (g f t) -> p g f t", g=3, f=4)
        uT = sA[:, 6144:8192].rearrange("p (f t) -> p f t", f=4)
        attnT = sA[:, 8192:10240].rearrange("p (f t) -> p f t", f=4)
        sguT = sA[:, 10240:12288].rearrange("p (f t) -> p f t", f=4)
        mergedT = sA[:, 0:4096].rearrange("p (f t) -> p f t", f=8)
        gbT = sA[:, 4096:8192].rearrange("p (f t) -> p f t", f=8)
        ffT = sA[:, 0:NFT * T].rearrange("p (f t) -> p f t", f=NFT)
        stg = sb("stg", [128, 2, 1024], BF16)
        sgl = stg
        xs = stg
        pbuf = sb("pbuf", [128, 4, 512], BF16)
        qz = sb("qz", [128, 2, 2, 512], BF16)
        accN = sb("accN", [128, 512], F32)
        accD = sb("accD", [128, 512], F32)
        sgm = sb("sgm", [128, 2048], F32)
        vz = sgm[:, 0:1024].bitcast(BF16).rearrange("p (j c) -> p j c", j=4)
        vt = sgm[:, 1024:1536]
        sgt = sgm[:, 1536:2048]
        t1 = sgm[:, 0:1024]
        ropet = sgm[:, 0:512].rearrange("p (j c) -> p j c", j=4)
        vnA = sb("vnA", [128, 2, 512], BF16)
        vnB = sb("vnB", [128, 2, 512], BF16)
        mtb = sb("mtb", [128, 512], BF16)
        m3 = sb("m3", [128, 3, 512], BF16)
        cosT = sb("cosT", [128, 48, 8], F32)
        sinT = sb("sinT", [128, 48, 8], F32)
        gfin = sb("gfin", [128, D], F32)
        lng = sb("lng", [128, 512], F32)
        lnb = sb("lnb", [128, 512], F32)
        bsp = sb("bsp", [128, 512], F32)
        WcT = sb("WcT", [128, 8, 128], BF16)
        ident = sb("ident", [128, 128], BF16)
        ones64 = sb("ones64", [128, 64], BF16)
        g1c = sb("g1c", [128, 8], F32)
        g2c = sb("g2c", [128, 8], F32)
        hbias = sb("hbias", [128, 1], F32)
        epsb = sb("epsb", [128, 1], F32)
        small = sb("small", [128, 64], F32)
        ss = small[:, 0:4]
        sd = small[:, 4:8]
        rstd = small[:, 8:12]
        bn6 = small[:, 16:28].rearrange("p (j s) -> p j s", j=2)
        mv = small[:, 28:32].rearrange("p (j s) -> p j s", j=2)
        lsd = small[:, 32:34]
        lrs = small[:, 34:36]
        nhalf = small[:, 40:44]

        mm = [ps("mm0", [128, 1024], F32), ps("mm1", [128, 1024], F32)]
        sps = [ps("sps0", [128, 512], F32), ps("sps1", [128, 512], F32)]
        pvN = ps("pvN", [128, 512], F32)
        pvD = ps("pvD", [128, 512], F32)

        R = lambda n: Res(n)
        r_xb = [[R(f"x{b}_{i}") for i in range(4)] for b in range(2)]
        r_x = r_xb[0]
        r_h = [R(f"h{i}") for i in range(4)]
        r_w = [R(f"w{i}") for i in range(3)]
        r_k = [[R(f"k{g}_{s}") for s in range(NSL[g])] for g in range(3)]
        r_v = [[R(f"v{g}_{s}") for s in range(NSL[g])] for g in range(3)]
        r_q = [R(f"q{g}") for g in range(3)]
        r_u = R("uT")
        r_at = [R(f"at{p}") for p in range(4)]
        r_sg = R("sguT")
        r_mg = [R(f"mg{i}") for i in range(4)]
        r_gb = [R(f"gb{i}") for i in range(4)]
        r_ff = [R(f"ff{i}") for i in range(NFT // 2)]
        r_stg = [R("stg0"), R("stg1")]
        r_xs = r_stg
        r_pb = [R(f"pb{i}") for i in range(4)]
        r_qz = [R("qz0"), R("qz1")]
        r_acc = R("acc")
        r_vz = R("vz")
        r_rt = r_vz
        r_vt = R("vt")
        r_sgt = R("sgt")
        r_t1 = R("t1")
        r_vn = [R("vn0"), R("vn1")]
        r_mm = [R("mm0"), R("mm1")]
        r_sps = [R("sps0"), R("sps1")]
        r_pv = R("pv")
        r_pvd = R("pvd")
        r_c = R("consts")
        r_small = [R(f"sm{i}") for i in range(4)]
        r_ln = R("lnsmall")
        r_cv = [R(f"cv{s}") for s in range(NSLAB)]
        for i in range(4):
            S.new_sem(f"xs0{i}")
            S.new_sem(f"xs1{i}")
            S.new_sem(f"st{i}")
        for i in range(3):
            S.new_sem(f"w{i}")
        for s in range(NSLAB):
            S.new_sem(f"cv{s}")
        S.new_sem("ld")
        S.new_sem("tap")

        state = {"mm": 0, "sps": 0, "w": 0, "pb": 0, "stg": 0, "xs": 0, "vn": 0, "qz": 0, "tr": 0}
        r_xs = r_stg
        trt = [t_[:, :].bitcast(BF16).rearrange("p (k t) -> p k t", k=8) for t_ in (sps[0], sps[1], pvN, pvD)]
        r_tr = [r_sps[0], r_sps[1], r_pv, r_pvd]

        def nxt(key, n):
            i = state[key]
            state[key] = (i + 1) % n
            return i

        def load_slab(s):
            _, k, c = slab_src(wd, s)
            slot = nxt("w", 3)
            S.dma("sp", f"w{slot}", wring[:, slot, 0:k * c], wbf[s][:, 0:k * c],
                  reads=[r_cv[s]], writes=[r_w[slot]])
            return wring[:, slot, 0:k * c].rearrange("p (k c) -> p k c", k=k), r_w[slot]

        def load_x(ci):
            b_ = ci % 2
            for tb in range(4):
                r0 = ci * T + tb * 128
                S.dma("sp", f"xs{b_}{tb}", xbuf2[:, b_, tb, :], xin[r0:r0 + 128, :], writes=[r_xb[b_][tb]])

        S.dma("sp", "ld", g1c[:, :], g1c_d, writes=[r_c])
        S.dma("sp", "ld", g2c[:, :], g2c_d, writes=[r_c])
        S.dma("sp", "ld", hbias[:, :], hbias_d, writes=[r_c])
        S.dma("sp", "ld", gfin[:, :], gfin_d, writes=[r_c])
        S.dma("sp", "ld", lng[:, :], lng_d, writes=[r_c])
        S.dma("sp", "ld", lnb[:, :], lnb_d, writes=[r_c])
        S.dma("sp", "ld", bsp[:, :], bsp_d, writes=[r_c])
        xf = xbuf2[:, 0, :, :].rearrange("p a b -> p (a b)")
        tmpA = xf[:, 0:1024]
        tmpI = xf[:, 1024:1536]
        S.dma("sp", "ld", tmpA, wsp_d, writes=r_x)
        S.dma("sp", "ld", tmpI[:, 0:48].bitcast(I32), pos, writes=r_x)
        ld_tok = ("ld", S.cnt["ld"])

        def setup_dve(h):
            onesf = xf[:, 1536:1664]
            trif = xf[:, 1664:1792]
            bandf = xf[:, 1792:1920]
            idf = xf[:, 1920:2048]
            modf = xf[:, 2048:2176]
            modi = xf[:, 2176:2304].bitcast(I32)
            h.memset(onesf, 1.0)
            h.memset(epsb[:, :], EPS)
            h.memset(nhalf, -0.5)
            h.memset(ones64[:, :], 1.0)
            h.memset(vnA[:, :, :], 0.0)
            h.memset(vnB[:, :, :], 0.0)
            h.memset(qz[:, :, :, :], 0.0)
            return h.memset(idf, 0.0)
        S.op("dve", setup_dve, writes=r_x + [r_c], deps=[ld_tok])

        def setup_pool(h):
            onesf = xf[:, 1536:1664]
            trif = xf[:, 1664:1792]
            bandf = xf[:, 1792:1920]
            idf = xf[:, 1920:2048]
            modi = xf[:, 2176:2304].bitcast(I32)
            h.affine_select(out=trif, in_=onesf, pattern=[[1, 128]], compare_op=ALU.is_ge, fill=0.0,
                            base=0, channel_multiplier=-1)
            h.affine_select(out=bandf, in_=onesf, pattern=[[-1, 128]], compare_op=ALU.is_ge, fill=0.0,
                            base=0, channel_multiplier=1)
            h.affine_select(out=idf, in_=onesf, pattern=[[-1, 128]], compare_op=ALU.is_equal, fill=0.0,
                            base=0, channel_multiplier=1)
            return h.iota(modi, pattern=[[1, 128]], base=128, channel_multiplier=-1)
        S.op("pool", setup_pool, reads=r_x, writes=r_x)

        INV2PI = float(1.0 / (2.0 * np.pi))
        C1 = 6.28125
        C2 = float(2.0 * np.pi - 6.28125)
        PI = float(np.pi)
        inv_freq = (ROPE_THETA ** (-np.arange(0, 16, 2, dtype=np.float32) / np.float32(16))).astype(np.float32)

        onesf = xf[:, 1536:1664]
        trif = xf[:, 1664:1792]
        bandf = xf[:, 1792:1920]
        idf = xf[:, 1920:2048]
        modf = xf[:, 2048:2176]
        modi = xf[:, 2176:2304].bitcast(I32)
        posf = xf[:, 2304:2352]
        ang = xf[:, 2432:2816].rearrange("p (b f) -> p b f", f=8)
        angf = xf[:, 2432:2816]
        ki = xf[:, 2816:3200].bitcast(I32)
        kf = xf[:, 3200:3584]
        fx = xf[:, 3584:3968]
        r_set = Res("setup")

        def SD(fn, eng="dve"):
            S.op(eng, fn, reads=r_x + [r_set], writes=r_x + [r_set, r_c])

        SD(lambda h: h.tensor_copy(out=ident[:, :], in_=idf))
        SD(lambda h: h.tensor_single_scalar(out=modi, in_=modi, scalar=3, op=ALU.bitwise_and))
        SD(lambda h: h.tensor_copy(out=modf, in_=modi))
        SD(lambda h: h.tensor_scalar(out=modf, in0=modf, scalar1=0.0, scalar2=None, op0=ALU.is_equal))

        def masks1(h):
            ins = None
            for j in range(2):
                h.tensor_copy(out=mtb[:, j * 128:(j + 1) * 128], in_=trif)
                h.tensor_copy(out=mtb[:, 256 + j * 128:256 + (j + 1) * 128], in_=bandf)
            for j in range(4):
                h.tensor_tensor(out=m3[:, 0, j * 128:(j + 1) * 128], in0=modf, in1=trif, op=ALU.mult)
                h.tensor_copy(out=m3[:, 1, j * 128:(j + 1) * 128], in_=modf)
                ins = h.tensor_tensor(out=m3[:, 2, j * 128:(j + 1) * 128], in0=modf, in1=bandf, op=ALU.mult)
            return ins
        SD(masks1)
        SD(lambda h: h.tensor_scalar(out=mtb[:, :], in0=mtb[:, :], scalar1=-1.0, scalar2=-NEG, op0=ALU.add, op1=ALU.mult))
        SD(lambda h: h.tensor_scalar(out=m3[:, :, :], in0=m3[:, :, :], scalar1=-1.0, scalar2=-NEG, op0=ALU.add, op1=ALU.mult))
        SD(lambda h: h.tensor_tensor(out=WcT[:, :, :], in0=tmpA.rearrange("p (g t) -> p g t", g=8),
                                     in1=trif.unsqueeze(1).to_broadcast([128, 8, 128]), op=ALU.mult))
        SD(lambda h: h.tensor_copy(out=posf, in_=tmpI[:, 0:48].bitcast(I32)))

        def angs(h):
            ins = None
            for f in range(8):
                ins = h.tensor_scalar(out=ang[:, :, f], in0=posf, scalar1=float(inv_freq[f]), scalar2=None, op0=ALU.mult)
            return ins
        SD(angs)
        SD(lambda h: h.tensor_scalar(out=ki, in0=angf, scalar1=INV2PI, scalar2=None, op0=ALU.mult))
        SD(lambda h: h.tensor_copy(out=kf, in_=ki))
        SD(lambda h: h.scalar_tensor_tensor(out=angf, in0=kf, scalar=-C1, in1=angf, op0=ALU.mult, op1=ALU.add))
        SD(lambda h: h.scalar_tensor_tensor(out=angf, in0=kf, scalar=-C2, in1=angf, op0=ALU.mult, op1=ALU.add))
        SD(lambda h: h.tensor_scalar(out=fx, in0=angf, scalar1=PI, scalar2=-2.0 * PI, op0=ALU.is_gt, op1=ALU.mult))
        SD(lambda h: h.tensor_tensor(out=angf, in0=angf, in1=fx, op=ALU.add))
        SD(lambda h: h.tensor_scalar(out=fx, in0=angf, scalar1=-PI, scalar2=2.0 * PI, op0=ALU.is_lt, op1=ALU.mult))
        SD(lambda h: h.tensor_tensor(out=angf, in0=angf, in1=fx, op=ALU.add))
        SD(lambda h: h.activation(out=sinT[:, :, :].rearrange("p b f -> p (b f)"), in_=angf, func=AF.Sin), eng="act")
        SD(lambda h: h.tensor_scalar(out=angf, in0=angf, scalar1=PI / 2.0, scalar2=None, op0=ALU.add))
        SD(lambda h: h.tensor_scalar(out=fx, in0=angf, scalar1=PI, scalar2=-2.0 * PI, op0=ALU.is_gt, op1=ALU.mult))
        SD(lambda h: h.tensor_tensor(out=angf, in0=angf, in1=fx, op=ALU.add))
        SD(lambda h: h.activation(out=cosT[:, :, :].rearrange("p b f -> p (b f)"), in_=angf, func=AF.Sin), eng="act")

        cv_order = [5, 8, 3, 6, 4, 7, 0, 1, 2, 9, 10, 11, 12, 15, 13, 14, 16, 17, 18]
        for j in range(6):
            cv_order += [19 + j, 25 + j]
        cv_order += list(range(31, 37))
        assert sorted(cv_order) == list(range(NSLAB))
        cv_state = {"i": 0, "toks": []}

        def emit_cv(n):
            for _ in range(n):
                i = cv_state["i"]
                if i >= NSLAB:
                    return
                sl_ = cv_order[i]
                if i >= 3:
                    S.wait_all("pool", [cv_state["toks"][i - 3]])
                src, k, c = slab_src(wd, sl_)
                dst = wbf[sl_][:, 0:k * c].rearrange("p (k c) -> p k c", k=k)
                cv_state["toks"].append(S.dma("pool", f"cv{sl_}", dst, src, writes=[r_cv[sl_]]))
                cv_state["i"] = i + 1

        def pe_groups(groups, reads, writes):
            def fn(h):
                ins = None
                for out, pairs, st0, sp0, tp in groups:
                    n = len(pairs)
                    for i, (l, r) in enumerate(pairs):
                        kw = {}
                        if tp is not None:
                            kw["tile_position"] = tp
                        ins = h.matmul(out, lhsT=l, rhs=r, start=(st0 and i == 0), stop=(sp0 and i == n - 1), **kw)
                return ins
            return S.op("pe", fn, reads=reads, writes=writes)

        def tap(name, ap, reads):
            if name in tap_out:
                S.dma("sp", "tap", tap_out[name], ap, reads=reads)

        def emit_rstd(tb, early):
            if early:
                S.op("act", lambda h, tb=tb: h.activation(out=sd[:, tb:tb + 1], in_=ss[:, tb:tb + 1], func=AF.Sqrt,
                                                          scale=1.0 / D, bias=epsb[:, 0:1]),
                     reads=[r_small[tb], r_c], writes=[r_small[tb]])
                S.op("dve", lambda h, tb=tb: h.reciprocal(out=rstd[:, tb:tb + 1], in_=sd[:, tb:tb + 1]),
                     reads=[r_small[tb]], writes=[r_small[tb]])
            else:
                S.op("dve", lambda h, tb=tb: h.tensor_scalar(out=sd[:, tb:tb + 1], in0=ss[:, tb:tb + 1], scalar1=1.0 / D, scalar2=EPS,
                                                             op0=ALU.mult, op1=ALU.add),
                     reads=[r_small[tb]], writes=[r_small[tb]])
                S.op("pool", lambda h, tb=tb: h.tensor_tensor(out=rstd[:, tb:tb + 1], in0=sd[:, tb:tb + 1], in1=nhalf[:, 0:1], op=ALU.pow),
                     reads=[r_small[tb], r_c], writes=[r_small[tb]])

        def norm_gen(gcol, b_, early=False):
            xb_ = xbuf2[:, b_]
            rx_ = r_xb[b_]
            sls = {}

            def A(tb):
                sl = nxt("xs", 2)
                sls[tb] = sl
                S.op("act", lambda h, tb=tb, sl=sl: h.activation(out=xs[:, sl, :], in_=xb_[:, tb, :], func=AF.Square,
                                                                   accum_out=ss[:, tb:tb + 1]),
                     reads=[rx_[tb]], writes=[r_xs[sl], r_small[tb]])
                emit_rstd(tb, early)
                S.op("act", lambda h, tb=tb, sl=sl: h.activation(out=xs[:, sl, :], in_=xb_[:, tb, :], func=AF.Copy,
                                                                   scale=rstd[:, tb:tb + 1]),
                     reads=[rx_[tb], r_small[tb]], writes=[r_xs[sl]])

            def Tr(tb):
                sl = sls[tb]
                ti = nxt("tr", 4)
                trp = trt[ti]

                def tr(h, sl=sl, trp=trp):
                    ins = None
                    for kc in range(8):
                        ins = h.transpose(out=trp[:, kc, :], in_=xs[:, sl, kc * 128:(kc + 1) * 128], identity=ident[:, :])
                    return ins
                S.op("pe", tr, reads=[r_xs[sl], r_c], writes=[r_tr[ti]])
                S.op("dve", lambda h, tb=tb, trp=trp: h.tensor_tensor(
                    out=hT[:, :, tb * 128:(tb + 1) * 128], in0=trp,
                    in1=gcol[:, :].unsqueeze(2).to_broadcast([128, 8, 128]), op=ALU.mult),
                    reads=[r_tr[ti], r_c], writes=[r_h[tb]])
            A(0); yield
            A(1); yield
            Tr(0); yield
            A(2); yield
            Tr(1); yield
            A(3); yield
            Tr(2); yield
            Tr(3)

        def norm_to_hT(gcol, b_, early=False):
            for _ in norm_gen(gcol, b_, early):
                pass

        def tok_sel(g, blk):
            if g == 0:
                return lambda kc: hT[:, kc, blk * 128:(blk + 1) * 128]
            return lambda kc: hT[:, kc, :].rearrange("p (i c) -> p c i", c=4)[:, blk, :]

        def qkv_stage(ci, g, with_q):
            slot0 = 4 * (ci % (NSL[g] // 4))
            kslab, r_ks = load_slab(3 + g)
            if with_q:
                qslab, r_qs = load_slab(g)
            nf = 8 if with_q else 4
            pend = []

            def emit_tr(tb, si):
                mj = nxt("tr", 4)
                trp = trt[mj]

                def tr(h, si=si, trp=trp, nf=nf):
                    ins = None
                    for f in range(nf):
                        ins = h.transpose(out=trp[:, f, :], in_=stg[:, si, f * 128:(f + 1) * 128], identity=ident[:, :])
                    return ins
                S.op("pe", tr, reads=[r_stg[si], r_c], writes=[r_tr[mj]])
                kf0 = 4 if with_q else 0
                if g == 0:
                    if with_q:
                        S.op("dve", lambda h, trp=trp, tb=tb: h.tensor_copy(out=qT[:, 0, :, tb * 128:(tb + 1) * 128], in_=trp[:, 0:4, :]),
                             reads=[r_tr[mj]], writes=[r_q[0]])
                    S.op("dve", lambda h, trp=trp, tb=tb, kf0=kf0: h.tensor_copy(out=kr[0][:, :, slot0 + tb, :], in_=trp[:, kf0:kf0 + 4, :]),
                         reads=[r_tr[mj]], writes=[r_k[0][slot0 + tb]])
                else:
                    if with_q:
                        S.op("dve", lambda h, trp=trp, tb=tb: h.tensor_copy(
                            out=qT[:, g, :, :].rearrange("p f (c j) -> p f c j", c=4)[:, :, :, tb * 32:(tb + 1) * 32],
                            in_=trp[:, 0:4, :].rearrange("p f (i c) -> p f c i", c=4)),
                            reads=[r_tr[mj]], writes=[r_q[g]])
                    S.op("dve", lambda h, trp=trp, tb=tb, kf0=kf0: h.tensor_copy(
                        out=kr[g][:, :, slot0:slot0 + 4, tb * 32:(tb + 1) * 32],
                        in_=trp[:, kf0:kf0 + 4, :].rearrange("p f (i c) -> p f c i", c=4)),
                        reads=[r_tr[mj]], writes=[r_k[g][slot0 + c] for c in range(4)])

            for tb in range(4):
                mi = nxt("mm", 2)
                groups = []
                rd = [r_h[tb], r_ks]
                if with_q:
                    groups.append((mm[mi][:, 0:512], [(hT[:, kc, tb * 128:(tb + 1) * 128], qslab[:, kc, :]) for kc in range(8)],
                                   True, True, None))
                    rd.append(r_qs)
                    kcol = 512
                else:
                    kcol = 0
                groups.append((mm[mi][:, kcol:kcol + 512], [(hT[:, kc, tb * 128:(tb + 1) * 128], kslab[:, kc, :]) for kc in range(8)],
                               True, True, None))
                pe_groups(groups, rd, [r_mm[mi]])
                W = kcol + 512
                si = nxt("stg", 2)
                S.op("act", lambda h, mi=mi, si=si, W=W: h.copy(out=stg[:, si, 0:W], in_=mm[mi][:, 0:W]),
                     reads=[r_mm[mi]], writes=[r_stg[si]])
                nh = W // 64
                blk = ci * 4 + tb

                def rope_views(mi, si, W, nh, blk):
                    mv_ = mm[mi][:, 0:W].rearrange("p (h d) -> p h d", d=64)
                    sv = stg[:, si, 0:W].rearrange("p (h d) -> p h d", d=64)
                    cb = cosT[:, blk, :].unsqueeze(1).to_broadcast([128, nh, 8])
                    sn = sinT[:, blk, :].unsqueeze(1).to_broadcast([128, nh, 8])
                    tt = [ropet[:, j, 0:nh * 8].rearrange("p (h d) -> p h d", d=8) for j in range(4)]
                    return mv_, sv, cb, sn, tt

                def rope1(h, a=(mi, si, W, nh, blk)):
                    mv_, sv, cb, sn, tt = rope_views(*a)
                    x1 = mv_[:, :, 0:8]
                    x2 = mv_[:, :, 8:16]
                    h.tensor_tensor(out=tt[0], in0=x1, in1=cb, op=ALU.mult)
                    h.tensor_tensor(out=tt[1], in0=x2, in1=sn, op=ALU.mult)
                    h.tensor_tensor(out=tt[2], in0=x2, in1=cb, op=ALU.mult)
                    return h.tensor_tensor(out=tt[3], in0=x1, in1=sn, op=ALU.mult)

                def rope2(h, a=(mi, si, W, nh, blk)):
                    mv_, sv, cb, sn, tt = rope_views(*a)
                    h.tensor_tensor(out=sv[:, :, 0:8], in0=tt[0], in1=tt[1], op=ALU.subtract)
                    return h.tensor_tensor(out=sv[:, :, 8:16], in0=tt[2], in1=tt[3], op=ALU.add)
                import os
                if "norope1" not in os.environ.get("ATT_DBG", ""):
                    S.op("dve", rope1, reads=[r_mm[mi], r_c, r_stg[si]], writes=[r_rt])
                if "norope2" not in os.environ.get("ATT_DBG", ""):
                    S.op("dve", rope2, reads=[r_rt], writes=[r_stg[si]])
                pend.append((tb, si))
                if len(pend) > 1:
                    emit_tr(*pend.pop(0))
            while pend:
                emit_tr(*pend.pop(0))
            vslab, r_vs = load_slab(6 + g)
            for bp in range(2):
                mi = nxt("mm", 2)
                groups = []
                for j in range(2):
                    sel = tok_sel(g, bp * 2 + j)
                    groups.append((mm[mi][:, j * 512:(j + 1) * 512], [(sel(kc), vslab[:, kc, :]) for kc in range(8)], True, True, None))
                pe_groups(groups, r_h + [r_vs], [r_mm[mi]])
                s0 = slot0 + bp * 2
                S.op("act", lambda h, mi=mi, s0=s0: h.copy(out=vr[g][:, s0:s0 + 2, :], in_=mm[mi][:, :].rearrange("p (j c) -> p j c", j=2)),
                     reads=[r_mm[mi]], writes=[r_v[g][s0], r_v[g][s0 + 1]])

        def attention(ci):
            tasks = []
            pvsets = [(pvN[:, :], pvD[:, :], [r_pv, r_pvd]), (mm[1][:, 0:512], mm[1][:, 512:1024], [r_mm[1]])]
            for p in range(4):
                hA, hB = 2 * p, 2 * p + 1
                for g in range(2):
                    pvN_, pvD_, rpv_ = pvsets[(p * 3 + g) % 2]
                    for qb in range(4):
                        n = 4 * ci + qb
                        own_s = n % 8
                        prv = n - 1 if g == 0 else n - 4
                        prv_s = prv % 8
                        halo_prev = prv < 4 * FIRST_OWN
                        qk = []
                        for j, (X, ks) in enumerate([(0, own_s), (1, own_s), (0, prv_s), (1, prv_s)]):
                            qk.append((j * 128, kr[g][:, p, ks, :], X, slice(qb * 128, (qb + 1) * 128)))
                        pv = []
                        qc = slice(qb * 128, (qb + 1) * 128)
                        pv.append((pvN_[0:64, qc], vr[g][:, own_s, hA * 64:(hA + 1) * 64], 0, True, False, (0, 0)))
                        pv.append((pvN_[0:64, qc], vr[g][:, prv_s, hA * 64:(hA + 1) * 64], 256, False, True, (0, 0)))
                        pv.append((pvN_[64:128, qc], vr[g][:, own_s, hB * 64:(hB + 1) * 64], 128, True, False, (0, 64)))
                        pv.append((pvN_[64:128, qc], vr[g][:, prv_s, hB * 64:(hB + 1) * 64], 384, False, True, (0, 64)))
                        pv.append((pvD_[0:64, qc], ones64[:, :], 0, True, False, (0, 0)))
                        pv.append((pvD_[0:64, qc], ones64[:, :], 256, False, True, (0, 0)))
                        pv.append((pvD_[64:128, qc], ones64[:, :], 128, True, False, (0, 64)))
                        pv.append((pvD_[64:128, qc], ones64[:, :], 384, False, True, (0, 64)))
                        tasks.append(dict(p=p, g=g, qk=qk, pv=pv, mask=mtb[:, :],
                                          bias=[(0, 256, False), (256, 512, True)] if halo_prev else [(0, 512, False)],
                                          rk=[r_k[g][own_s], r_k[g][prv_s]], rv=[r_v[g][own_s], r_v[g][prv_s]],
                                          first=(qb == 0), last=(qb == 3), pvs=(pvN_, pvD_, rpv_)))
                g = 2
                pvN_, pvD_, rpv_ = pvsets[(p * 3 + g) % 2]
                for X, (hh, tp, prow) in enumerate([(hA, (0, 0), slice(0, 64)), (hB, (0, 64), slice(64, 128))]):
                    for d in (4, 3, 2, 1, 0):
                        cj = ci - d
                        qk, pv, rk, rv = [], [], [], []
                        for m in range(4):
                            sl = (4 * cj + m) % 20
                            qk.append((m * 128, kr[2][:, p, sl, :], X, slice(m * 128, (m + 1) * 128)))
                            qc = slice(m * 128, (m + 1) * 128)
                            pv.append((pvN_[prow, qc], vr[2][:, sl, hh * 64:(hh + 1) * 64], m * 128, d == 4 and m == 0, d == 0 and m == 3, tp))
                            pv.append((pvD_[prow, qc], ones64[:, :], m * 128, d == 4 and m == 0, d == 0 and m == 3, tp))
                            rk.append(r_k[2][sl])
                            rv.append(r_v[2][sl])
                        mt = 0 if d == 0 else (2 if d == 4 else 1)
                        tasks.append(dict(p=p, g=2, qk=qk, pv=pv, mask=m3[:, mt, :],
                                          bias=[(0, 512, cj < FIRST_OWN)], rk=rk, rv=rv,
                                          first=(X == 0 and d == 4), last=(X == 1 and d == 0), pvs=(pvN_, pvD_, rpv_)))

            import os
            dbgm0 = os.environ.get("ATT_DBG", "")

            def emit_front(t):
                if t["first"]:
                    zs = nxt("qz", 2)
                    state["qz_cur"] = zs
                    g_, p_ = t["g"], t["p"]

                    def zc(h, zs=zs, g_=g_, p_=p_):
                        h.tensor_copy(out=qz[0:64, 0, zs, :], in_=qT[0:64, g_, p_, :])
                        return h.tensor_copy(out=qz[64:128, 1, zs, :], in_=qT[64:128, g_, p_, :])
                    S.op("dve" if ci == FIRST_OWN else "pool", zc, reads=[r_q[g_]], writes=[r_qz[zs]])
                zs = state["qz_cur"]
                si = nxt("sps", 2)
                pi = nxt("pb", 4)
                t["pi"] = pi
                sp = sps[si]
                nq_ = len(t["qk"])
                pe_groups([(sp[:, :], [(ident[:, :], t["mask"])], True, False, None)] +
                          [(sp[:, c0:c0 + 128], [(l, qz[:, X, zs, qc])], False, j_ == nq_ - 1, None)
                           for j_, (c0, l, X, qc) in enumerate(t["qk"])],
                          t["rk"] + [r_qz[zs], r_c], [r_sps[si]])

                def ex(h, t=t, sp=sp, pi=pi):
                    ins = None
                    for (a, b, hb) in t["bias"]:
                        if hb:
                            ins = h.activation(out=pbuf[:, pi, a:b], in_=sp[:, a:b], func=AF.Exp, bias=hbias[:, 0:1], scale=0.125)
                        else:
                            ins = h.activation(out=pbuf[:, pi, a:b], in_=sp[:, a:b], func=AF.Exp, scale=0.125)
                    return ins
                if "noexp" in dbgm0:
                    return
                S.op("act", ex, reads=[r_sps[si], r_c], writes=[r_pb[pi]])
                if "nopool" in dbgm0:
                    return

            def emit_back(t):
                pi = t["pi"]
                pe_groups([(o, [(l, pbuf[:, pi, c0:c0 + 128])], st0, sp0, tp) for (o, l, c0, st0, sp0, tp) in t["pv"]],
                          t["rv"] + [r_pb[pi], r_c], t["pvs"][2])
                if t["last"]:
                    g, p = t["g"], t["p"]
                    pn_, pd_, rp_ = t["pvs"]
                    if g == 0:
                        def ev(h, pn_=pn_, pd_=pd_):
                            h.tensor_copy(out=accN[:, :], in_=pn_)
                            return h.tensor_copy(out=accD[:, :], in_=pd_)
                    else:
                        def ev(h, pn_=pn_, pd_=pd_):
                            av = accN[:, :].rearrange("p (j c) -> p c j", c=4)
                            dv = accD[:, :].rearrange("p (j c) -> p c j", c=4)
                            h.tensor_tensor(out=av, in0=av, in1=pn_.rearrange("p (c j) -> p c j", c=4), op=ALU.add)
                            return h.tensor_tensor(out=dv, in0=dv, in1=pd_.rearrange("p (c j) -> p c j", c=4), op=ALU.add)
                    S.op("dve", ev, reads=rp_, writes=[r_acc])
                    if g == 2:
                        S.op("act", lambda h: h.activation(out=accD[:, :], in_=accD[:, :], func=AF.Ln), reads=[r_acc], writes=[r_acc])
                        S.op("act", lambda h: h.activation(out=accD[:, :], in_=accD[:, :], func=AF.Exp, scale=-1.0), reads=[r_acc], writes=[r_acc])
                        S.op("dve", lambda h, p=p: h.tensor_tensor(out=attnT[:, p, :], in0=accN[:, :], in1=accD[:, :], op=ALU.mult),
                             reads=[r_acc], writes=[r_at[p]])

            import os
            dbgm = os.environ.get("ATT_DBG", "")
            if "g01" in dbgm:
                tasks = [t for t in tasks if t["g"] < 2]
            if "g2" in dbgm:
                tasks = [t for t in tasks if t["g"] == 2]
            if "p0" in dbgm:
                tasks = [t for t in tasks if t["p"] == 0]
            pend_ = []
            for t in tasks:
                emit_front(t)
                pend_.append(t)
                if len(pend_) > 2:
                    emit_back(pend_.pop(0))
                yield
            while pend_:
                emit_back(pend_.pop(0))

        def sgu_pre(ci):
            uslab, r_us = load_slab(9)
            for fp in range(2):
                mi = nxt("mm", 2)
                groups = [(mm[mi][:, j * 512:(j + 1) * 512],
                           [(uslab[:, kc, (fp * 2 + j) * 128:(fp * 2 + j + 1) * 128], hT[:, kc, :]) for kc in range(8)],
                           True, True, None) for j in range(2)]
                pe_groups(groups, r_h + [r_us], [r_mm[mi]])
                S.op("act", lambda h, mi=mi, fp=fp: h.activation(out=uT[:, fp * 2:fp * 2 + 2, :],
                                                                 in_=mm[mi][:, :].rearrange("p (j c) -> p j c", j=2), func=AF.Gelu),
                     reads=[r_mm[mi]], writes=[r_u])
            vslab, r_vs = load_slab(10)
            for tp_ in range(2):
                mi = nxt("mm", 2)
                groups = [(mm[mi][:, j * 512:(j + 1) * 512],
                           [(hT[:, kc, (tp_ * 2 + j) * 128:(tp_ * 2 + j + 1) * 128], vslab[:, kc, :]) for kc in range(8)],
                           True, True, None) for j in range(2)]
                pe_groups(groups, r_h + [r_vs], [r_mm[mi]])
                S.op("act", lambda h, mi=mi, tp_=tp_: h.activation(out=vz[:, tp_ * 2:tp_ * 2 + 2, :],
                                                                   in_=mm[mi][:, :].rearrange("p (j c) -> p j c", j=2), func=AF.Gelu),
                     reads=[r_mm[mi]], writes=[r_vz])

        def sgu_stage(ci):
            for tp_ in range(2):
                def stats1(h, tp_=tp_):
                    h.bn_stats(out=bn6[:, 0, :], in_=vz[:, tp_ * 2, :])
                    return h.bn_stats(out=bn6[:, 1, :], in_=vz[:, tp_ * 2 + 1, :])

                def stats2(h):
                    h.bn_aggr(out=mv[:, 0, :], in_=bn6[:, 0, :])
                    return h.bn_aggr(out=mv[:, 1, :], in_=bn6[:, 1, :])
                S.op("dve", stats1, reads=[r_vz], writes=[r_ln])
                S.op("dve", stats2, reads=[r_ln], writes=[r_ln])
                if ci == FIRST_OWN:
                    S.op("act", lambda h: h.activation(out=lsd[:, 0:2], in_=mv[:, :, 1], func=AF.Sqrt, scale=1.0, bias=epsb[:, 0:1]),
                         reads=[r_ln, r_c], writes=[r_ln])
                    S.op("dve", lambda h: h.reciprocal(out=lrs[:, 0:2], in_=lsd[:, 0:2]), reads=[r_ln], writes=[r_ln])
                else:
                    S.op("dve", lambda h: h.tensor_scalar(out=lsd[:, 0:2], in0=mv[:, :, 1], scalar1=EPS, scalar2=None, op0=ALU.add),
                         reads=[r_ln], writes=[r_ln])
                    S.op("pool", lambda h: h.tensor_tensor(out=lrs[:, 0:2], in0=lsd[:, 0:2], in1=nhalf[:, 0:2], op=ALU.pow),
                         reads=[r_ln, r_c], writes=[r_ln])
                yield
                for j in range(2):
                    tb = tp_ * 2 + j
                    vi = nxt("vn", 2)
                    S.op("dve", lambda h, j=j, tb=tb: h.tensor_scalar(out=vt, in0=vz[:, tb, :], scalar1=mv[:, j, 0:1], scalar2=lrs[:, j:j + 1],
                                                                      op0=ALU.subtract, op1=ALU.mult),
                         reads=[r_vz, r_ln], writes=[r_vt])
                    S.op("dve", lambda h: h.tensor_tensor(out=vt, in0=vt, in1=lng[:, :], op=ALU.mult), reads=[r_vt, r_c], writes=[r_vt])

                    def nrm3(h, vi=vi):
                        v4 = vt.rearrange("p (g e c) -> p g e c", g=4, e=2)
                        b4 = lnb[:, :].rearrange("p (g e c) -> p g e c", g=4, e=2)
                        a4 = vnA[:, vi, :].rearrange("p (g e c) -> p g e c", g=4, e=2)
                        c4 = vnB[:, vi, :].rearrange("p (g e c) -> p g e c", g=4, e=2)
                        h.tensor_tensor(out=a4[:, :, 0, :], in0=v4[:, :, 0, :], in1=b4[:, :, 0, :], op=ALU.add)
                        return h.tensor_tensor(out=c4[:, :, 1, :], in0=v4[:, :, 1, :], in1=b4[:, :, 1, :], op=ALU.add)
                    S.op("dve", nrm3, reads=[r_vt, r_c], writes=[r_vn[vi]])
                    si = 0
                    yield
                    groups = [(mm[si][:, ft * 128:(ft + 1) * 128],
                               [(vnA[:, vi, ft * 128:(ft + 1) * 128], WcT[:, 2 * ft, :]),
                                (vnB[:, vi, ft * 128:(ft + 1) * 128], WcT[:, 2 * ft + 1, :])], True, True, None) for ft in range(4)]
                    pe_groups(groups, [r_vn[vi], r_c], [r_mm[si]])
                    S.op("dve", lambda h, si=si: h.tensor_tensor(out=sgt, in0=mm[si][:, 0:512], in1=bsp[:, :], op=ALU.add),
                         reads=[r_mm[si], r_c], writes=[r_sgt])
                    S.op("dve", lambda h, tb=tb: h.tensor_tensor(out=sguT[:, :, tb * 128:(tb + 1) * 128],
                                                                 in0=sgt.rearrange("p (f t) -> p f t", f=4),
                                                                 in1=uT[:, :, tb * 128:(tb + 1) * 128], op=ALU.mult),
                         reads=[r_sgt, r_u], writes=[r_sg])

        def merge_stage(ci):
            xbuf = xbuf2[:, ci % 2]
            r_x = r_xb[ci % 2]
            n2 = norm_gen(g2c, ci % 2, ci == FIRST_OWN)
            S.fence(r_q + [r_u], r_mg + r_gb)
            S.fence([r_vz, r_vt, r_sgt], [r_t1])
            for hs in range(2):
                gslab, r_gs = load_slab(11 + hs)
                for tq in range(2):
                    tp_ = hs * 2 + tq
                    mi = nxt("mm", 2)
                    groups = [(mm[mi][:, j * 512:(j + 1) * 512],
                               [(gslab[:, kc, (tq * 2 + j) * 128:(tq * 2 + j + 1) * 128], hT[:, kc, :]) for kc in range(8)],
                               True, True, None) for j in range(2)]
                    pe_groups(groups, r_h + [r_gs], [r_mm[mi]])
                    S.op("act", lambda h, mi=mi, tp_=tp_: h.activation(out=mergedT[:, tp_ * 2:tp_ * 2 + 2, :],
                                                                       in_=mm[mi][:, :].rearrange("p (j c) -> p j c", j=2), func=AF.Sigmoid),
                         reads=[r_mm[mi]], writes=[r_mg[tp_]])
            paslab, r_pa = load_slab(15)
            for tp_ in range(4):
                mi = nxt("mm", 2)
                groups = [(mm[mi][:, j * 512:(j + 1) * 512],
                           [(paslab[:, kc, (tp_ * 2 + j) * 128:(tp_ * 2 + j + 1) * 128], attnT[:, kc, :]) for kc in range(4)],
                           True, True, None) for j in range(2)]
                pe_groups(groups, r_at + [r_pa], [r_mm[mi]])
                S.op("dve", lambda h, mi=mi, tp_=tp_: h.tensor_tensor(out=mergedT[:, tp_ * 2:tp_ * 2 + 2, :],
                                                                      in0=mm[mi][:, :].rearrange("p (j c) -> p j c", j=2),
                                                                      in1=mergedT[:, tp_ * 2:tp_ * 2 + 2, :], op=ALU.mult),
                     reads=[r_mm[mi], r_mg[tp_]], writes=[r_mg[tp_]])
            for hs in range(2):
                gslab, r_gs = load_slab(13 + hs)
                for tq in range(2):
                    tp_ = hs * 2 + tq
                    mi = nxt("mm", 2)
                    groups = [(mm[mi][:, j * 512:(j + 1) * 512],
                               [(gslab[:, kc, (tq * 2 + j) * 128:(tq * 2 + j + 1) * 128], hT[:, kc, :]) for kc in range(8)],
                               True, True, None) for j in range(2)]
                    pe_groups(groups, r_h + [r_gs], [r_mm[mi]])
                    S.op("act", lambda h, mi=mi, tp_=tp_: h.activation(out=gbT[:, tp_ * 2:tp_ * 2 + 2, :],
                                                                       in_=mm[mi][:, :].rearrange("p (j c) -> p j c", j=2), func=AF.Sigmoid),
                         reads=[r_mm[mi]], writes=[r_gb[tp_]])
            pbslab, r_pbs = load_slab(16)
            for tp_ in range(4):
                mi = nxt("mm", 2)
                groups = [(mm[mi][:, j * 512:(j + 1) * 512],
                           [(pbslab[:, kc, (tp_ * 2 + j) * 128:(tp_ * 2 + j + 1) * 128], sguT[:, kc, :]) for kc in range(4)],
                           True, True, None) for j in range(2)]
                pe_groups(groups, [r_sg, r_pbs], [r_mm[mi]])

                S.op("dve", lambda h, mi=mi, tp_=tp_: h.tensor_tensor(
                    out=t1, in0=mm[mi][:, :], in1=gbT[:, tp_ * 2:tp_ * 2 + 2, :].rearrange("p a b -> p (a b)"), op=ALU.mult),
                    reads=[r_mm[mi], r_gb[tp_]], writes=[r_t1])
                S.op("dve", lambda h, tp_=tp_: h.tensor_tensor(
                    out=mergedT[:, tp_ * 2:tp_ * 2 + 2, :].rearrange("p a b -> p (a b)"), in0=t1,
                    in1=mergedT[:, tp_ * 2:tp_ * 2 + 2, :].rearrange("p a b -> p (a b)"), op=ALU.add),
                    reads=[r_t1, r_mg[tp_]], writes=[r_mg[tp_]])
            oslabs = [load_slab(17), load_slab(18)]
            for tp_ in range(2):
                for hf in range(2):
                    oslab, r_os = oslabs[hf]
                    mi = nxt("mm", 2)
                    groups = [(mm[mi][:, j * 512:(j + 1) * 512],
                               [(mergedT[:, kc, (tp_ * 2 + j) * 128:(tp_ * 2 + j + 1) * 128], oslab[:, kc, :]) for kc in range(8)],
                               True, True, None) for j in range(2)]
                    pe_groups(groups, r_mg + [r_os], [r_mm[mi]])
                    S.op("dve", lambda h, mi=mi, tp_=tp_, hf=hf: h.tensor_tensor(
                        out=xbuf[:, tp_ * 2:tp_ * 2 + 2, hf * 512:(hf + 1) * 512],
                        in0=xbuf[:, tp_ * 2:tp_ * 2 + 2, hf * 512:(hf + 1) * 512],
                        in1=mm[mi][:, :].rearrange("p (j c) -> p j c", j=2), op=ALU.add),
                        reads=[r_mm[mi]], writes=[r_x[tp_ * 2], r_x[tp_ * 2 + 1]])
                    if tp_ == 0 and hf == 1:
                        next(n2, None)
                        next(n2, None)
                    if tp_ == 1 and hf == 0:
                        next(n2, None)
            for _ in n2:
                pass

        def ffn_stage(ci, nxt_norm):
            xbuf = xbuf2[:, ci % 2]
            r_x = r_xb[ci % 2]
            S.fence(r_q + [r_u, r_sg] + r_at + r_mg + r_gb, r_ff)
            S.fence(r_stg, r_stg)
            for s in range(6):
                gs, r_gs = load_slab(19 + s)
                us, r_us = load_slab(25 + s)
                ntp = 2 if s < 5 else 1
                for tq in range(ntp):
                    fi = s * 2 + tq
                    mg_ = nxt("mm", 2)
                    groups = [(mm[mg_][:, j * 512:(j + 1) * 512],
                               [(gs[:, kc, (tq * 2 + j) * 128:(tq * 2 + j + 1) * 128], hT[:, kc, :]) for kc in range(8)],
                               True, True, None) for j in range(2)]
                    pe_groups(groups, r_h + [r_gs], [r_mm[mg_]])
                    si = nxt("stg", 2)
                    S.op("act", lambda h, mg_=mg_, si=si: h.activation(out=sgl[:, si, :], in_=mm[mg_][:, :], func=AF.Silu),
                         reads=[r_mm[mg_]], writes=[r_stg[si]])
                    mu_ = nxt("mm", 2)
                    groups = [(mm[mu_][:, j * 512:(j + 1) * 512],
                               [(us[:, kc, (tq * 2 + j) * 128:(tq * 2 + j + 1) * 128], hT[:, kc, :]) for kc in range(8)],
                               True, True, None) for j in range(2)]
                    pe_groups(groups, r_h + [r_us], [r_mm[mu_]])
                    S.op("dve", lambda h, mu_=mu_, si=si, fi=fi: h.tensor_tensor(
                        out=ffT[:, fi * 2:fi * 2 + 2, :].rearrange("p a b -> p (a b)"), in0=sgl[:, si, :], in1=mm[mu_][:, :], op=ALU.mult),
                        reads=[r_mm[mu_], r_stg[si]], writes=[r_ff[fi]])
            for sp_ in range(3):
                da, r_da = load_slab(31 + 2 * sp_)
                db, r_db = load_slab(32 + 2 * sp_)
                nkb = 4 if sp_ < 2 else 2
                for tb in range(4):
                    mi = nxt("mm", 2)
                    groups = []
                    for hf in range(2):
                        pairs = [(ffT[:, sp_ * 8 + kl, tb * 128:(tb + 1) * 128], da[:, kl, hf * 512:(hf + 1) * 512]) for kl in range(4)]
                        pairs += [(ffT[:, sp_ * 8 + 4 + kl, tb * 128:(tb + 1) * 128], db[:, kl, hf * 512:(hf + 1) * 512]) for kl in range(nkb)]
                        groups.append((mm[mi][:, hf * 512:(hf + 1) * 512], pairs, True, True, None))
                    rff = [r_ff[sp_ * 4 + q_] for q_ in range(2 + nkb // 2)]
                    pe_groups(groups, rff + [r_da, r_db], [r_mm[mi]])
                    S.op("dve", lambda h, mi=mi, tb=tb: h.tensor_tensor(out=xbuf[:, tb, :], in0=xbuf[:, tb, :], in1=mm[mi][:, :], op=ALU.add),
                         reads=[r_mm[mi]], writes=[r_x[tb]])
                    if nxt_norm is not None:
                        next(nxt_norm, None)
            if nxt_norm is not None:
                for _ in nxt_norm:
                    pass
            for tb in range(4):
                sl = nxt("xs", 2)
                S.op("act", lambda h, tb=tb, sl=sl: h.activation(out=xs[:, sl, :], in_=xbuf[:, tb, :], func=AF.Square,
                                                                   accum_out=ss[:, tb:tb + 1]),
                     reads=[r_x[tb]], writes=[r_xs[sl], r_small[tb]])
                emit_rstd(tb, ci == FIRST_OWN)
                S.op("dve", lambda h, tb=tb: h.scalar_tensor_tensor(out=xbuf[:, tb, :], in0=xbuf[:, tb, :], scalar=rstd[:, tb:tb + 1],
                                                                    in1=gfin[:, :], op0=ALU.mult, op1=ALU.mult),
                     reads=[r_x[tb], r_small[tb], r_c], writes=[r_x[tb]])
                r0 = (ci - FIRST_OWN) * T + tb * 128
                S.dma("pool", f"st{tb}", y[r0:r0 + 128, :], xbuf[:, tb, :], reads=[r_x[tb]])

        load_x(0)
        emit_cv(NSLAB)
        norm_to_hT(g1c, 0, True)
        for ci in range(nchunks):
            nn = None
            own = ci >= FIRST_OWN
            if ci + 1 < nchunks:
                if not own:
                    load_x(ci + 1)
                nn = norm_gen(g1c, (ci + 1) % 2, ci + 1 <= FIRST_OWN)
            if ci == FIRST_OWN:
                tap("hT", hT[:, :, :], r_h)
            if not own:
                qkv_stage(ci, 2, False)
                if ci == FIRST_OWN - 1:
                    qkv_stage(ci, 0, False)
                    qkv_stage(ci, 1, False)
                if nn is not None:
                    for _ in nn:
                        pass
                continue
            S.fence(r_ff + r_mg + r_gb, r_q + [r_u, r_sg] + r_at)
            S.fence([r_t1], [r_vz, r_vt, r_sgt])
            for g in range(3):
                qkv_stage(ci, g, True)
            if ci == FIRST_OWN:
                tap("qT", qT, r_q)
                for g in range(3):
                    tap(f"kr{g}", kr[g][:, :, :, :], r_k[g])
                    tap(f"vr{g}", vr[g][:, :, :], r_v[g])
            sgu_pre(ci)
            ga_ = attention(ci)
            gs_ = sgu_stage(ci)
            k_ = 0
            for _ in ga_:
                k_ += 1
                if k_ % 9 == 4:
                    next(gs_, None)
            for _ in gs_:
                pass
            if ci == FIRST_OWN:
                tap("attnT", attnT, r_at)
                tap("sguT", sguT, [r_sg])
                tap("uT", uT, [r_u])
            if ci + 1 < nchunks:
                load_x(ci + 1)
            merge_stage(ci)
            if ci == FIRST_OWN:
                tap("mergedT", mergedT, r_mg)
                tap("x1", xbuf2[:, ci % 2], r_xb[ci % 2])
            ffn_stage(ci, nn)

        fin = [(f"st{i}", S.cnt[f"st{i}"]) for i in range(4)] + [("tap", S.cnt["tap"])]
        S.wait_all("sp", fin)
        S.emit()
    return nc


def make_in_maps(x, positions, norm1_g, w_in, sgu_ln_g, sgu_ln_b, w_spatial, b_spatial, w_proj_attn, w_proj_sgu,
                 w_out, norm2_g, w_ffn_gate, w_ffn_up, w_ffn_down, final_g):
    f32 = np.float32
    x = np.asarray(x, f32)
    positions = np.asarray(positions, np.int32)
    shared = {
        "w_in": np.ascontiguousarray(np.asarray(w_in, f32)[0]),
        "wpa": np.ascontiguousarray(np.asarray(w_proj_attn, f32)[0]),
        "wps": np.ascontiguousarray(np.asarray(w_proj_sgu, f32)[0]),
        "wout": np.ascontiguousarray(np.asarray(w_out, f32)[0]),
        "wg": np.ascontiguousarray(np.asarray(w_ffn_gate, f32)[0]),
        "wu": np.ascontiguousarray(np.asarray(w_ffn_up, f32)[0]),
        "wd": np.ascontiguousarray(np.asarray(w_ffn_down, f32)[0]),
        "g1c": np.ascontiguousarray(np.asarray(norm1_g, f32)[0].reshape(8, 128).T),
        "g2c": np.ascontiguousarray(np.asarray(norm2_g, f32)[0].reshape(8, 128).T),
        "gfin": np.ascontiguousarray(np.broadcast_to(np.asarray(final_g, f32)[None, :], (128, D))),
        "lng": np.ascontiguousarray(np.broadcast_to(np.asarray(sgu_ln_g, f32)[0][None, :], (128, 512))),
        "lnb": np.ascontiguousarray(np.broadcast_to(np.asarray(sgu_ln_b, f32)[0][None, :], (128, 512))),
        "bsp": np.ascontiguousarray(np.repeat(np.asarray(b_spatial, f32)[0].reshape(4, 2, 128), 64, axis=1)
                                    .transpose(1, 0, 2).reshape(128, 512)),
        "wsp": np.ascontiguousarray(np.asarray(w_spatial, f32)[0].transpose(2, 0, 1).reshape(128, 1024)),
    }
    in_maps = []
    for core in range(NCORE):
        b, half = core // 2, core % 2
        s0 = half * OWN
        xin = np.zeros((OWN + HALO, D), f32)
        pin = np.zeros((OWN + HALO,), np.int32)
        xin[HALO:] = x[b, s0:s0 + OWN]
        pin[HALO:] = positions[b, s0:s0 + OWN]
        if half == 1:
            xin[:HALO] = x[b, s0 - HALO:s0]
            pin[:HALO] = positions[b, s0 - HALO:s0]
        m = dict(shared)
        m["xin"] = xin
        m["pos"] = np.ascontiguousarray(pin.reshape(48, 128).T)
        m["hbias"] = np.full((128, 1), 0.0 if half == 1 else NEG, f32)
        in_maps.append(m)
    return in_maps


_NC_CACHE = {}


def kernel(x, positions, norm1_g, w_in, sgu_ln_g, sgu_ln_b, w_spatial, b_spatial, w_proj_attn, w_proj_sgu,
           w_out, norm2_g, w_ffn_gate, w_ffn_up, w_ffn_down, final_g):
    in_maps = make_in_maps(x, positions, norm1_g, w_in, sgu_ln_g, sgu_ln_b, w_spatial, b_spatial, w_proj_attn,
                           w_proj_sgu, w_out, norm2_g, w_ffn_gate, w_ffn_up, w_ffn_down, final_g)
    if "nc" not in _NC_CACHE:
        _NC_CACHE["nc"] = build_program()
    nc = _NC_CACHE["nc"]
    res = run_bass_kernel_spmd(nc, in_maps, core_ids=list(range(NCORE)))
    out = np.zeros((4, SEQ, D), np.float32)
    for core in range(NCORE):
        b, half = core // 2, core % 2
        out[b, half * OWN:(half + 1) * OWN] = np.asarray(res.results[core]["y"], np.float32)
    return out
```

```python
import contextlib
import os
import numpy as np
import concourse.bass as bass
import concourse.mybir as mybir
from concourse.bass_utils import run_bass_kernel_spmd

F32 = mybir.dt.float32
BF16 = mybir.dt.bfloat16
I32 = mybir.dt.int32
AF = mybir.ActivationFunctionType
ALU = mybir.AluOpType

D = 1024
SEQ = 8192
NCORE = 8
OWN = 4096
HALO = 2048
T = 512
NCH = (OWN + HALO) // T
FIRST_OWN = HALO // T
DFF = 2816
NFT = DFF // 128
EPS = 1e-6
ROPE_THETA = 500000.0
NSLAB = 37
NEG = -30000.0


class Res:
    __slots__ = ("name", "last_w", "readers")

    def __init__(self, name):
        self.name = name
        self.last_w = None
        self.readers = {}


class Sched:
    ENG = ("pe", "act", "dve", "pool", "sp")

    def __init__(self, nc, stack):
        self.nc = nc
        self.stack = stack
        self.prog = {e: [] for e in self.ENG}
        self.sem = {}
        self.cnt = {}
        self.waited = {e: {} for e in self.ENG}
        for e in self.ENG:
            self.new_sem("c_" + e)

    def new_sem(self, name):
        s = self.stack.enter_context(self.nc.semaphore(name))
        self.sem[name] = s
        self.cnt[name] = 0
        return name

    def _deps(self, eng, reads, writes, deps):
        toks = {}

        def add(t):
            if t is None:
                return
            s, v = t
            if toks.get(s, 0) < v:
                toks[s] = v
        for t in deps:
            add(t)
        for r in reads:
            add(r.last_w)
        for r in writes:
            add(r.last_w)
            for s, v in r.readers.items():
                add((s, v))
        own = "c_" + eng
        for s, v in toks.items():
            if s == own and eng == "pe":
                continue
            if self.waited[eng].get(s, 0) < v:
                self.prog[eng].append(("wait", s, v))
                self.waited[eng][s] = v

    def op(self, eng, fn, reads=(), writes=(), deps=(), sem=None, inc=1):
        self._deps(eng, reads, writes, deps)
        s = sem or ("c_" + eng)
        self.cnt[s] += inc
        tok = (s, self.cnt[s])
        self.prog[eng].append(("inst", fn, s, inc))
        for r in reads:
            if r.readers.get(s, 0) < tok[1]:
                r.readers[s] = tok[1]
        for r in writes:
            r.last_w = tok
            r.readers = {}
        return tok

    def dma(self, eng, sem, out, in_, reads=(), writes=(), deps=()):
        def fn(h):
            return h.dma_start(out=out, in_=in_)
        return self.op(eng, fn, reads=reads, writes=writes, deps=deps, sem=sem, inc=16)

    def wait_all(self, eng, toks):
        for t in toks:
            if t is None:
                continue
            s, v = t
            if self.waited[eng].get(s, 0) < v:
                self.prog[eng].append(("wait", s, v))
                self.waited[eng][s] = v

    def fence(self, src, dst):
        for d in dst:
            for r in src:
                if r.last_w is not None:
                    s, v = r.last_w
                    if d.readers.get(s, 0) < v:
                        d.readers[s] = v
                for s, v in r.readers.items():
                    if d.readers.get(s, 0) < v:
                        d.readers[s] = v

    def emit(self):
        nc = self.nc
        with nc.Block() as block:
            def run(e):
                def body(h):
                    for it in self.prog[e]:
                        if it[0] == "wait":
                            h.wait_ge(self.sem[it[1]], it[2])
                        else:
                            ins = it[1](h)
                            ins.then_inc(self.sem[it[2]], it[3])
                return body
            block.tensor(run("pe"))
            block.scalar(run("act"))
            block.vector(run("dve"))
            block.gpsimd(run("pool"))
            block.sync(run("sp"))


def slab_src(wd, s):
    if s < 15:
        return wd["w_in"][:, s * 512:(s + 1) * 512].rearrange("(k p) c -> p k c", p=128), 8, 512
    if s == 15:
        return wd["wpa"].rearrange("(k p) c -> p k c", p=128), 4, 1024
    if s == 16:
        return wd["wps"].rearrange("(k p) c -> p k c", p=128), 4, 1024
    if s < 19:
        h = s - 17
        return wd["wout"][:, h * 512:(h + 1) * 512].rearrange("(k p) c -> p k c", p=128), 8, 512
    if s < 31:
        nm = "wg" if s < 25 else "wu"
        j = (s - 19) % 6
        c = 512 if j < 5 else 256
        return wd[nm][:, j * 512:j * 512 + c].rearrange("(k p) c -> p k c", p=128), 8, c
    j = s - 31
    k = 4 if j < 5 else 2
    return wd["wd"][j * 512:j * 512 + k * 128, :].rearrange("(k p) c -> p k c", p=128), k, 1024


def build_program(taps=None, nchunks=NCH, stages=5):
    taps = taps or {}
    nc = bass.Bass("TRN2", target_bir_lowering=False)

    def din(name, shape, dt=F32):
        return nc.dram_tensor(name, shape, dt, kind="ExternalInput").ap()

    xin = din("xin", [OWN + HALO, D])
    pos = din("pos", [128, 48], I32)
    hbias_d = din("hbias", [128, 1])
    wd = {"w_in": din("w_in", [D, 7680]), "wpa": din("wpa", [512, D]), "wps": din("wps", [512, D]),
          "wout": din("wout", [D, D]), "wg": din("wg", [D, DFF]), "wu": din("wu", [D, DFF]),
          "wd": din("wd", [DFF, D])}
    g1c_d = din("g1c", [128, 8])
    g2c_d = din("g2c", [128, 8])
    gfin_d = din("gfin", [128, D])
    lng_d = din("lng", [128, 512])
    lnb_d = din("lnb", [128, 512])
    bsp_d = din("bsp", [128, 512])
    wsp_d = din("wsp", [128, 1024])
    y = nc.dram_tensor("y", [OWN, D], F32, kind="ExternalOutput").ap()
    wbf = nc.dram_tensor("wbf", [NSLAB, 128, 4096], BF16, kind="Internal").ap()
    tap_out = {}
    for nm, (shape, dt) in taps.items():
        tap_out[nm] = nc.dram_tensor("tap_" + nm, shape, dt, kind="ExternalOutput").ap()

    with contextlib.ExitStack() as st:
        S = Sched(nc, st)

        def sb(name, shape, dt):
            return st.enter_context(nc.sbuf_tensor("s_" + name, shape, dt))

        def ps(name, shape, dt):
            return st.enter_context(nc.psum_tensor("p_" + name, shape, dt))

        xbuf2 = sb("xbuf", [128, 2, 4, D], F32)
        hT = sb("hT", [128, 8, T], BF16)
        wring = sb("wring", [128, 3, 4096], BF16)
        kr = [sb("kr0", [128, 4, 8, 128], BF16), sb("kr1", [128, 4, 8, 128], BF16),
              sb("kr2", [128, 4, 20, 128], BF16)]
        vr = [sb("vr0", [128, 8, 512], BF16), sb("vr1", [128, 8, 512], BF16), sb("vr2", [128, 20, 512], BF16)]
        NSL = [8, 8, 20]
        sA = sb("sA", [128, 12288], BF16)
        qT = sA[:, 0:6144].rearrange("p (g f t) -> p g f t", g=3, f=4)
        uT = sA[:, 6144:8192].rearrange("p (f t) -> p f t", f=4)
        attnT = sA[:, 8192:10240].rearrange("p (f t) -> p f t", f=4)
        sguT = sA[:, 10240:12288].rearrange("p (f t) -> p f t", f=4)
        mergedT = sA[:, 0:4096].rearrange("p (f t) -> p f t", f=8)
        gbT = sA[:, 4096:8192].rearrange("p (f t) -> p f t", f=8)
        ffT = sA[:, 0:NFT * T].rearrange("p (f t) -> p f t", f=NFT)
        stg = sb("stg", [128, 2, 1024], BF16)
        sgl = stg
        xs = stg
        pbuf = sb("pbuf", [128, 4, 512], BF16)
        qz = sb("qz", [128, 2, 2, 512], BF16)
        accN = sb("accN", [128, 512], F32)
        accD = sb("accD", [128, 512], F32)
        sgm = sb("sgm", [128, 2048], F32)
        vz = sgm[:, 0:1024].bitcast(BF16).rearrange("p (j c) -> p j c", j=4)
        vt = sgm[:, 1024:1536]
        sgt = sgm[:, 1536:2048]
        t1 = sgm[:, 0:1024]
        ropet = sgm[:, 0:512].rearrange("p (j c) -> p j c", j=4)
        vnA = sb("vnA", [128, 2, 512], BF16)
        vnB = sb("vnB", [128, 2, 512], BF16)
        mtb = sb("mtb", [128, 512], BF16)
        m3 = sb("m3", [128, 3, 512], BF16)
        cosT = sb("cosT", [128, 48, 8], F32)
        sinT = sb("sinT", [128, 48, 8], F32)
        gfin = sb("gfin", [128, D], F32)
        lng = sb("lng", [128, 512], F32)
        lnb = sb("lnb", [128, 512], F32)
        bsp = sb("bsp", [128, 512], F32)
        WcT = sb("WcT", [128, 8, 128], BF16)
        ident = sb("ident", [128, 128], BF16)
        ones64 = sb("ones64", [128, 64], BF16)
        g1c = sb("g1c", [128, 8], F32)
        g2c = sb("g2c", [128, 8], F32)
        hbias = sb("hbias", [128, 1], F32)
        epsb = sb("epsb", [128, 1], F32)
        small = sb("small", [128, 64], F32)
        ss = small[:, 0:4]
        sd = small[:, 4:8]
        rstd = small[:, 8:12]
        bn6 = small[:, 16:28].rearrange("p (j s) -> p j s", j=2)
        mv = small[:, 28:32].rearrange("p (j s) -> p j s", j=2)
        lsd = small[:, 32:34]
        lrs = small[:, 34:36]
        nhalf = small[:, 40:44]

        mm = [ps("mm0", [128, 1024], F32), ps("mm1", [128, 1024], F32)]
        sps = [ps("sps0", [128, 512], F32), ps("sps1", [128, 512], F32)]
        pvN = ps("pvN", [128, 512], F32)
        pvD = ps("pvD", [128, 512], F32)

        R = lambda n: Res(n)
        r_xb = [[R(f"x{b}_{i}") for i in range(4)] for b in range(2)]
        r_x = r_xb[1]
        r_h = [R(f"h{i}") for i in range(4)]
        r_w = [R(f"w{i}") for i in range(3)]
        r_k = [[R(f"k{g}_{s}") for s in range(NSL[g])] for g in range(3)]
        r_v = [[R(f"v{g}_{s}") for s in range(NSL[g])] for g in range(3)]
        r_q = [R(f"q{g}") for g in range(3)]
        r_u = R("uT")
        r_at = [R(f"at{p}") for p in range(4)]
        r_sg = R("sguT")
        r_mg = [R(f"mg{i}") for i in range(4)]
        r_gb = [R(f"gb{i}") for i in range(4)]
        r_ff = [R(f"ff{i}") for i in range(NFT // 2)]
        r_stg = [R("stg0"), R("stg1")]
        r_xs = r_stg
        r_pb = [R(f"pb{i}") for i in range(4)]
        r_qz = [R("qz0"), R("qz1")]
        r_acc = R("acc")
        r_vz = R("vz")
        r_rt = r_vz
        r_vt = R("vt")
        r_sgt = R("sgt")
        r_t1 = R("t1")
        r_vn = [R("vn0"), R("vn1")]
        r_mm = [R("mm0"), R("mm1")]
        r_sps = [R("sps0"), R("sps1")]
        r_pv = R("pv")
        r_pvd = R("pvd")
        r_c = R("consts")
        r_small = [R(f"sm{i}") for i in range(4)]
        r_ln = R("lnsmall")
        r_cv = [R(f"cv{s}") for s in range(NSLAB)]
        for i in range(4):
            S.new_sem(f"xs0{i}")
            S.new_sem(f"xs1{i}")
            S.new_sem(f"st{i}")
        for i in range(3):
            S.new_sem(f"w{i}")
        for s in range(NSLAB):
            S.new_sem(f"cv{s}")
            S.new_sem(f"wb{s}")
        S.new_sem("ld")
        S.new_sem("tap")

        state = {"mm": 0, "sps": 0, "w": 0, "pb": 0, "stg": 0, "xs": 0, "vn": 0, "qz": 0, "tr": 0}
        r_xs = r_stg
        trt = [t_[:, :].bitcast(BF16).rearrange("p (k t) -> p k t", k=8) for t_ in (sps[0], sps[1], pvN, pvD)]
        r_tr = [r_sps[0], r_sps[1], r_pv, r_pvd]

        def nxt(key, n):
            i = state[key]
            state[key] = (i + 1) % n
            return i

        first_used = set()

        def load_slab(s):
            src, k, c = slab_src(wd, s)
            slot = nxt("w", 3)
            flat = wring[:, slot, 0:k * c]
            view = flat.rearrange("p (k c) -> p k c", k=k)
            if s not in first_used:
                first_used.add(s)
                S.dma("pool", f"cv{s}", view, src, writes=[r_w[slot]])
                S.dma("sp", f"wb{s}", wbf[s][:, 0:k * c], flat, reads=[r_w[slot]], writes=[r_cv[s]])
            else:
                S.dma("sp", f"w{slot}", flat, wbf[s][:, 0:k * c], reads=[r_cv[s]], writes=[r_w[slot]])
            return view, r_w[slot]

        def load_x(ci):
            b_ = ci % 2
            for tb in range(4):
                r0 = ci * T + tb * 128
                S.dma("sp", f"xs{b_}{tb}", xbuf2[:, b_, tb, :], xin[r0:r0 + 128, :], writes=[r_xb[b_][tb]])

        cv_order = [5, 8, 3, 6, 4, 7, 0, 1, 2, 9, 10, 11, 12, 15, 13, 14, 16, 17, 18]
        for j in range(6):
            cv_order += [19 + j, 25 + j]
        cv_order += list(range(31, 37))
        assert sorted(cv_order) == list(range(NSLAB))
        cv_state = {"i": 0, "toks": []}

        def emit_cv(n):
            for _ in range(n):
                i = cv_state["i"]
                if i >= NSLAB:
                    return
                sl_ = cv_order[i]
                if i >= 3:
                    S.wait_all("pool", [cv_state["toks"][i - 3]])
                src, k, c = slab_src(wd, sl_)
                dst = wbf[sl_][:, 0:k * c].rearrange("p (k c) -> p k c", k=k)
                cv_state["toks"].append(S.dma("pool", f"cv{sl_}", dst, src, writes=[r_cv[sl_]]))
                cv_state["i"] = i + 1

        S.dma("sp", "ld", g1c[:, :], g1c_d, writes=[r_c])
        S.dma("sp", "ld", g2c[:, :], g2c_d, writes=[r_c])
        S.dma("sp", "ld", hbias[:, :], hbias_d, writes=[r_c])
        S.dma("sp", "ld", gfin[:, :], gfin_d, writes=[r_c])
        S.dma("sp", "ld", lng[:, :], lng_d, writes=[r_c])
        S.dma("sp", "ld", lnb[:, :], lnb_d, writes=[r_c])
        S.dma("sp", "ld", bsp[:, :], bsp_d, writes=[r_c])
        xf = xbuf2[:, 1, :, :].rearrange("p a b -> p (a b)")
        tmpA = xf[:, 0:1024]
        tmpI = xf[:, 1024:1536]
        S.dma("sp", "ld", tmpA, wsp_d, writes=r_x)
        S.dma("sp", "ld", tmpI[:, 0:48].bitcast(I32), pos, writes=r_x)
        ld_tok = ("ld", S.cnt["ld"])

        def setup_dve(h):
            onesf = xf[:, 1536:1664]
            trif = xf[:, 1664:1792]
            bandf = xf[:, 1792:1920]
            idf = xf[:, 1920:2048]
            modf = xf[:, 2048:2176]
            modi = xf[:, 2176:2304].bitcast(I32)
            h.memset(onesf, 1.0)
            h.memset(epsb[:, :], EPS)
            h.memset(nhalf, -0.5)
            h.memset(ones64[:, :], 1.0)
            h.memset(vnA[:, :, :], 0.0)
            h.memset(vnB[:, :, :], 0.0)
            h.memset(qz[:, :, :, :], 0.0)
            return h.memset(idf, 0.0)
        S.op("dve", setup_dve, writes=r_x + [r_c], deps=[ld_tok])

        def setup_pool(h):
            onesf = xf[:, 1536:1664]
            trif = xf[:, 1664:1792]
            bandf = xf[:, 1792:1920]
            idf = xf[:, 1920:2048]
            modi = xf[:, 2176:2304].bitcast(I32)
            h.affine_select(out=trif, in_=onesf, pattern=[[1, 128]], compare_op=ALU.is_ge, fill=0.0,
                            base=0, channel_multiplier=-1)
            h.affine_select(out=bandf, in_=onesf, pattern=[[-1, 128]], compare_op=ALU.is_ge, fill=0.0,
                            base=0, channel_multiplier=1)
            h.affine_select(out=idf, in_=onesf, pattern=[[-1, 128]], compare_op=ALU.is_equal, fill=0.0,
                            base=0, channel_multiplier=1)
            return h.iota(modi, pattern=[[1, 128]], base=128, channel_multiplier=-1)
        S.op("pool", setup_pool, reads=r_x, writes=r_x)

        INV2PI = float(1.0 / (2.0 * np.pi))
        C1 = 6.28125
        C2 = float(2.0 * np.pi - 6.28125)
        PI = float(np.pi)
        inv_freq = (ROPE_THETA ** (-np.arange(0, 16, 2, dtype=np.float32) / np.float32(16))).astype(np.float32)

        onesf = xf[:, 1536:1664]
        trif = xf[:, 1664:1792]
        bandf = xf[:, 1792:1920]
        idf = xf[:, 1920:2048]
        modf = xf[:, 2048:2176]
        modi = xf[:, 2176:2304].bitcast(I32)
        posf = xf[:, 2304:2352]
        ang = xf[:, 2432:2816].rearrange("p (b f) -> p b f", f=8)
        angf = xf[:, 2432:2816]
        ki = xf[:, 2816:3200].bitcast(I32)
        kf = xf[:, 3200:3584]
        fx = xf[:, 3584:3968]
        r_set = Res("setup")

        def SD(fn, eng="dve"):
            S.op(eng, fn, reads=r_x + [r_set], writes=r_x + [r_set, r_c])

        SD(lambda h: h.tensor_copy(out=ident[:, :], in_=idf))
        SD(lambda h: h.tensor_single_scalar(out=modi, in_=modi, scalar=3, op=ALU.bitwise_and))
        SD(lambda h: h.tensor_copy(out=modf, in_=modi))
        SD(lambda h: h.tensor_scalar(out=modf, in0=modf, scalar1=0.0, scalar2=None, op0=ALU.is_equal))

        def masks1(h):
            ins = None
            for j in range(2):
                h.tensor_copy(out=mtb[:, j * 128:(j + 1) * 128], in_=trif)
                h.tensor_copy(out=mtb[:, 256 + j * 128:256 + (j + 1) * 128], in_=bandf)
            for j in range(4):
                h.tensor_tensor(out=m3[:, 0, j * 128:(j + 1) * 128], in0=modf, in1=trif, op=ALU.mult)
                h.tensor_copy(out=m3[:, 1, j * 128:(j + 1) * 128], in_=modf)
                ins = h.tensor_tensor(out=m3[:, 2, j * 128:(j + 1) * 128], in0=modf, in1=bandf, op=ALU.mult)
            return ins
        SD(masks1)
        SD(lambda h: h.tensor_scalar(out=mtb[:, :], in0=mtb[:, :], scalar1=-1.0, scalar2=-NEG, op0=ALU.add, op1=ALU.mult))
        SD(lambda h: h.tensor_scalar(out=m3[:, :, :], in0=m3[:, :, :], scalar1=-1.0, scalar2=-NEG, op0=ALU.add, op1=ALU.mult))
        SD(lambda h: h.tensor_tensor(out=WcT[:, :, :], in0=tmpA.rearrange("p (g t) -> p g t", g=8),
                                     in1=trif.unsqueeze(1).to_broadcast([128, 8, 128]), op=ALU.mult))
        SD(lambda h: h.tensor_copy(out=posf, in_=tmpI[:, 0:48].bitcast(I32)))

        def angs(h):
            ins = None
            for f in range(8):
                ins = h.tensor_scalar(out=ang[:, :, f], in0=posf, scalar1=float(inv_freq[f]), scalar2=None, op0=ALU.mult)
            return ins
        SD(angs)
        SD(lambda h: h.tensor_scalar(out=ki, in0=angf, scalar1=INV2PI, scalar2=None, op0=ALU.mult))
        SD(lambda h: h.tensor_copy(out=kf, in_=ki))
        SD(lambda h: h.scalar_tensor_tensor(out=angf, in0=kf, scalar=-C1, in1=angf, op0=ALU.mult, op1=ALU.add))
        SD(lambda h: h.scalar_tensor_tensor(out=angf, in0=kf, scalar=-C2, in1=angf, op0=ALU.mult, op1=ALU.add))
        SD(lambda h: h.tensor_scalar(out=fx, in0=angf, scalar1=PI, scalar2=-2.0 * PI, op0=ALU.is_gt, op1=ALU.mult))
        SD(lambda h: h.tensor_tensor(out=angf, in0=angf, in1=fx, op=ALU.add))
        SD(lambda h: h.tensor_scalar(out=fx, in0=angf, scalar1=-PI, scalar2=2.0 * PI, op0=ALU.is_lt, op1=ALU.mult))
        SD(lambda h: h.tensor_tensor(out=angf, in0=angf, in1=fx, op=ALU.add))
        SD(lambda h: h.activation(out=sinT[:, :, :].rearrange("p b f -> p (b f)"), in_=angf, func=AF.Sin), eng="act")
        SD(lambda h: h.tensor_scalar(out=angf, in0=angf, scalar1=PI / 2.0, scalar2=None, op0=ALU.add))
        SD(lambda h: h.tensor_scalar(out=fx, in0=angf, scalar1=PI, scalar2=-2.0 * PI, op0=ALU.is_gt, op1=ALU.mult))
        SD(lambda h: h.tensor_tensor(out=angf, in0=angf, in1=fx, op=ALU.add))
        SD(lambda h: h.activation(out=cosT[:, :, :].rearrange("p b f -> p (b f)"), in_=angf, func=AF.Sin), eng="act")

        def pe_groups(groups, reads, writes):
            def fn(h):
                ins = None
                for out, pairs, st0, sp0, tp in groups:
                    n = len(pairs)
                    for i, (l, r) in enumerate(pairs):
                        kw = {}
                        if tp is not None:
                            kw["tile_position"] = tp
                        ins = h.matmul(out, lhsT=l, rhs=r, start=(st0 and i == 0), stop=(sp0 and i == n - 1), **kw)
                return ins
            return S.op("pe", fn, reads=reads, writes=writes)

        def tap(name, ap, reads):
            if name in tap_out:
                S.dma("sp", "tap", tap_out[name], ap, reads=reads)

        def emit_rstd(tb, early):
            if early:
                S.op("act", lambda h, tb=tb: h.activation(out=sd[:, tb:tb + 1], in_=ss[:, tb:tb + 1], func=AF.Sqrt,
                                                          scale=1.0 / D, bias=epsb[:, 0:1]),
                     reads=[r_small[tb], r_c], writes=[r_small[tb]])
                S.op("dve", lambda h, tb=tb: h.reciprocal(out=rstd[:, tb:tb + 1], in_=sd[:, tb:tb + 1]),
                     reads=[r_small[tb]], writes=[r_small[tb]])
            else:
                S.op("dve", lambda h, tb=tb: h.tensor_scalar(out=sd[:, tb:tb + 1], in0=ss[:, tb:tb + 1], scalar1=1.0 / D, scalar2=EPS,
                                                             op0=ALU.mult, op1=ALU.add),
                     reads=[r_small[tb]], writes=[r_small[tb]])
                S.op("pool", lambda h, tb=tb: h.tensor_tensor(out=rstd[:, tb:tb + 1], in0=sd[:, tb:tb + 1], in1=nhalf[:, 0:1], op=ALU.pow),
                     reads=[r_small[tb], r_c], writes=[r_small[tb]])

        def norm_gen(gcol, b_, early=False):
            xb_ = xbuf2[:, b_]
            rx_ = r_xb[b_]
            sls = {}

            def A(tb):
                sl = nxt("xs", 2)
                sls[tb] = sl
                S.op("act", lambda h, tb=tb, sl=sl: h.activation(out=xs[:, sl, :], in_=xb_[:, tb, :], func=AF.Square,
                                                                   accum_out=ss[:, tb:tb + 1]),
                     reads=[rx_[tb]], writes=[r_xs[sl], r_small[tb]])
                emit_rstd(tb, early)
                S.op("act", lambda h, tb=tb, sl=sl: h.activation(out=xs[:, sl, :], in_=xb_[:, tb, :], func=AF.Copy,
                                                                   scale=rstd[:, tb:tb + 1]),
                     reads=[rx_[tb], r_small[tb]], writes=[r_xs[sl]])

            def Tr(tb):
                sl = sls[tb]
                ti = nxt("tr", 4)
                trp = trt[ti]

                def tr(h, sl=sl, trp=trp):
                    ins = None
                    for kc in range(8):
                        ins = h.transpose(out=trp[:, kc, :], in_=xs[:, sl, kc * 128:(kc + 1) * 128], identity=ident[:, :])
                    return ins
                S.op("pe", tr, reads=[r_xs[sl], r_c], writes=[r_tr[ti]])
                S.op("dve", lambda h, tb=tb, trp=trp: h.tensor_tensor(
                    out=hT[:, :, tb * 128:(tb + 1) * 128], in0=trp,
                    in1=gcol[:, :].unsqueeze(2).to_broadcast([128, 8, 128]), op=ALU.mult),
                    reads=[r_tr[ti], r_c], writes=[r_h[tb]])
            A(0); yield
            A(1); yield
            Tr(0); yield
            A(2); yield
            Tr(1); yield
            A(3); yield
            Tr(2); yield
            Tr(3)

        def norm_to_hT(gcol, b_, early=False):
            for _ in norm_gen(gcol, b_, early):
                pass

        def tok_sel(g, blk):
            if g == 0:
                return lambda kc: hT[:, kc, blk * 128:(blk + 1) * 128]
            return lambda kc: hT[:, kc, :].rearrange("p (i c) -> p c i", c=4)[:, blk, :]

        def qkv_stage(ci, g, with_q):
            slot0 = 4 * (ci % (NSL[g] // 4))
            kslab, r_ks = load_slab(3 + g)
            if with_q:
                qslab, r_qs = load_slab(g)
            nf = 8 if with_q else 4
            pend = []

            def emit_tr(tb, si):
                mj = nxt("tr", 4)
                trp = trt[mj]

                def tr(h, si=si, trp=trp, nf=nf):
                    ins = None
                    for f in range(nf):
                        ins = h.transpose(out=trp[:, f, :], in_=stg[:, si, f * 128:(f + 1) * 128], identity=ident[:, :])
                    return ins
                S.op("pe", tr, reads=[r_stg[si], r_c], writes=[r_tr[mj]])
                kf0 = 4 if with_q else 0
                if g == 0:
                    if with_q:
                        S.op("dve", lambda h, trp=trp, tb=tb: h.tensor_copy(out=qT[:, 0, :, tb * 128:(tb + 1) * 128], in_=trp[:, 0:4, :]),
                             reads=[r_tr[mj]], writes=[r_q[0]])
                    S.op("dve", lambda h, trp=trp, tb=tb, kf0=kf0: h.tensor_copy(out=kr[0][:, :, slot0 + tb, :], in_=trp[:, kf0:kf0 + 4, :]),
                         reads=[r_tr[mj]], writes=[r_k[0][slot0 + tb]])
                else:
                    if with_q:
                        S.op("dve", lambda h, trp=trp, tb=tb: h.tensor_copy(
                            out=qT[:, g, :, :].rearrange("p f (c j) -> p f c j", c=4)[:, :, :, tb * 32:(tb + 1) * 32],
                            in_=trp[:, 0:4, :].rearrange("p f (i c) -> p f c i", c=4)),
                            reads=[r_tr[mj]], writes=[r_q[g]])
                    S.op("dve", lambda h, trp=trp, tb=tb, kf0=kf0: h.tensor_copy(
                        out=kr[g][:, :, slot0:slot0 + 4, tb * 32:(tb + 1) * 32],
                        in_=trp[:, kf0:kf0 + 4, :].rearrange("p f (i c) -> p f c i", c=4)),
                        reads=[r_tr[mj]], writes=[r_k[g][slot0 + c] for c in range(4)])

            for tb in range(4):
                mi = nxt("mm", 2)
                groups = []
                rd = [r_h[tb], r_ks]
                if with_q:
                    groups.append((mm[mi][:, 0:512], [(hT[:, kc, tb * 128:(tb + 1) * 128], qslab[:, kc, :]) for kc in range(8)],
                                   True, True, None))
                    rd.append(r_qs)
                    kcol = 512
                else:
                    kcol = 0
                groups.append((mm[mi][:, kcol:kcol + 512], [(hT[:, kc, tb * 128:(tb + 1) * 128], kslab[:, kc, :]) for kc in range(8)],
                               True, True, None))
                pe_groups(groups, rd, [r_mm[mi]])
                W = kcol + 512
                si = nxt("stg", 2)
                S.op("act", lambda h, mi=mi, si=si, W=W: h.copy(out=stg[:, si, 0:W], in_=mm[mi][:, 0:W]),
                     reads=[r_mm[mi]], writes=[r_stg[si]])
                nh = W // 64
                blk = ci * 4 + tb

                def rope_views(mi, si, W, nh, blk):
                    mv_ = mm[mi][:, 0:W].rearrange("p (h d) -> p h d", d=64)
                    sv = stg[:, si, 0:W].rearrange("p (h d) -> p h d", d=64)
                    cb = cosT[:, blk, :].unsqueeze(1).to_broadcast([128, nh, 8])
                    sn = sinT[:, blk, :].unsqueeze(1).to_broadcast([128, nh, 8])
                    tt = [ropet[:, j, 0:nh * 8].rearrange("p (h d) -> p h d", d=8) for j in range(4)]
                    return mv_, sv, cb, sn, tt

                def rope1(h, a=(mi, si, W, nh, blk)):
                    mv_, sv, cb, sn, tt = rope_views(*a)
                    x1 = mv_[:, :, 0:8]
                    x2 = mv_[:, :, 8:16]
                    h.tensor_tensor(out=tt[0], in0=x1, in1=cb, op=ALU.mult)
                    h.tensor_tensor(out=tt[1], in0=x2, in1=sn, op=ALU.mult)
                    h.tensor_tensor(out=tt[2], in0=x2, in1=cb, op=ALU.mult)
                    return h.tensor_tensor(out=tt[3], in0=x1, in1=sn, op=ALU.mult)

                def rope2(h, a=(mi, si, W, nh, blk)):
                    mv_, sv, cb, sn, tt = rope_views(*a)
                    h.tensor_tensor(out=sv[:, :, 0:8], in0=tt[0], in1=tt[1], op=ALU.subtract)
                    return h.tensor_tensor(out=sv[:, :, 8:16], in0=tt[2], in1=tt[3], op=ALU.add)
                import os
                if "norope1" not in os.environ.get("ATT_DBG", ""):
                    S.op("dve", rope1, reads=[r_mm[mi], r_c, r_stg[si]], writes=[r_rt])
                if "norope2" not in os.environ.get("ATT_DBG", ""):
                    S.op("dve", rope2, reads=[r_rt], writes=[r_stg[si]])
                pend.append((tb, si))
                if len(pend) > 1:
                    emit_tr(*pend.pop(0))
            while pend:
                emit_tr(*pend.pop(0))
            vslab, r_vs = load_slab(6 + g)
            for bp in range(2):
                mi = nxt("mm", 2)
                groups = []
                for j in range(2):
                    sel = tok_sel(g, bp * 2 + j)
                    groups.append((mm[mi][:, j * 512:(j + 1) * 512], [(sel(kc), vslab[:, kc, :]) for kc in range(8)], True, True, None))
                pe_groups(groups, r_h + [r_vs], [r_mm[mi]])
                s0 = slot0 + bp * 2
                S.op("act", lambda h, mi=mi, s0=s0: h.copy(out=vr[g][:, s0:s0 + 2, :], in_=mm[mi][:, :].rearrange("p (j c) -> p j c", j=2)),
                     reads=[r_mm[mi]], writes=[r_v[g][s0], r_v[g][s0 + 1]])

        def attention(ci):
            tasks = []
            pvsets = [(pvN[:, :], pvD[:, :], [r_pv, r_pvd]), (mm[1][:, 0:512], mm[1][:, 512:1024], [r_mm[1]])]
            for p in range(4):
                hA, hB = 2 * p, 2 * p + 1
                for g in range(2):
                    pvN_, pvD_, rpv_ = pvsets[(p * 3 + g) % 2]
                    for qb in range(4):
                        n = 4 * ci + qb
                        own_s = n % 8
                        prv = n - 1 if g == 0 else n - 4
                        prv_s = prv % 8
                        halo_prev = prv < 4 * FIRST_OWN
                        qk = []
                        for j, (X, ks) in enumerate([(0, own_s), (1, own_s), (0, prv_s), (1, prv_s)]):
                            qk.append((j * 128, kr[g][:, p, ks, :], X, slice(qb * 128, (qb + 1) * 128)))
                        pv = []
                        qc = slice(qb * 128, (qb + 1) * 128)
                        pv.append((pvN_[0:64, qc], vr[g][:, own_s, hA * 64:(hA + 1) * 64], 0, True, False, (0, 0)))
                        pv.append((pvN_[0:64, qc], vr[g][:, prv_s, hA * 64:(hA + 1) * 64], 256, False, True, (0, 0)))
                        pv.append((pvN_[64:128, qc], vr[g][:, own_s, hB * 64:(hB + 1) * 64], 128, True, False, (0, 64)))
                        pv.append((pvN_[64:128, qc], vr[g][:, prv_s, hB * 64:(hB + 1) * 64], 384, False, True, (0, 64)))
                        pv.append((pvD_[0:64, qc], ones64[:, :], 0, True, False, (0, 0)))
                        pv.append((pvD_[0:64, qc], ones64[:, :], 256, False, True, (0, 0)))
                        pv.append((pvD_[64:128, qc], ones64[:, :], 128, True, False, (0, 64)))
                        pv.append((pvD_[64:128, qc], ones64[:, :], 384, False, True, (0, 64)))
                        tasks.append(dict(p=p, g=g, qk=qk, pv=pv, mask=mtb[:, :],
                                          bias=[(0, 256, False), (256, 512, True)] if halo_prev else [(0, 512, False)],
                                          rk=[r_k[g][own_s], r_k[g][prv_s]], rv=[r_v[g][own_s], r_v[g][prv_s]],
                                          first=(qb == 0), last=(qb == 3), pvs=(pvN_, pvD_, rpv_)))
                g = 2
                pvN_, pvD_, rpv_ = pvsets[(p * 3 + g) % 2]
                for X, (hh, tp, prow) in enumerate([(hA, (0, 0), slice(0, 64)), (hB, (0, 64), slice(64, 128))]):
                    for d in (4, 3, 2, 1, 0):
                        cj = ci - d
                        qk, pv, rk, rv = [], [], [], []
                        for m in range(4):
                            sl = (4 * cj + m) % 20
                            qk.append((m * 128, kr[2][:, p, sl, :], X, slice(m * 128, (m + 1) * 128)))
                            qc = slice(m * 128, (m + 1) * 128)
                            pv.append((pvN_[prow, qc], vr[2][:, sl, hh * 64:(hh + 1) * 64], m * 128, d == 4 and m == 0, d == 0 and m == 3, tp))
                            pv.append((pvD_[prow, qc], ones64[:, :], m * 128, d == 4 and m == 0, d == 0 and m == 3, tp))
                            rk.append(r_k[2][sl])
                            rv.append(r_v[2][sl])
                        mt = 0 if d == 0 else (2 if d == 4 else 1)
                        tasks.append(dict(p=p, g=2, qk=qk, pv=pv, mask=m3[:, mt, :],
                                          bias=[(0, 512, cj < FIRST_OWN)], rk=rk, rv=rv,
                                          first=(X == 0 and d == 4), last=(X == 1 and d == 0), pvs=(pvN_, pvD_, rpv_)))

            import os
            dbgm0 = os.environ.get("ATT_DBG", "")

            def emit_front(t):
                if t["first"]:
                    zs = nxt("qz", 2)
                    state["qz_cur"] = zs
                    g_, p_ = t["g"], t["p"]

                    def zc(h, zs=zs, g_=g_, p_=p_):
                        h.tensor_copy(out=qz[0:64, 0, zs, :], in_=qT[0:64, g_, p_, :])
                        return h.tensor_copy(out=qz[64:128, 1, zs, :], in_=qT[64:128, g_, p_, :])
                    S.op("dve" if ci == FIRST_OWN else "pool", zc, reads=[r_q[g_]], writes=[r_qz[zs]])
                zs = state["qz_cur"]
                si = nxt("sps", 2)
                pi = nxt("pb", 4)
                t["pi"] = pi
                sp = sps[si]
                nq_ = len(t["qk"])
                pe_groups([(sp[:, :], [(ident[:, :], t["mask"])], True, False, None)] +
                          [(sp[:, c0:c0 + 128], [(l, qz[:, X, zs, qc])], False, j_ == nq_ - 1, None)
                           for j_, (c0, l, X, qc) in enumerate(t["qk"])],
                          t["rk"] + [r_qz[zs], r_c], [r_sps[si]])

                def ex(h, t=t, sp=sp, pi=pi):
                    ins = None
                    for (a, b, hb) in t["bias"]:
                        if hb:
                            ins = h.activation(out=pbuf[:, pi, a:b], in_=sp[:, a:b], func=AF.Exp, bias=hbias[:, 0:1], scale=0.125)
                        else:
                            ins = h.activation(out=pbuf[:, pi, a:b], in_=sp[:, a:b], func=AF.Exp, scale=0.125)
                    return ins
                if "noexp" in dbgm0:
                    return
                S.op("act", ex, reads=[r_sps[si], r_c], writes=[r_pb[pi]])
                if "nopool" in dbgm0:
                    return

            def emit_back(t):
                pi = t["pi"]
                pe_groups([(o, [(l, pbuf[:, pi, c0:c0 + 128])], st0, sp0, tp) for (o, l, c0, st0, sp0, tp) in t["pv"]],
                          t["rv"] + [r_pb[pi], r_c], t["pvs"][2])
                if t["last"]:
                    g, p = t["g"], t["p"]
                    pn_, pd_, rp_ = t["pvs"]
                    if g == 0:
                        def ev(h, pn_=pn_, pd_=pd_):
                            h.tensor_copy(out=accN[:, :], in_=pn_)
                            return h.tensor_copy(out=accD[:, :], in_=pd_)
                    else:
                        def ev(h, pn_=pn_, pd_=pd_):
                            av = accN[:, :].rearrange("p (j c) -> p c j", c=4)
                            dv = accD[:, :].rearrange("p (j c) -> p c j", c=4)
                            h.tensor_tensor(out=av, in0=av, in1=pn_.rearrange("p (c j) -> p c j", c=4), op=ALU.add)
                            return h.tensor_tensor(out=dv, in0=dv, in1=pd_.rearrange("p (c j) -> p c j", c=4), op=ALU.add)
                    S.op("dve", ev, reads=rp_, writes=[r_acc])
                    if g == 2:
                        S.op("act", lambda h: h.activation(out=accD[:, :], in_=accD[:, :], func=AF.Ln), reads=[r_acc], writes=[r_acc])
                        S.op("act", lambda h: h.activation(out=accD[:, :], in_=accD[:, :], func=AF.Exp, scale=-1.0), reads=[r_acc], writes=[r_acc])
                        S.op("dve", lambda h, p=p: h.tensor_tensor(out=attnT[:, p, :], in0=accN[:, :], in1=accD[:, :], op=ALU.mult),
                             reads=[r_acc], writes=[r_at[p]])

            import os
            dbgm = os.environ.get("ATT_DBG", "")
            if "g01" in dbgm:
                tasks = [t for t in tasks if t["g"] < 2]
            if "g2" in dbgm:
                tasks = [t for t in tasks if t["g"] == 2]
            if "p0" in dbgm:
                tasks = [t for t in tasks if t["p"] == 0]
            pend_ = []
            for t in tasks:
                emit_front(t)
                pend_.append(t)
                if len(pend_) > 2:
                    emit_back(pend_.pop(0))
                yield
            while pend_:
                emit_back(pend_.pop(0))

        def sgu_pre(ci):
            uslab, r_us = load_slab(9)
            for fp in range(2):
                mi = nxt("mm", 2)
                groups = [(mm[mi][:, j * 512:(j + 1) * 512],
                           [(uslab[:, kc, (fp * 2 + j) * 128:(fp * 2 + j + 1) * 128], hT[:, kc, :]) for kc in range(8)],
                           True, True, None) for j in range(2)]
                pe_groups(groups, r_h + [r_us], [r_mm[mi]])
                S.op("act", lambda h, mi=mi, fp=fp: h.activation(out=uT[:, fp * 2:fp * 2 + 2, :],
                                                                 in_=mm[mi][:, :].rearrange("p (j c) -> p j c", j=2), func=AF.Gelu),
                     reads=[r_mm[mi]], writes=[r_u])
            vslab, r_vs = load_slab(10)
            for tp_ in range(2):
                mi = nxt("mm", 2)
                groups = [(mm[mi][:, j * 512:(j + 1) * 512],
                           [(hT[:, kc, (tp_ * 2 + j) * 128:(tp_ * 2 + j + 1) * 128], vslab[:, kc, :]) for kc in range(8)],
                           True, True, None) for j in range(2)]
                pe_groups(groups, r_h + [r_vs], [r_mm[mi]])
                S.op("act", lambda h, mi=mi, tp_=tp_: h.activation(out=vz[:, tp_ * 2:tp_ * 2 + 2, :],
                                                                   in_=mm[mi][:, :].rearrange("p (j c) -> p j c", j=2), func=AF.Gelu),
                     reads=[r_mm[mi]], writes=[r_vz])

        def sgu_stage(ci):
            for tp_ in range(2):
                def stats1(h, tp_=tp_):
                    h.bn_stats(out=bn6[:, 0, :], in_=vz[:, tp_ * 2, :])
                    return h.bn_stats(out=bn6[:, 1, :], in_=vz[:, tp_ * 2 + 1, :])

                def stats2(h):
                    h.bn_aggr(out=mv[:, 0, :], in_=bn6[:, 0, :])
                    return h.bn_aggr(out=mv[:, 1, :], in_=bn6[:, 1, :])
                S.op("dve", stats1, reads=[r_vz], writes=[r_ln])
                S.op("dve", stats2, reads=[r_ln], writes=[r_ln])
                if ci == FIRST_OWN:
                    S.op("act", lambda h: h.activation(out=lsd[:, 0:2], in_=mv[:, :, 1], func=AF.Sqrt, scale=1.0, bias=epsb[:, 0:1]),
                         reads=[r_ln, r_c], writes=[r_ln])
                    S.op("dve", lambda h: h.reciprocal(out=lrs[:, 0:2], in_=lsd[:, 0:2]), reads=[r_ln], writes=[r_ln])
                else:
                    S.op("dve", lambda h: h.tensor_scalar(out=lsd[:, 0:2], in0=mv[:, :, 1], scalar1=EPS, scalar2=None, op0=ALU.add),
                         reads=[r_ln], writes=[r_ln])
                    S.op("pool", lambda h: h.tensor_tensor(out=lrs[:, 0:2], in0=lsd[:, 0:2], in1=nhalf[:, 0:2], op=ALU.pow),
                         reads=[r_ln, r_c], writes=[r_ln])
                yield
                for j in range(2):
                    tb = tp_ * 2 + j
                    vi = nxt("vn", 2)
                    S.op("dve", lambda h, j=j, tb=tb: h.tensor_scalar(out=vt, in0=vz[:, tb, :], scalar1=mv[:, j, 0:1], scalar2=lrs[:, j:j + 1],
                                                                      op0=ALU.subtract, op1=ALU.mult),
                         reads=[r_vz, r_ln], writes=[r_vt])
                    S.op("dve", lambda h: h.tensor_tensor(out=vt, in0=vt, in1=lng[:, :], op=ALU.mult), reads=[r_vt, r_c], writes=[r_vt])

                    def nrm3(h, vi=vi):
                        v4 = vt.rearrange("p (g e c) -> p g e c", g=4, e=2)
                        b4 = lnb[:, :].rearrange("p (g e c) -> p g e c", g=4, e=2)
                        a4 = vnA[:, vi, :].rearrange("p (g e c) -> p g e c", g=4, e=2)
                        c4 = vnB[:, vi, :].rearrange("p (g e c) -> p g e c", g=4, e=2)
                        h.tensor_tensor(out=a4[:, :, 0, :], in0=v4[:, :, 0, :], in1=b4[:, :, 0, :], op=ALU.add)
                        return h.tensor_tensor(out=c4[:, :, 1, :], in0=v4[:, :, 1, :], in1=b4[:, :, 1, :], op=ALU.add)
                    S.op("dve", nrm3, reads=[r_vt, r_c], writes=[r_vn[vi]])
                    si = 0
                    yield
                    groups = [(mm[si][:, ft * 128:(ft + 1) * 128],
                               [(vnA[:, vi, ft * 128:(ft + 1) * 128], WcT[:, 2 * ft, :]),
                                (vnB[:, vi, ft * 128:(ft + 1) * 128], WcT[:, 2 * ft + 1, :])], True, True, None) for ft in range(4)]
                    pe_groups(groups, [r_vn[vi], r_c], [r_mm[si]])
                    S.op("dve", lambda h, si=si: h.tensor_tensor(out=sgt, in0=mm[si][:, 0:512], in1=bsp[:, :], op=ALU.add),
                         reads=[r_mm[si], r_c], writes=[r_sgt])
                    S.op("dve", lambda h, tb=tb: h.tensor_tensor(out=sguT[:, :, tb * 128:(tb + 1) * 128],
                                                                 in0=sgt.rearrange("p (f t) -> p f t", f=4),
                                                                 in1=uT[:, :, tb * 128:(tb + 1) * 128], op=ALU.mult),
                         reads=[r_sgt, r_u], writes=[r_sg])

        def merge_stage(ci):
            xbuf = xbuf2[:, ci % 2]
            r_x = r_xb[ci % 2]
            n2 = norm_gen(g2c, ci % 2, ci == FIRST_OWN)
            S.fence(r_q + [r_u], r_mg + r_gb)
            S.fence([r_vz, r_vt, r_sgt], [r_t1])
            for hs in range(2):
                gslab, r_gs = load_slab(11 + hs)
                for tq in range(2):
                    tp_ = hs * 2 + tq
                    mi = nxt("mm", 2)
                    groups = [(mm[mi][:, j * 512:(j + 1) * 512],
                               [(gslab[:, kc, (tq * 2 + j) * 128:(tq * 2 + j + 1) * 128], hT[:, kc, :]) for kc in range(8)],
                               True, True, None) for j in range(2)]
                    pe_groups(groups, r_h + [r_gs], [r_mm[mi]])
                    S.op("act", lambda h, mi=mi, tp_=tp_: h.activation(out=mergedT[:, tp_ * 2:tp_ * 2 + 2, :],
                                                                       in_=mm[mi][:, :].rearrange("p (j c) -> p j c", j=2), func=AF.Sigmoid),
                         reads=[r_mm[mi]], writes=[r_mg[tp_]])
            paslab, r_pa = load_slab(15)
            for tp_ in range(4):
                mi = nxt("mm", 2)
                groups = [(mm[mi][:, j * 512:(j + 1) * 512],
                           [(paslab[:, kc, (tp_ * 2 + j) * 128:(tp_ * 2 + j + 1) * 128], attnT[:, kc, :]) for kc in range(4)],
                           True, True, None) for j in range(2)]
                pe_groups(groups, r_at + [r_pa], [r_mm[mi]])
                S.op("dve", lambda h, mi=mi, tp_=tp_: h.tensor_tensor(out=mergedT[:, tp_ * 2:tp_ * 2 + 2, :],
                                                                      in0=mm[mi][:, :].rearrange("p (j c) -> p j c", j=2),
                                                                      in1=mergedT[:, tp_ * 2:tp_ * 2 + 2, :], op=ALU.mult),
                     reads=[r_mm[mi], r_mg[tp_]], writes=[r_mg[tp_]])
            for hs in range(2):
                gslab, r_gs = load_slab(13 + hs)
                for tq in range(2):
                    tp_ = hs * 2 + tq
                    mi = nxt("mm", 2)
                    groups = [(mm[mi][:, j * 512:(j + 1) * 512],
                               [(gslab[:, kc, (tq * 2 + j) * 128:(tq * 2 + j + 1) * 128], hT[:, kc, :]) for kc in range(8)],
                               True, True, None) for j in range(2)]
                    pe_groups(groups, r_h + [r_gs], [r_mm[mi]])
                    S.op("act", lambda h, mi=mi, tp_=tp_: h.activation(out=gbT[:, tp_ * 2:tp_ * 2 + 2, :],
                                                                       in_=mm[mi][:, :].rearrange("p (j c) -> p j c", j=2), func=AF.Sigmoid),
                         reads=[r_mm[mi]], writes=[r_gb[tp_]])
            pbslab, r_pbs = load_slab(16)
            for tp_ in range(4):
                mi = nxt("mm", 2)
                groups = [(mm[mi][:, j * 512:(j + 1) * 512],
                           [(pbslab[:, kc, (tp_ * 2 + j) * 128:(tp_ * 2 + j + 1) * 128], sguT[:, kc, :]) for kc in range(4)],
                           True, True, None) for j in range(2)]
                pe_groups(groups, [r_sg, r_pbs], [r_mm[mi]])

                S.op("dve", lambda h, mi=mi, tp_=tp_: h.tensor_tensor(
                    out=t1, in0=mm[mi][:, :], in1=gbT[:, tp_ * 2:tp_ * 2 + 2, :].rearrange("p a b -> p (a b)"), op=ALU.mult),
                    reads=[r_mm[mi], r_gb[tp_]], writes=[r_t1])
                S.op("dve", lambda h, tp_=tp_: h.tensor_tensor(
                    out=mergedT[:, tp_ * 2:tp_ * 2 + 2, :].rearrange("p a b -> p (a b)"), in0=t1,
                    in1=mergedT[:, tp_ * 2:tp_ * 2 + 2, :].rearrange("p a b -> p (a b)"), op=ALU.add),
                    reads=[r_t1, r_mg[tp_]], writes=[r_mg[tp_]])
            oslabs = [load_slab(17), load_slab(18)]
            for tp_ in range(2):
                for hf in range(2):
                    oslab, r_os = oslabs[hf]
                    mi = nxt("mm", 2)
                    groups = [(mm[mi][:, j * 512:(j + 1) * 512],
                               [(mergedT[:, kc, (tp_ * 2 + j) * 128:(tp_ * 2 + j + 1) * 128], oslab[:, kc, :]) for kc in range(8)],
                               True, True, None) for j in range(2)]
                    pe_groups(groups, r_mg + [r_os], [r_mm[mi]])
                    S.op("dve", lambda h, mi=mi, tp_=tp_, hf=hf: h.tensor_tensor(
                        out=xbuf[:, tp_ * 2:tp_ * 2 + 2, hf * 512:(hf + 1) * 512],
                        in0=xbuf[:, tp_ * 2:tp_ * 2 + 2, hf * 512:(hf + 1) * 512],
                        in1=mm[mi][:, :].rearrange("p (j c) -> p j c", j=2), op=ALU.add),
                        reads=[r_mm[mi]], writes=[r_x[tp_ * 2], r_x[tp_ * 2 + 1]])
                    if tp_ == 0 and hf == 1:
                        next(n2, None)
                        next(n2, None)
                    if tp_ == 1 and hf == 0:
                        next(n2, None)
            for _ in n2:
                pass

        def ffn_stage(ci, nxt_norm):
            xbuf = xbuf2[:, ci % 2]
            r_x = r_xb[ci % 2]
            S.fence(r_q + [r_u, r_sg] + r_at + r_mg + r_gb, r_ff)
            S.fence(r_stg, r_stg)
            for s in range(6):
                gs, r_gs = load_slab(19 + s)
                us, r_us = load_slab(25 + s)
                ntp = 2 if s < 5 else 1
                for tq in range(ntp):
                    fi = s * 2 + tq
                    mg_ = nxt("mm", 2)
                    groups = [(mm[mg_][:, j * 512:(j + 1) * 512],
                               [(gs[:, kc, (tq * 2 + j) * 128:(tq * 2 + j + 1) * 128], hT[:, kc, :]) for kc in range(8)],
                               True, True, None) for j in range(2)]
                    pe_groups(groups, r_h + [r_gs], [r_mm[mg_]])
                    si = nxt("stg", 2)
                    S.op("act", lambda h, mg_=mg_, si=si: h.activation(out=sgl[:, si, :], in_=mm[mg_][:, :], func=AF.Silu),
                         reads=[r_mm[mg_]], writes=[r_stg[si]])
                    mu_ = nxt("mm", 2)
                    groups = [(mm[mu_][:, j * 512:(j + 1) * 512],
                               [(us[:, kc, (tq * 2 + j) * 128:(tq * 2 + j + 1) * 128], hT[:, kc, :]) for kc in range(8)],
                               True, True, None) for j in range(2)]
                    pe_groups(groups, r_h + [r_us], [r_mm[mu_]])
                    S.op("dve", lambda h, mu_=mu_, si=si, fi=fi: h.tensor_tensor(
                        out=ffT[:, fi * 2:fi * 2 + 2, :].rearrange("p a b -> p (a b)"), in0=sgl[:, si, :], in1=mm[mu_][:, :], op=ALU.mult),
                        reads=[r_mm[mu_], r_stg[si]], writes=[r_ff[fi]])
            for s in range(6):
                ds, r_ds = load_slab(31 + s)
                nk = 4 if s < 5 else 2
                for tb in range(4):
                    mi = nxt("mm", 2)
                    groups = [(mm[mi][:, hf * 512:(hf + 1) * 512],
                               [(ffT[:, s * 4 + kl, tb * 128:(tb + 1) * 128], ds[:, kl, hf * 512:(hf + 1) * 512]) for kl in range(nk)],
                               True, True, None) for hf in range(2)]
                    pe_groups(groups, [r_ff[s * 2 + kl // 2] for kl in range(0, nk, 2)] + [r_ds], [r_mm[mi]])
                    S.op("dve", lambda h, mi=mi, tb=tb: h.tensor_tensor(out=xbuf[:, tb, :], in0=xbuf[:, tb, :], in1=mm[mi][:, :], op=ALU.add),
                         reads=[r_mm[mi]], writes=[r_x[tb]])
                    if nxt_norm is not None and (s * 4 + tb) % 2 == 1:
                        next(nxt_norm, None)
            if nxt_norm is not None:
                for _ in nxt_norm:
                    pass
            for tb in range(4):
                sl = nxt("xs", 2)
                S.op("act", lambda h, tb=tb, sl=sl: h.activation(out=xs[:, sl, :], in_=xbuf[:, tb, :], func=AF.Square,
                                                                   accum_out=ss[:, tb:tb + 1]),
                     reads=[r_x[tb]], writes=[r_xs[sl], r_small[tb]])
                emit_rstd(tb, ci == FIRST_OWN)
                S.op("dve", lambda h, tb=tb: h.scalar_tensor_tensor(out=xbuf[:, tb, :], in0=xbuf[:, tb, :], scalar=rstd[:, tb:tb + 1],
                                                                    in1=gfin[:, :], op0=ALU.mult, op1=ALU.mult),
                     reads=[r_x[tb], r_small[tb], r_c], writes=[r_x[tb]])
                r0 = (ci - FIRST_OWN) * T + tb * 128
                S.dma("pool", f"st{tb}", y[r0:r0 + 128, :], xbuf[:, tb, :], reads=[r_x[tb]])

        load_x(0)
        norm_to_hT(g1c, 0, True)
        for ci in range(nchunks):
            nn = None
            own = ci >= FIRST_OWN
            if ci + 1 < nchunks:
                if not own:
                    load_x(ci + 1)
                nn = norm_gen(g1c, (ci + 1) % 2, ci + 1 <= FIRST_OWN)
            if ci == FIRST_OWN:
                tap("hT", hT[:, :, :], r_h)
            if not own:
                qkv_stage(ci, 2, False)
                if ci == FIRST_OWN - 1:
                    qkv_stage(ci, 0, False)
                    qkv_stage(ci, 1, False)
                if nn is not None:
                    for _ in nn:
                        pass
                continue
            S.fence(r_ff + r_mg + r_gb, r_q + [r_u, r_sg] + r_at)
            S.fence([r_t1], [r_vz, r_vt, r_sgt])
            for g in range(3):
                qkv_stage(ci, g, True)
            if ci == FIRST_OWN:
                tap("qT", qT, r_q)
                for g in range(3):
                    tap(f"kr{g}", kr[g][:, :, :, :], r_k[g])
                    tap(f"vr{g}", vr[g][:, :, :], r_v[g])
            sgu_pre(ci)
            ga_ = attention(ci)
            gs_ = sgu_stage(ci)
            k_ = 0
            for _ in ga_:
                k_ += 1
                if k_ % 9 == 4:
                    next(gs_, None)
            for _ in gs_:
                pass
            if ci == FIRST_OWN:
                tap("attnT", attnT, r_at)
                tap("sguT", sguT, [r_sg])
                tap("uT", uT, [r_u])
            if ci + 1 < nchunks:
                load_x(ci + 1)
            merge_stage(ci)
            if ci == FIRST_OWN:
                tap("mergedT", mergedT, r_mg)
                tap("x1", xbuf2[:, ci % 2], r_xb[ci % 2])
            ffn_stage(ci, nn)

        fin = [(f"st{i}", S.cnt[f"st{i}"]) for i in range(4)] + [("tap", S.cnt["tap"])]
        S.wait_all("sp", fin)
        S.emit()
    return nc


def make_in_maps(x, positions, norm1_g, w_in, sgu_ln_g, sgu_ln_b, w_spatial, b_spatial, w_proj_attn, w_proj_sgu,
                 w_out, norm2_g, w_ffn_gate, w_ffn_up, w_ffn_down, final_g):
    f32 = np.float32
    x = np.asarray(x, f32)
    positions = np.asarray(positions, np.int32)
    shared = {
        "w_in": np.ascontiguousarray(np.asarray(w_in, f32)[0]),
        "wpa": np.ascontiguousarray(np.asarray(w_proj_attn, f32)[0]),
        "wps": np.ascontiguousarray(np.asarray(w_proj_sgu, f32)[0]),
        "wout": np.ascontiguousarray(np.asarray(w_out, f32)[0]),
        "wg": np.ascontiguousarray(np.asarray(w_ffn_gate, f32)[0]),
        "wu": np.ascontiguousarray(np.asarray(w_ffn_up, f32)[0]),
        "wd": np.ascontiguousarray(np.asarray(w_ffn_down, f32)[0]),
        "g1c": np.ascontiguousarray(np.asarray(norm1_g, f32)[0].reshape(8, 128).T),
        "g2c": np.ascontiguousarray(np.asarray(norm2_g, f32)[0].reshape(8, 128).T),
        "gfin": np.ascontiguousarray(np.broadcast_to(np.asarray(final_g, f32)[None, :], (128, D))),
        "lng": np.ascontiguousarray(np.broadcast_to(np.asarray(sgu_ln_g, f32)[0][None, :], (128, 512))),
        "lnb": np.ascontiguousarray(np.broadcast_to(np.asarray(sgu_ln_b, f32)[0][None, :], (128, 512))),
        "bsp": np.ascontiguousarray(np.repeat(np.asarray(b_spatial, f32)[0].reshape(4, 2, 128), 64, axis=1)
                                    .transpose(1, 0, 2).reshape(128, 512)),
        "wsp": np.ascontiguousarray(np.asarray(w_spatial, f32)[0].transpose(2, 0, 1).reshape(128, 1024)),
    }
    in_maps = []
    for core in range(NCORE):
        b, half = core // 2, core % 2
        s0 = half * OWN
        xin = np.zeros((OWN + HALO, D), f32)
        pin = np.zeros((OWN + HALO,), np.int32)
        xin[HALO:] = x[b, s0:s0 + OWN]
        pin[HALO:] = positions[b, s0:s0 + OWN]
        if half == 1:
            xin[:HALO] = x[b, s0 - HALO:s0]
            pin[:HALO] = positions[b, s0 - HALO:s0]
        m = dict(shared)
        m["xin"] = xin
        m["pos"] = np.ascontiguousarray(pin.reshape(48, 128).T)
        m["hbias"] = np.full((128, 1), 0.0 if half == 1 else NEG, f32)
        in_maps.append(m)
    return in_maps


_NC_CACHE = {}


def kernel(x, positions, norm1_g, w_in, sgu_ln_g, sgu_ln_b, w_spatial, b_spatial, w_proj_attn, w_proj_sgu,
           w_out, norm2_g, w_ffn_gate, w_ffn_up, w_ffn_down, final_g):
    in_maps = make_in_maps(x, positions, norm1_g, w_in, sgu_ln_g, sgu_ln_b, w_spatial, b_spatial, w_proj_attn,
                           w_proj_sgu, w_out, norm2_g, w_ffn_gate, w_ffn_up, w_ffn_down, final_g)
    if "nc" not in _NC_CACHE:
        _NC_CACHE["nc"] = build_program()
    nc = _NC_CACHE["nc"]
    res = run_bass_kernel_spmd(nc, in_maps, core_ids=list(range(NCORE)))
    out = np.zeros((4, SEQ, D), np.float32)
    for core in range(NCORE):
        b, half = core // 2, core % 2
        out[b, half * OWN:(half + 1) * OWN] = np.asarray(res.results[core]["y"], np.float32)
    return out
```

```python
import contextlib
import os
import numpy as np
import concourse.bass as bass
import concourse.mybir as mybir
from concourse.bass_utils import run_bass_kernel_spmd

F32 = mybir.dt.float32
BF16 = mybir.dt.bfloat16
I32 = mybir.dt.int32
AF = mybir.ActivationFunctionType
ALU = mybir.AluOpType

D = 1024
SEQ = 8192
NCORE = 8
OWN = 4096
HALO = 2048
T = 512
NCH = (OWN + HALO) // T
FIRST_OWN = HALO // T
DFF = 2816
NFT = DFF // 128
EPS = 1e-6
ROPE_THETA = 500000.0
NSLAB = 37
NEG = -30000.0


class Res:
    __slots__ = ("name", "last_w", "readers")

    def __init__(self, name):
        self.name = name
        self.last_w = None
        self.readers = {}


class Sched:
    ENG = ("pe", "act", "dve", "pool", "sp")

    def __init__(self, nc, stack):
        self.nc = nc
        self.stack = stack
        self.prog = {e: [] for e in self.ENG}
        self.sem = {}
        self.cnt = {}
        self.waited = {e: {} for e in self.ENG}
        for e in self.ENG:
            self.new_sem("c_" + e)

    def new_sem(self, name):
        s = self.stack.enter_context(self.nc.semaphore(name))
        self.sem[name] = s
        self.cnt[name] = 0
        return name

    def _deps(self, eng, reads, writes, deps):
        toks = {}

        def add(t):
            if t is None:
                return
            s, v = t
            if toks.get(s, 0) < v:
                toks[s] = v
        for t in deps:
            add(t)
        for r in reads:
            add(r.last_w)
        for r in writes:
            add(r.last_w)
            for s, v in r.readers.items():
                add((s, v))
        own = "c_" + eng
        for s, v in toks.items():
            if s == own and eng == "pe":
                continue
            if self.waited[eng].get(s, 0) < v:
                self.prog[eng].append(("wait", s, v))
                self.waited[eng][s] = v

    def op(self, eng, fn, reads=(), writes=(), deps=(), sem=None, inc=1):
        self._deps(eng, reads, writes, deps)
        s = sem or ("c_" + eng)
        self.cnt[s] += inc
        tok = (s, self.cnt[s])
        self.prog[eng].append(("inst", fn, s, inc))
        for r in reads:
            if r.readers.get(s, 0) < tok[1]:
                r.readers[s] = tok[1]
        for r in writes:
            r.last_w = tok
            r.readers = {}
        return tok

    def dma(self, eng, sem, out, in_, reads=(), writes=(), deps=()):
        def fn(h):
            return h.dma_start(out=out, in_=in_)
        return self.op(eng, fn, reads=reads, writes=writes, deps=deps, sem=sem, inc=16)

    def wait_all(self, eng, toks):
        for t in toks:
            if t is None:
                continue
            s, v = t
            if self.waited[eng].get(s, 0) < v:
                self.prog[eng].append(("wait", s, v))
                self.waited[eng][s] = v

    def fence(self, src, dst):
        for d in dst:
            for r in src:
                if r.last_w is not None:
                    s, v = r.last_w
                    if d.readers.get(s, 0) < v:
                        d.readers[s] = v
                for s, v in r.readers.items():
                    if d.readers.get(s, 0) < v:
                        d.readers[s] = v

    def emit(self):
        nc = self.nc
        with nc.Block() as block:
            def run(e):
                def body(h):
                    for it in self.prog[e]:
                        if it[0] == "wait":
                            h.wait_ge(self.sem[it[1]], it[2])
                        else:
                            ins = it[1](h)
                            ins.then_inc(self.sem[it[2]], it[3])
                return body
            block.tensor(run("pe"))
            block.scalar(run("act"))
            block.vector(run("dve"))
            block.gpsimd(run("pool"))
            block.sync(run("sp"))


def slab_src(wd, s):
    if s < 15:
        return wd["w_in"][:, s * 512:(s + 1) * 512].rearrange("(k p) c -> p k c", p=128), 8, 512
    if s == 15:
        return wd["wpa"].rearrange("(k p) c -> p k c", p=128), 4, 1024
    if s == 16:
        return wd["wps"].rearrange("(k p) c -> p k c", p=128), 4, 1024
    if s < 19:
        h = s - 17
        return wd["wout"][:, h * 512:(h + 1) * 512].rearrange("(k p) c -> p k c", p=128), 8, 512
    if s < 31:
        nm = "wg" if s < 25 else "wu"
        j = (s - 19) % 6
        c = 512 if j < 5 else 256
        return wd[nm][:, j * 512:j * 512 + c].rearrange("(k p) c -> p k c", p=128), 8, c
    j = s - 31
    k = 4 if j < 5 else 2
    return wd["wd"][j * 512:j * 512 + k * 128, :].rearrange("(k p) c -> p k c", p=128), k, 1024


def build_program(taps=None, nchunks=NCH, stages=5):
    taps = taps or {}
    nc = bass.Bass("TRN2", target_bir_lowering=False)

    def din(name, shape, dt=F32):
        return nc.dram_tensor(name, shape, dt, kind="ExternalInput").ap()

    xin = din("xin", [OWN + HALO, D])
    pos = din("pos", [128, 48], I32)
    hbias_d = din("hbias", [128, 1])
    wd = {"w_in": din("w_in", [D, 7680]), "wpa": din("wpa", [512, D]), "wps": din("wps", [512, D]),
          "wout": din("wout", [D, D]), "wg": din("wg", [D, DFF]), "wu": din("wu", [D, DFF]),
          "wd": din("wd", [DFF, D])}
    g1c_d = din("g1c", [128, 8])
    g2c_d = din("g2c", [128, 8])
    gfin_d = din("gfin", [128, D])
    lng_d = din("lng", [128, 512])
    lnb_d = din("lnb", [128, 512])
    bsp_d = din("bsp", [128, 512])
    wsp_d = din("wsp", [128, 1024])
    y = nc.dram_tensor("y", [OWN, D], F32, kind="ExternalOutput").ap()
    wbf = nc.dram_tensor("wbf", [NSLAB, 128, 4096], BF16, kind="Internal").ap()
    tap_out = {}
    for nm, (shape, dt) in taps.items():
        tap_out[nm] = nc.dram_tensor("tap_" + nm, shape, dt, kind="ExternalOutput").ap()

    with contextlib.ExitStack() as st:
        S = Sched(nc, st)

        def sb(name, shape, dt):
            return st.enter_context(nc.sbuf_tensor("s_" + name, shape, dt))

        def ps(name, shape, dt):
            return st.enter_context(nc.psum_tensor("p_" + name, shape, dt))

        xbuf2 = sb("xbuf", [128, 2, 4, D], F32)
        hT = sb("hT", [128, 8, T], BF16)
        wring = sb("wring", [128, 3, 4096], BF16)
        kr = [sb("kr0", [128, 4, 8, 128], BF16), sb("kr1", [128, 4, 8, 128], BF16),
              sb("kr2", [128, 4, 20, 128], BF16)]
        vr = [sb("vr0", [128, 8, 512], BF16), sb("vr1", [128, 8, 512], BF16), sb("vr2", [128, 20, 512], BF16)]
        NSL = [8, 8, 20]
        sA = sb("sA", [128, 12288], BF16)
        qT = sA[:, 0:6144].rearrange("p (g f t) -> p g f t", g=3, f=4)
        uT = sA[:, 6144:8192].rearrange("p (f t) -> p f t", f=4)
        attnT = sA[:, 8192:10240].rearrange("p (f t) -> p f t", f=4)
        sguT = sA[:, 10240:12288].rearrange("p (f t) -> p f t", f=4)
        mergedT = sA[:, 0:4096].rearrange("p (f t) -> p f t", f=8)
        gbT = sA[:, 4096:8192].rearrange("p (f t) -> p f t", f=8)
        ffT = sA[:, 0:NFT * T].rearrange("p (f t) -> p f t", f=NFT)
        stg = sb("stg", [128, 2, 1024], BF16)
        sgl = stg
        xs = stg
        pbuf = sb("pbuf", [128, 4, 512], BF16)
        qz = sb("qz", [128, 2, 2, 512], BF16)
        accN = sb("accN", [128, 512], F32)
        accD = sb("accD", [128, 512], F32)
        sgm = sb("sgm", [128, 2048], F32)
        vz = sgm[:, 0:1024].bitcast(BF16).rearrange("p (j c) -> p j c", j=4)
        vt = sgm[:, 1024:1536]
        sgt = sgm[:, 1536:2048]
        t1 = sgm[:, 0:1024]
        ropet = sgm[:, 0:512].rearrange("p (j c) -> p j c", j=4)
        vnA = sb("vnA", [128, 2, 512], BF16)
        vnB = sb("vnB", [128, 2, 512], BF16)
        mtb = sb("mtb", [128, 512], BF16)
        m3 = sb("m3", [128, 3, 512], BF16)
        cosT = sb("cosT", [128, 48, 8], F32)
        sinT = sb("sinT", [128, 48, 8], F32)
        gfin = sb("gfin", [128, D], F32)
        lng = sb("lng", [128, 512], F32)
        lnb = sb("lnb", [128, 512], F32)
        bsp = sb("bsp", [128, 512], F32)
        WcT = sb("WcT", [128, 8, 128], BF16)
        ident = sb("ident", [128, 128], BF16)
        ones64 = sb("ones64", [128, 64], BF16)
        g1c = sb("g1c", [128, 8], F32)
        g2c = sb("g2c", [128, 8], F32)
        hbias = sb("hbias", [128, 1], F32)
        epsb = sb("epsb", [128, 1], F32)
        small = sb("small", [128, 64], F32)
        ss = small[:, 0:4]
        sd = small[:, 4:8]
        rstd = small[:, 8:12]
        bn6 = small[:, 16:28].rearrange("p (j s) -> p j s", j=2)
        mv = small[:, 28:32].rearrange("p (j s) -> p j s", j=2)
        lsd = small[:, 32:34]
        lrs = small[:, 34:36]
        nhalf = small[:, 40:44]

        mm = [ps("mm0", [128, 1024], F32), ps("mm1", [128, 1024], F32)]
        sps = [ps("sps0", [128, 512], F32), ps("sps1", [128, 512], F32)]
        pvN = ps("pvN", [128, 512], F32)
        pvD = ps("pvD", [128, 512], F32)

        R = lambda n: Res(n)
        r_xb = [[R(f"x{b}_{i}") for i in range(4)] for b in range(2)]
        r_x = r_xb[1]
        r_h = [R(f"h{i}") for i in range(4)]
        r_w = [R(f"w{i}") for i in range(3)]
        r_k = [[R(f"k{g}_{s}") for s in range(NSL[g])] for g in range(3)]
        r_v = [[R(f"v{g}_{s}") for s in range(NSL[g])] for g in range(3)]
        r_q = [R(f"q{g}") for g in range(3)]
        r_u = R("uT")
        r_at = [R(f"at{p}") for p in range(4)]
        r_sg = R("sguT")
        r_mg = [R(f"mg{i}") for i in range(4)]
        r_gb = [R(f"gb{i}") for i in range(4)]
        r_ff = [R(f"ff{i}") for i in range(NFT // 2)]
        r_stg = [R("stg0"), R("stg1")]
        r_xs = r_stg
        r_pb = [R(f"pb{i}") for i in range(4)]
        r_qz = [R("qz0"), R("qz1")]
        r_acc = R("acc")
        r_vz = R("vz")
        r_rt = r_vz
        r_vt = R("vt")
        r_sgt = R("sgt")
        r_t1 = R("t1")
        r_vn = [R("vn0"), R("vn1")]
        r_mm = [R("mm0"), R("mm1")]
        r_sps = [R("sps0"), R("sps1")]
        r_pv = R("pv")
        r_pvd = R("pvd")
        r_c = R("consts")
        r_small = [R(f"sm{i}") for i in range(4)]
        r_ln = R("lnsmall")
        r_cv = [R(f"cv{s}") for s in range(NSLAB)]
        for i in range(4):
            S.new_sem(f"xs0{i}")
            S.new_sem(f"xs1{i}")
            S.new_sem(f"st{i}")
        for i in range(3):
            S.new_sem(f"w{i}")
        for s in range(NSLAB):
            S.new_sem(f"cv{s}")
        S.new_sem("ld")
        S.new_sem("tap")

        state = {"mm": 0, "sps": 0, "w": 0, "pb": 0, "stg": 0, "xs": 0, "vn": 0, "qz": 0, "tr": 0}
        r_xs = r_stg
        trt = [t_[:, :].bitcast(BF16).rearrange("p (k t) -> p k t", k=8) for t_ in (sps[0], sps[1], pvN, pvD)]
        r_tr = [r_sps[0], r_sps[1], r_pv, r_pvd]

        def nxt(key, n):
            i = state[key]
            state[key] = (i + 1) % n
            return i

        def load_slab(s):
            _, k, c = slab_src(wd, s)
            slot = nxt("w", 3)
            S.dma("sp", f"w{slot}", wring[:, slot, 0:k * c], wbf[s][:, 0:k * c],
                  reads=[r_cv[s]], writes=[r_w[slot]])
            return wring[:, slot, 0:k * c].rearrange("p (k c) -> p k c", k=k), r_w[slot]

        def load_x(ci):
            b_ = ci % 2
            for tb in range(4):
                r0 = ci * T + tb * 128
                S.dma("sp", f"xs{b_}{tb}", xbuf2[:, b_, tb, :], xin[r0:r0 + 128, :], writes=[r_xb[b_][tb]])

        cv_order = [5, 8, 3, 6, 4, 7, 0, 1, 2, 9, 10, 11, 12, 15, 13, 14, 16, 17, 18]
        for j in range(6):
            cv_order += [19 + j, 25 + j]
        cv_order += list(range(31, 37))
        assert sorted(cv_order) == list(range(NSLAB))
        cv_state = {"i": 0, "toks": []}

        def emit_cv(n):
            for _ in range(n):
                i = cv_state["i"]
                if i >= NSLAB:
                    return
                sl_ = cv_order[i]
                if i >= 3:
                    S.wait_all("pool", [cv_state["toks"][i - 3]])
                src, k, c = slab_src(wd, sl_)
                dst = wbf[sl_][:, 0:k * c].rearrange("p (k c) -> p k c", k=k)
                cv_state["toks"].append(S.dma("pool", f"cv{sl_}", dst, src, writes=[r_cv[sl_]]))
                cv_state["i"] = i + 1

        S.dma("sp", "ld", g1c[:, :], g1c_d, writes=[r_c])
        S.dma("sp", "ld", g2c[:, :], g2c_d, writes=[r_c])
        S.dma("sp", "ld", hbias[:, :], hbias_d, writes=[r_c])
        S.dma("sp", "ld", gfin[:, :], gfin_d, writes=[r_c])
        S.dma("sp", "ld", lng[:, :], lng_d, writes=[r_c])
        S.dma("sp", "ld", lnb[:, :], lnb_d, writes=[r_c])
        S.dma("sp", "ld", bsp[:, :], bsp_d, writes=[r_c])
        xf = xbuf2[:, 1, :, :].rearrange("p a b -> p (a b)")
        tmpA = xf[:, 0:1024]
        tmpI = xf[:, 1024:1536]
        S.dma("sp", "ld", tmpA, wsp_d, writes=r_x)
        S.dma("sp", "ld", tmpI[:, 0:48].bitcast(I32), pos, writes=r_x)
        ld_tok = ("ld", S.cnt["ld"])

        def setup_dve(h):
            onesf = xf[:, 1536:1664]
            trif = xf[:, 1664:1792]
            bandf = xf[:, 1792:1920]
            idf = xf[:, 1920:2048]
            modf = xf[:, 2048:2176]
            modi = xf[:, 2176:2304].bitcast(I32)
            h.memset(onesf, 1.0)
            h.memset(epsb[:, :], EPS)
            h.memset(nhalf, -0.5)
            h.memset(ones64[:, :], 1.0)
            h.memset(vnA[:, :, :], 0.0)
            h.memset(vnB[:, :, :], 0.0)
            h.memset(qz[:, :, :, :], 0.0)
            return h.memset(idf, 0.0)
        S.op("dve", setup_dve, writes=r_x + [r_c], deps=[ld_tok])

        def setup_pool(h):
            onesf = xf[:, 1536:1664]
            trif = xf[:, 1664:1792]
            bandf = xf[:, 1792:1920]
            idf = xf[:, 1920:2048]
            modi = xf[:, 2176:2304].bitcast(I32)
            h.affine_select(out=trif, in_=onesf, pattern=[[1, 128]], compare_op=ALU.is_ge, fill=0.0,
                            base=0, channel_multiplier=-1)
            h.affine_select(out=bandf, in_=onesf, pattern=[[-1, 128]], compare_op=ALU.is_ge, fill=0.0,
                            base=0, channel_multiplier=1)
            h.affine_select(out=idf, in_=onesf, pattern=[[-1, 128]], compare_op=ALU.is_equal, fill=0.0,
                            base=0, channel_multiplier=1)
            return h.iota(modi, pattern=[[1, 128]], base=128, channel_multiplier=-1)
        emit_cv(3)
        S.op("pool", setup_pool, reads=r_x, writes=r_x)
        emit_cv(NSLAB)

        INV2PI = float(1.0 / (2.0 * np.pi))
        C1 = 6.28125
        C2 = float(2.0 * np.pi - 6.28125)
        PI = float(np.pi)
        inv_freq = (ROPE_THETA ** (-np.arange(0, 16, 2, dtype=np.float32) / np.float32(16))).astype(np.float32)

        onesf = xf[:, 1536:1664]
        trif = xf[:, 1664:1792]
        bandf = xf[:, 1792:1920]
        idf = xf[:, 1920:2048]
        modf = xf[:, 2048:2176]
        modi = xf[:, 2176:2304].bitcast(I32)
        posf = xf[:, 2304:2352]
        ang = xf[:, 2432:2816].rearrange("p (b f) -> p b f", f=8)
        angf = xf[:, 2432:2816]
        ki = xf[:, 2816:3200].bitcast(I32)
        kf = xf[:, 3200:3584]
        fx = xf[:, 3584:3968]
        r_set = Res("setup")

        def SD(fn, eng="dve"):
            S.op(eng, fn, reads=r_x + [r_set], writes=r_x + [r_set, r_c])

        SD(lambda h: h.tensor_copy(out=ident[:, :], in_=idf))
        SD(lambda h: h.tensor_single_scalar(out=modi, in_=modi, scalar=3, op=ALU.bitwise_and))
        SD(lambda h: h.tensor_copy(out=modf, in_=modi))
        SD(lambda h: h.tensor_scalar(out=modf, in0=modf, scalar1=0.0, scalar2=None, op0=ALU.is_equal))

        def masks1(h):
            ins = None
            for j in range(2):
                h.tensor_copy(out=mtb[:, j * 128:(j + 1) * 128], in_=trif)
                h.tensor_copy(out=mtb[:, 256 + j * 128:256 + (j + 1) * 128], in_=bandf)
            for j in range(4):
                h.tensor_tensor(out=m3[:, 0, j * 128:(j + 1) * 128], in0=modf, in1=trif, op=ALU.mult)
                h.tensor_copy(out=m3[:, 1, j * 128:(j + 1) * 128], in_=modf)
                ins = h.tensor_tensor(out=m3[:, 2, j * 128:(j + 1) * 128], in0=modf, in1=bandf, op=ALU.mult)
            return ins
        SD(masks1)
        SD(lambda h: h.tensor_scalar(out=mtb[:, :], in0=mtb[:, :], scalar1=-1.0, scalar2=-NEG, op0=ALU.add, op1=ALU.mult))
        SD(lambda h: h.tensor_scalar(out=m3[:, :, :], in0=m3[:, :, :], scalar1=-1.0, scalar2=-NEG, op0=ALU.add, op1=ALU.mult))
        SD(lambda h: h.tensor_tensor(out=WcT[:, :, :], in0=tmpA.rearrange("p (g t) -> p g t", g=8),
                                     in1=trif.unsqueeze(1).to_broadcast([128, 8, 128]), op=ALU.mult))
        SD(lambda h: h.tensor_copy(out=posf, in_=tmpI[:, 0:48].bitcast(I32)))

        def angs(h):
            ins = None
            for f in range(8):
                ins = h.tensor_scalar(out=ang[:, :, f], in0=posf, scalar1=float(inv_freq[f]), scalar2=None, op0=ALU.mult)
            return ins
        SD(angs)
        SD(lambda h: h.tensor_scalar(out=ki, in0=angf, scalar1=INV2PI, scalar2=None, op0=ALU.mult))
        SD(lambda h: h.tensor_copy(out=kf, in_=ki))
        SD(lambda h: h.scalar_tensor_tensor(out=angf, in0=kf, scalar=-C1, in1=angf, op0=ALU.mult, op1=ALU.add))
        SD(lambda h: h.scalar_tensor_tensor(out=angf, in0=kf, scalar=-C2, in1=angf, op0=ALU.mult, op1=ALU.add))
        SD(lambda h: h.tensor_scalar(out=fx, in0=angf, scalar1=PI, scalar2=-2.0 * PI, op0=ALU.is_gt, op1=ALU.mult))
        SD(lambda h: h.tensor_tensor(out=angf, in0=angf, in1=fx, op=ALU.add))
        SD(lambda h: h.tensor_scalar(out=fx, in0=angf, scalar1=-PI, scalar2=2.0 * PI, op0=ALU.is_lt, op1=ALU.mult))
        SD(lambda h: h.tensor_tensor(out=angf, in0=angf, in1=fx, op=ALU.add))
        SD(lambda h: h.activation(out=sinT[:, :, :].rearrange("p b f -> p (b f)"), in_=angf, func=AF.Sin), eng="act")
        SD(lambda h: h.tensor_scalar(out=angf, in0=angf, scalar1=PI / 2.0, scalar2=None, op0=ALU.add))
        SD(lambda h: h.tensor_scalar(out=fx, in0=angf, scalar1=PI, scalar2=-2.0 * PI, op0=ALU.is_gt, op1=ALU.mult))
        SD(lambda h: h.tensor_tensor(out=angf, in0=angf, in1=fx, op=ALU.add))
        SD(lambda h: h.activation(out=cosT[:, :, :].rearrange("p b f -> p (b f)"), in_=angf, func=AF.Sin), eng="act")

        def pe_groups(groups, reads, writes):
            def fn(h):
                ins = None
                for out, pairs, st0, sp0, tp in groups:
                    n = len(pairs)
                    for i, (l, r) in enumerate(pairs):
                        kw = {}
                        if tp is not None:
                            kw["tile_position"] = tp
                        ins = h.matmul(out, lhsT=l, rhs=r, start=(st0 and i == 0), stop=(sp0 and i == n - 1), **kw)
                return ins
            return S.op("pe", fn, reads=reads, writes=writes)

        def tap(name, ap, reads):
            if name in tap_out:
                S.dma("sp", "tap", tap_out[name], ap, reads=reads)

        def emit_rstd(tb, early):
            if early:
                S.op("act", lambda h, tb=tb: h.activation(out=sd[:, tb:tb + 1], in_=ss[:, tb:tb + 1], func=AF.Sqrt,
                                                          scale=1.0 / D, bias=epsb[:, 0:1]),
                     reads=[r_small[tb], r_c], writes=[r_small[tb]])
                S.op("dve", lambda h, tb=tb: h.reciprocal(out=rstd[:, tb:tb + 1], in_=sd[:, tb:tb + 1]),
                     reads=[r_small[tb]], writes=[r_small[tb]])
            else:
                S.op("dve", lambda h, tb=tb: h.tensor_scalar(out=sd[:, tb:tb + 1], in0=ss[:, tb:tb + 1], scalar1=1.0 / D, scalar2=EPS,
                                                             op0=ALU.mult, op1=ALU.add),
                     reads=[r_small[tb]], writes=[r_small[tb]])
                S.op("pool", lambda h, tb=tb: h.tensor_tensor(out=rstd[:, tb:tb + 1], in0=sd[:, tb:tb + 1], in1=nhalf[:, 0:1], op=ALU.pow),
                     reads=[r_small[tb], r_c], writes=[r_small[tb]])

        def norm_gen(gcol, b_, early=False):
            xb_ = xbuf2[:, b_]
            rx_ = r_xb[b_]
            sls = {}

            def A(tb):
                sl = nxt("xs", 2)
                sls[tb] = sl
                S.op("act", lambda h, tb=tb, sl=sl: h.activation(out=xs[:, sl, :], in_=xb_[:, tb, :], func=AF.Square,
                                                                   accum_out=ss[:, tb:tb + 1]),
                     reads=[rx_[tb]], writes=[r_xs[sl], r_small[tb]])
                emit_rstd(tb, early)
                S.op("act", lambda h, tb=tb, sl=sl: h.activation(out=xs[:, sl, :], in_=xb_[:, tb, :], func=AF.Copy,
                                                                   scale=rstd[:, tb:tb + 1]),
                     reads=[rx_[tb], r_small[tb]], writes=[r_xs[sl]])

            def Tr(tb):
                sl = sls[tb]
                ti = nxt("tr", 4)
                trp = trt[ti]

                def tr(h, sl=sl, trp=trp):
                    ins = None
                    for kc in range(8):
                        ins = h.transpose(out=trp[:, kc, :], in_=xs[:, sl, kc * 128:(kc + 1) * 128], identity=ident[:, :])
                    return ins
                S.op("pe", tr, reads=[r_xs[sl], r_c], writes=[r_tr[ti]])
                S.op("dve", lambda h, tb=tb, trp=trp: h.tensor_tensor(
                    out=hT[:, :, tb * 128:(tb + 1) * 128], in0=trp,
                    in1=gcol[:, :].unsqueeze(2).to_broadcast([128, 8, 128]), op=ALU.mult),
                    reads=[r_tr[ti], r_c], writes=[r_h[tb]])
            A(0); yield
            A(1); yield
            Tr(0); yield
            A(2); yield
            Tr(1); yield
            A(3); yield
            Tr(2); yield
            Tr(3)

        def norm_to_hT(gcol, b_, early=False):
            for _ in norm_gen(gcol, b_, early):
                pass

        def tok_sel(g, blk):
            if g == 0:
                return lambda kc: hT[:, kc, blk * 128:(blk + 1) * 128]
            return lambda kc: hT[:, kc, :].rearrange("p (i c) -> p c i", c=4)[:, blk, :]

        def qkv_stage(ci, g, with_q):
            slot0 = 4 * (ci % (NSL[g] // 4))
            kslab, r_ks = load_slab(3 + g)
            if with_q:
                qslab, r_qs = load_slab(g)
            nf = 8 if with_q else 4
            pend = []

            def emit_tr(tb, si):
                mj = nxt("tr", 4)
                trp = trt[mj]

                def tr(h, si=si, trp=trp, nf=nf):
                    ins = None
                    for f in range(nf):
                        ins = h.transpose(out=trp[:, f, :], in_=stg[:, si, f * 128:(f + 1) * 128], identity=ident[:, :])
                    return ins
                S.op("pe", tr, reads=[r_stg[si], r_c], writes=[r_tr[mj]])
                kf0 = 4 if with_q else 0
                if g == 0:
                    if with_q:
                        S.op("dve", lambda h, trp=trp, tb=tb: h.tensor_copy(out=qT[:, 0, :, tb * 128:(tb + 1) * 128], in_=trp[:, 0:4, :]),
                             reads=[r_tr[mj]], writes=[r_q[0]])
                    S.op("dve", lambda h, trp=trp, tb=tb, kf0=kf0: h.tensor_copy(out=kr[0][:, :, slot0 + tb, :], in_=trp[:, kf0:kf0 + 4, :]),
                         reads=[r_tr[mj]], writes=[r_k[0][slot0 + tb]])
                else:
                    if with_q:
                        S.op("dve", lambda h, trp=trp, tb=tb: h.tensor_copy(
                            out=qT[:, g, :, :].rearrange("p f (c j) -> p f c j", c=4)[:, :, :, tb * 32:(tb + 1) * 32],
                            in_=trp[:, 0:4, :].rearrange("p f (i c) -> p f c i", c=4)),
                            reads=[r_tr[mj]], writes=[r_q[g]])
                    S.op("dve", lambda h, trp=trp, tb=tb, kf0=kf0: h.tensor_copy(
                        out=kr[g][:, :, slot0:slot0 + 4, tb * 32:(tb + 1) * 32],
                        in_=trp[:, kf0:kf0 + 4, :].rearrange("p f (i c) -> p f c i", c=4)),
                        reads=[r_tr[mj]], writes=[r_k[g][slot0 + c] for c in range(4)])

            for tb in range(4):
                mi = nxt("mm", 2)
                groups = []
                rd = [r_h[tb], r_ks]
                if with_q:
                    groups.append((mm[mi][:, 0:512], [(hT[:, kc, tb * 128:(tb + 1) * 128], qslab[:, kc, :]) for kc in range(8)],
                                   True, True, None))
                    rd.append(r_qs)
                    kcol = 512
                else:
                    kcol = 0
                groups.append((mm[mi][:, kcol:kcol + 512], [(hT[:, kc, tb * 128:(tb + 1) * 128], kslab[:, kc, :]) for kc in range(8)],
                               True, True, None))
                pe_groups(groups, rd, [r_mm[mi]])
                W = kcol + 512
                si = nxt("stg", 2)
                S.op("act", lambda h, mi=mi, si=si, W=W: h.copy(out=stg[:, si, 0:W], in_=mm[mi][:, 0:W]),
                     reads=[r_mm[mi]], writes=[r_stg[si]])
                nh = W // 64
                blk = ci * 4 + tb

                def rope_views(mi, si, W, nh, blk):
                    mv_ = mm[mi][:, 0:W].rearrange("p (h d) -> p h d", d=64)
                    sv = stg[:, si, 0:W].rearrange("p (h d) -> p h d", d=64)
                    cb = cosT[:, blk, :].unsqueeze(1).to_broadcast([128, nh, 8])
                    sn = sinT[:, blk, :].unsqueeze(1).to_broadcast([128, nh, 8])
                    tt = [ropet[:, j, 0:nh * 8].rearrange("p (h d) -> p h d", d=8) for j in range(4)]
                    return mv_, sv, cb, sn, tt

                def rope1(h, a=(mi, si, W, nh, blk)):
                    mv_, sv, cb, sn, tt = rope_views(*a)
                    x1 = mv_[:, :, 0:8]
                    x2 = mv_[:, :, 8:16]
                    h.tensor_tensor(out=tt[0], in0=x1, in1=cb, op=ALU.mult)
                    h.tensor_tensor(out=tt[1], in0=x2, in1=sn, op=ALU.mult)
                    h.tensor_tensor(out=tt[2], in0=x2, in1=cb, op=ALU.mult)
                    return h.tensor_tensor(out=tt[3], in0=x1, in1=sn, op=ALU.mult)

                def rope2(h, a=(mi, si, W, nh, blk)):
                    mv_, sv, cb, sn, tt = rope_views(*a)
                    h.tensor_tensor(out=sv[:, :, 0:8], in0=tt[0], in1=tt[1], op=ALU.subtract)
                    return h.tensor_tensor(out=sv[:, :, 8:16], in0=tt[2], in1=tt[3], op=ALU.add)
                import os
                if "norope1" not in os.environ.get("ATT_DBG", ""):
                    S.op("dve", rope1, reads=[r_mm[mi], r_c, r_stg[si]], writes=[r_rt])
                if "norope2" not in os.environ.get("ATT_DBG", ""):
                    S.op("dve", rope2, reads=[r_rt], writes=[r_stg[si]])
                pend.append((tb, si))
                if len(pend) > 1:
                    emit_tr(*pend.pop(0))
            while pend:
                emit_tr(*pend.pop(0))
            vslab, r_vs = load_slab(6 + g)
            for bp in range(2):
                mi = nxt("mm", 2)
                groups = []
                for j in range(2):
                    sel = tok_sel(g, bp * 2 + j)
                    groups.append((mm[mi][:, j * 512:(j + 1) * 512], [(sel(kc), vslab[:, kc, :]) for kc in range(8)], True, True, None))
                pe_groups(groups, r_h + [r_vs], [r_mm[mi]])
                s0 = slot0 + bp * 2
                S.op("act", lambda h, mi=mi, s0=s0: h.copy(out=vr[g][:, s0:s0 + 2, :], in_=mm[mi][:, :].rearrange("p (j c) -> p j c", j=2)),
                     reads=[r_mm[mi]], writes=[r_v[g][s0], r_v[g][s0 + 1]])

        def attention(ci):
            tasks = []
            pvsets = [(pvN[:, :], pvD[:, :], [r_pv, r_pvd]), (mm[1][:, 0:512], mm[1][:, 512:1024], [r_mm[1]])]
            for p in range(4):
                hA, hB = 2 * p, 2 * p + 1
                for g in range(2):
                    pvN_, pvD_, rpv_ = pvsets[(p * 3 + g) % 2]
                    for qb in range(4):
                        n = 4 * ci + qb
                        own_s = n % 8
                        prv = n - 1 if g == 0 else n - 4
                        prv_s = prv % 8
                        halo_prev = prv < 4 * FIRST_OWN
                        qk = []
                        for j, (X, ks) in enumerate([(0, own_s), (1, own_s), (0, prv_s), (1, prv_s)]):
                            qk.append((j * 128, kr[g][:, p, ks, :], X, slice(qb * 128, (qb + 1) * 128)))
                        pv = []
                        qc = slice(qb * 128, (qb + 1) * 128)
                        pv.append((pvN_[0:64, qc], vr[g][:, own_s, hA * 64:(hA + 1) * 64], 0, True, False, (0, 0)))
                        pv.append((pvN_[0:64, qc], vr[g][:, prv_s, hA * 64:(hA + 1) * 64], 256, False, True, (0, 0)))
                        pv.append((pvN_[64:128, qc], vr[g][:, own_s, hB * 64:(hB + 1) * 64], 128, True, False, (0, 64)))
                        pv.append((pvN_[64:128, qc], vr[g][:, prv_s, hB * 64:(hB + 1) * 64], 384, False, True, (0, 64)))
                        pv.append((pvD_[0:64, qc], ones64[:, :], 0, True, False, (0, 0)))
                        pv.append((pvD_[0:64, qc], ones64[:, :], 256, False, True, (0, 0)))
                        pv.append((pvD_[64:128, qc], ones64[:, :], 128, True, False, (0, 64)))
                        pv.append((pvD_[64:128, qc], ones64[:, :], 384, False, True, (0, 64)))
                        tasks.append(dict(p=p, g=g, qk=qk, pv=pv, mask=mtb[:, :],
                                          bias=[(0, 256, False), (256, 512, True)] if halo_prev else [(0, 512, False)],
                                          rk=[r_k[g][own_s], r_k[g][prv_s]], rv=[r_v[g][own_s], r_v[g][prv_s]],
                                          first=(qb == 0), last=(qb == 3), pvs=(pvN_, pvD_, rpv_)))
                g = 2
                pvN_, pvD_, rpv_ = pvsets[(p * 3 + g) % 2]
                for X, (hh, tp, prow) in enumerate([(hA, (0, 0), slice(0, 64)), (hB, (0, 64), slice(64, 128))]):
                    for d in (4, 3, 2, 1, 0):
                        cj = ci - d
                        qk, pv, rk, rv = [], [], [], []
                        for m in range(4):
                            sl = (4 * cj + m) % 20
                            qk.append((m * 128, kr[2][:, p, sl, :], X, slice(m * 128, (m + 1) * 128)))
                            qc = slice(m * 128, (m + 1) * 128)
                            pv.append((pvN_[prow, qc], vr[2][:, sl, hh * 64:(hh + 1) * 64], m * 128, d == 4 and m == 0, d == 0 and m == 3, tp))
                            pv.append((pvD_[prow, qc], ones64[:, :], m * 128, d == 4 and m == 0, d == 0 and m == 3, tp))
                            rk.append(r_k[2][sl])
                            rv.append(r_v[2][sl])
                        mt = 0 if d == 0 else (2 if d == 4 else 1)
                        tasks.append(dict(p=p, g=2, qk=qk, pv=pv, mask=m3[:, mt, :],
                                          bias=[(0, 512, cj < FIRST_OWN)], rk=rk, rv=rv,
                                          first=(X == 0 and d == 4), last=(X == 1 and d == 0), pvs=(pvN_, pvD_, rpv_)))

            import os
            dbgm0 = os.environ.get("ATT_DBG", "")

            def emit_front(t):
                if t["first"]:
                    zs = nxt("qz", 2)
                    state["qz_cur"] = zs
                    g_, p_ = t["g"], t["p"]

                    def zc(h, zs=zs, g_=g_, p_=p_):
                        h.tensor_copy(out=qz[0:64, 0, zs, :], in_=qT[0:64, g_, p_, :])
                        return h.tensor_copy(out=qz[64:128, 1, zs, :], in_=qT[64:128, g_, p_, :])
                    S.op("dve" if ci == FIRST_OWN else "pool", zc, reads=[r_q[g_]], writes=[r_qz[zs]])
                zs = state["qz_cur"]
                si = nxt("sps", 2)
                pi = nxt("pb", 4)
                t["pi"] = pi
                sp = sps[si]
                nq_ = len(t["qk"])
                pe_groups([(sp[:, :], [(ident[:, :], t["mask"])], True, False, None)] +
                          [(sp[:, c0:c0 + 128], [(l, qz[:, X, zs, qc])], False, j_ == nq_ - 1, None)
                           for j_, (c0, l, X, qc) in enumerate(t["qk"])],
                          t["rk"] + [r_qz[zs], r_c], [r_sps[si]])

                def ex(h, t=t, sp=sp, pi=pi):
                    ins = None
                    for (a, b, hb) in t["bias"]:
                        if hb:
                            ins = h.activation(out=pbuf[:, pi, a:b], in_=sp[:, a:b], func=AF.Exp, bias=hbias[:, 0:1], scale=0.125)
                        else:
                            ins = h.activation(out=pbuf[:, pi, a:b], in_=sp[:, a:b], func=AF.Exp, scale=0.125)
                    return ins
                if "noexp" in dbgm0:
                    return
                S.op("act", ex, reads=[r_sps[si], r_c], writes=[r_pb[pi]])
                if "nopool" in dbgm0:
                    return

            def emit_back(t):
                pi = t["pi"]
                pe_groups([(o, [(l, pbuf[:, pi, c0:c0 + 128])], st0, sp0, tp) for (o, l, c0, st0, sp0, tp) in t["pv"]],
                          t["rv"] + [r_pb[pi], r_c], t["pvs"][2])
                if t["last"]:
                    g, p = t["g"], t["p"]
                    pn_, pd_, rp_ = t["pvs"]
                    if g == 0:
                        def ev(h, pn_=pn_, pd_=pd_):
                            h.tensor_copy(out=accN[:, :], in_=pn_)
                            return h.tensor_copy(out=accD[:, :], in_=pd_)
                    else:
                        def ev(h, pn_=pn_, pd_=pd_):
                            av = accN[:, :].rearrange("p (j c) -> p c j", c=4)
                            dv = accD[:, :].rearrange("p (j c) -> p c j", c=4)
                            h.tensor_tensor(out=av, in0=av, in1=pn_.rearrange("p (c j) -> p c j", c=4), op=ALU.add)
                            return h.tensor_tensor(out=dv, in0=dv, in1=pd_.rearrange("p (c j) -> p c j", c=4), op=ALU.add)
                    S.op("dve", ev, reads=rp_, writes=[r_acc])
                    if g == 2:
                        S.op("act", lambda h: h.activation(out=accD[:, :], in_=accD[:, :], func=AF.Ln), reads=[r_acc], writes=[r_acc])
                        S.op("act", lambda h: h.activation(out=accD[:, :], in_=accD[:, :], func=AF.Exp, scale=-1.0), reads=[r_acc], writes=[r_acc])
                        S.op("dve", lambda h, p=p: h.tensor_tensor(out=attnT[:, p, :], in0=accN[:, :], in1=accD[:, :], op=ALU.mult),
                             reads=[r_acc], writes=[r_at[p]])

            import os
            dbgm = os.environ.get("ATT_DBG", "")
            if "g01" in dbgm:
                tasks = [t for t in tasks if t["g"] < 2]
            if "g2" in dbgm:
                tasks = [t for t in tasks if t["g"] == 2]
            if "p0" in dbgm:
                tasks = [t for t in tasks if t["p"] == 0]
            pend_ = []
            for t in tasks:
                emit_front(t)
                pend_.append(t)
                if len(pend_) > 2:
                    emit_back(pend_.pop(0))
                yield
            while pend_:
                emit_back(pend_.pop(0))

        def sgu_pre(ci):
            uslab, r_us = load_slab(9)
            for fp in range(2):
                mi = nxt("mm", 2)
                groups = [(mm[mi][:, j * 512:(j + 1) * 512],
                           [(uslab[:, kc, (fp * 2 + j) * 128:(fp * 2 + j + 1) * 128], hT[:, kc, :]) for kc in range(8)],
                           True, True, None) for j in range(2)]
                pe_groups(groups, r_h + [r_us], [r_mm[mi]])
                S.op("act", lambda h, mi=mi, fp=fp: h.activation(out=uT[:, fp * 2:fp * 2 + 2, :],
                                                                 in_=mm[mi][:, :].rearrange("p (j c) -> p j c", j=2), func=AF.Gelu),
                     reads=[r_mm[mi]], writes=[r_u])
            vslab, r_vs = load_slab(10)
            for tp_ in range(2):
                mi = nxt("mm", 2)
                groups = [(mm[mi][:, j * 512:(j + 1) * 512],
                           [(hT[:, kc, (tp_ * 2 + j) * 128:(tp_ * 2 + j + 1) * 128], vslab[:, kc, :]) for kc in range(8)],
                           True, True, None) for j in range(2)]
                pe_groups(groups, r_h + [r_vs], [r_mm[mi]])
                S.op("act", lambda h, mi=mi, tp_=tp_: h.activation(out=vz[:, tp_ * 2:tp_ * 2 + 2, :],
                                                                   in_=mm[mi][:, :].rearrange("p (j c) -> p j c", j=2), func=AF.Gelu),
                     reads=[r_mm[mi]], writes=[r_vz])

        def sgu_stage(ci):
            for tp_ in range(2):
                def stats1(h, tp_=tp_):
                    h.bn_stats(out=bn6[:, 0, :], in_=vz[:, tp_ * 2, :])
                    return h.bn_stats(out=bn6[:, 1, :], in_=vz[:, tp_ * 2 + 1, :])

                def stats2(h):
                    h.bn_aggr(out=mv[:, 0, :], in_=bn6[:, 0, :])
                    return h.bn_aggr(out=mv[:, 1, :], in_=bn6[:, 1, :])
                S.op("dve", stats1, reads=[r_vz], writes=[r_ln])
                S.op("dve", stats2, reads=[r_ln], writes=[r_ln])
                if ci == FIRST_OWN:
                    S.op("act", lambda h: h.activation(out=lsd[:, 0:2], in_=mv[:, :, 1], func=AF.Sqrt, scale=1.0, bias=epsb[:, 0:1]),
                         reads=[r_ln, r_c], writes=[r_ln])
                    S.op("dve", lambda h: h.reciprocal(out=lrs[:, 0:2], in_=lsd[:, 0:2]), reads=[r_ln], writes=[r_ln])
                else:
                    S.op("dve", lambda h: h.tensor_scalar(out=lsd[:, 0:2], in0=mv[:, :, 1], scalar1=EPS, scalar2=None, op0=ALU.add),
                         reads=[r_ln], writes=[r_ln])
                    S.op("pool", lambda h: h.tensor_tensor(out=lrs[:, 0:2], in0=lsd[:, 0:2], in1=nhalf[:, 0:2], op=ALU.pow),
                         reads=[r_ln, r_c], writes=[r_ln])
                yield
                for j in range(2):
                    tb = tp_ * 2 + j
                    vi = nxt("vn", 2)
                    S.op("dve", lambda h, j=j, tb=tb: h.tensor_scalar(out=vt, in0=vz[:, tb, :], scalar1=mv[:, j, 0:1], scalar2=lrs[:, j:j + 1],
                                                                      op0=ALU.subtract, op1=ALU.mult),
                         reads=[r_vz, r_ln], writes=[r_vt])
                    S.op("dve", lambda h: h.tensor_tensor(out=vt, in0=vt, in1=lng[:, :], op=ALU.mult), reads=[r_vt, r_c], writes=[r_vt])

                    def nrm3(h, vi=vi):
                        v4 = vt.rearrange("p (g e c) -> p g e c", g=4, e=2)
                        b4 = lnb[:, :].rearrange("p (g e c) -> p g e c", g=4, e=2)
                        a4 = vnA[:, vi, :].rearrange("p (g e c) -> p g e c", g=4, e=2)
                        c4 = vnB[:, vi, :].rearrange("p (g e c) -> p g e c", g=4, e=2)
                        h.tensor_tensor(out=a4[:, :, 0, :], in0=v4[:, :, 0, :], in1=b4[:, :, 0, :], op=ALU.add)
                        return h.tensor_tensor(out=c4[:, :, 1, :], in0=v4[:, :, 1, :], in1=b4[:, :, 1, :], op=ALU.add)
                    S.op("dve", nrm3, reads=[r_vt, r_c], writes=[r_vn[vi]])
                    si = 0
                    yield
                    groups = [(mm[si][:, ft * 128:(ft + 1) * 128],
                               [(vnA[:, vi, ft * 128:(ft + 1) * 128], WcT[:, 2 * ft, :]),
                                (vnB[:, vi, ft * 128:(ft + 1) * 128], WcT[:, 2 * ft + 1, :])], True, True, None) for ft in range(4)]
                    pe_groups(groups, [r_vn[vi], r_c], [r_mm[si]])
                    S.op("dve", lambda h, si=si: h.tensor_tensor(out=sgt, in0=mm[si][:, 0:512], in1=bsp[:, :], op=ALU.add),
                         reads=[r_mm[si], r_c], writes=[r_sgt])
                    S.op("dve", lambda h, tb=tb: h.tensor_tensor(out=sguT[:, :, tb * 128:(tb + 1) * 128],
                                                                 in0=sgt.rearrange("p (f t) -> p f t", f=4),
                                                                 in1=uT[:, :, tb * 128:(tb + 1) * 128], op=ALU.mult),
                         reads=[r_sgt, r_u], writes=[r_sg])

        def merge_stage(ci):
            xbuf = xbuf2[:, ci % 2]
            r_x = r_xb[ci % 2]
            n2 = norm_gen(g2c, ci % 2, ci == FIRST_OWN)
            S.fence(r_q + [r_u], r_mg + r_gb)
            S.fence([r_vz, r_vt, r_sgt], [r_t1])
            for hs in range(2):
                gslab, r_gs = load_slab(11 + hs)
                for tq in range(2):
                    tp_ = hs * 2 + tq
                    mi = nxt("mm", 2)
                    groups = [(mm[mi][:, j * 512:(j + 1) * 512],
                               [(gslab[:, kc, (tq * 2 + j) * 128:(tq * 2 + j + 1) * 128], hT[:, kc, :]) for kc in range(8)],
                               True, True, None) for j in range(2)]
                    pe_groups(groups, r_h + [r_gs], [r_mm[mi]])
                    S.op("act", lambda h, mi=mi, tp_=tp_: h.activation(out=mergedT[:, tp_ * 2:tp_ * 2 + 2, :],
                                                                       in_=mm[mi][:, :].rearrange("p (j c) -> p j c", j=2), func=AF.Sigmoid),
                         reads=[r_mm[mi]], writes=[r_mg[tp_]])
            paslab, r_pa = load_slab(15)
            for tp_ in range(4):
                mi = nxt("mm", 2)
                groups = [(mm[mi][:, j * 512:(j + 1) * 512],
                           [(paslab[:, kc, (tp_ * 2 + j) * 128:(tp_ * 2 + j + 1) * 128], attnT[:, kc, :]) for kc in range(4)],
                           True, True, None) for j in range(2)]
                pe_groups(groups, r_at + [r_pa], [r_mm[mi]])
                S.op("dve", lambda h, mi=mi, tp_=tp_: h.tensor_tensor(out=mergedT[:, tp_ * 2:tp_ * 2 + 2, :],
                                                                      in0=mm[mi][:, :].rearrange("p (j c) -> p j c", j=2),
                                                                      in1=mergedT[:, tp_ * 2:tp_ * 2 + 2, :], op=ALU.mult),
                     reads=[r_mm[mi], r_mg[tp_]], writes=[r_mg[tp_]])
            for hs in range(2):
                gslab, r_gs = load_slab(13 + hs)
                for tq in range(2):
                    tp_ = hs * 2 + tq
                    mi = nxt("mm", 2)
                    groups = [(mm[mi][:, j * 512:(j + 1) * 512],
                               [(gslab[:, kc, (tq * 2 + j) * 128:(tq * 2 + j + 1) * 128], hT[:, kc, :]) for kc in range(8)],
                               True, True, None) for j in range(2)]
                    pe_groups(groups, r_h + [r_gs], [r_mm[mi]])
                    S.op("act", lambda h, mi=mi, tp_=tp_: h.activation(out=gbT[:, tp_ * 2:tp_ * 2 + 2, :],
                                                                       in_=mm[mi][:, :].rearrange("p (j c) -> p j c", j=2), func=AF.Sigmoid),
                         reads=[r_mm[mi]], writes=[r_gb[tp_]])
            pbslab, r_pbs = load_slab(16)
            for tp_ in range(4):
                mi = nxt("mm", 2)
                groups = [(mm[mi][:, j * 512:(j + 1) * 512],
                           [(pbslab[:, kc, (tp_ * 2 + j) * 128:(tp_ * 2 + j + 1) * 128], sguT[:, kc, :]) for kc in range(4)],
                           True, True, None) for j in range(2)]
                pe_groups(groups, [r_sg, r_pbs], [r_mm[mi]])

                S.op("dve", lambda h, mi=mi, tp_=tp_: h.tensor_tensor(
                    out=t1, in0=mm[mi][:, :], in1=gbT[:, tp_ * 2:tp_ * 2 + 2, :].rearrange("p a b -> p (a b)"), op=ALU.mult),
                    reads=[r_mm[mi], r_gb[tp_]], writes=[r_t1])
                S.op("dve", lambda h, tp_=tp_: h.tensor_tensor(
                    out=mergedT[:, tp_ * 2:tp_ * 2 + 2, :].rearrange("p a b -> p (a b)"), in0=t1,
                    in1=mergedT[:, tp_ * 2:tp_ * 2 + 2, :].rearrange("p a b -> p (a b)"), op=ALU.add),
                    reads=[r_t1, r_mg[tp_]], writes=[r_mg[tp_]])
            oslabs = [load_slab(17), load_slab(18)]
            for tp_ in range(2):
                for hf in range(2):
                    oslab, r_os = oslabs[hf]
                    mi = nxt("mm", 2)
                    groups = [(mm[mi][:, j * 512:(j + 1) * 512],
                               [(mergedT[:, kc, (tp_ * 2 + j) * 128:(tp_ * 2 + j + 1) * 128], oslab[:, kc, :]) for kc in range(8)],
                               True, True, None) for j in range(2)]
                    pe_groups(groups, r_mg + [r_os], [r_mm[mi]])
                    S.op("dve", lambda h, mi=mi, tp_=tp_, hf=hf: h.tensor_tensor(
                        out=xbuf[:, tp_ * 2:tp_ * 2 + 2, hf * 512:(hf + 1) * 512],
                        in0=xbuf[:, tp_ * 2:tp_ * 2 + 2, hf * 512:(hf + 1) * 512],
                        in1=mm[mi][:, :].rearrange("p (j c) -> p j c", j=2), op=ALU.add),
                        reads=[r_mm[mi]], writes=[r_x[tp_ * 2], r_x[tp_ * 2 + 1]])
                    if tp_ == 0 and hf == 1:
                        next(n2, None)
                        next(n2, None)
                    if tp_ == 1 and hf == 0:
                        next(n2, None)
            for _ in n2:
                pass

        def ffn_stage(ci, nxt_norm):
            xbuf = xbuf2[:, ci % 2]
            r_x = r_xb[ci % 2]
            S.fence(r_q + [r_u, r_sg] + r_at + r_mg + r_gb, r_ff)
            S.fence(r_stg, r_stg)
            for s in range(6):
                gs, r_gs = load_slab(19 + s)
                us, r_us = load_slab(25 + s)
                ntp = 2 if s < 5 else 1
                gl = []
                for tq in range(ntp):
                    mg_ = nxt("mm", 2)
                    groups = [(mm[mg_][:, j * 512:(j + 1) * 512],
                               [(gs[:, kc, (tq * 2 + j) * 128:(tq * 2 + j + 1) * 128], hT[:, kc, :]) for kc in range(8)],
                               True, True, None) for j in range(2)]
                    pe_groups(groups, r_h + [r_gs], [r_mm[mg_]])
                    si = nxt("stg", 2)
                    S.op("act", lambda h, mg_=mg_, si=si: h.activation(out=sgl[:, si, :], in_=mm[mg_][:, :], func=AF.Silu),
                         reads=[r_mm[mg_]], writes=[r_stg[si]])
                    gl.append(si)
                for tq in range(ntp):
                    fi = s * 2 + tq
                    si = gl[tq]
                    mu_ = nxt("mm", 2)
                    groups = [(mm[mu_][:, j * 512:(j + 1) * 512],
                               [(us[:, kc, (tq * 2 + j) * 128:(tq * 2 + j + 1) * 128], hT[:, kc, :]) for kc in range(8)],
                               True, True, None) for j in range(2)]
                    pe_groups(groups, r_h + [r_us], [r_mm[mu_]])
                    S.op("dve", lambda h, mu_=mu_, si=si, fi=fi: h.tensor_tensor(
                        out=ffT[:, fi * 2:fi * 2 + 2, :].rearrange("p a b -> p (a b)"), in0=sgl[:, si, :], in1=mm[mu_][:, :], op=ALU.mult),
                        reads=[r_mm[mu_], r_stg[si]], writes=[r_ff[fi]])
            for s in range(6):
                ds, r_ds = load_slab(31 + s)
                nk = 4 if s < 5 else 2
                for tb in range(4):
                    mi = nxt("mm", 2)
                    groups = [(mm[mi][:, hf * 512:(hf + 1) * 512],
                               [(ffT[:, s * 4 + kl, tb * 128:(tb + 1) * 128], ds[:, kl, hf * 512:(hf + 1) * 512]) for kl in range(nk)],
                               True, True, None) for hf in range(2)]
                    pe_groups(groups, [r_ff[s * 2 + kl // 2] for kl in range(0, nk, 2)] + [r_ds], [r_mm[mi]])
                    S.op("dve", lambda h, mi=mi, tb=tb: h.tensor_tensor(out=xbuf[:, tb, :], in0=xbuf[:, tb, :], in1=mm[mi][:, :], op=ALU.add),
                         reads=[r_mm[mi]], writes=[r_x[tb]])
                    if nxt_norm is not None and (s * 4 + tb) % 2 == 1:
                        next(nxt_norm, None)
            if nxt_norm is not None:
                for _ in nxt_norm:
                    pass
            for tb in range(4):
                sl = nxt("xs", 2)
                S.op("act", lambda h, tb=tb, sl=sl: h.activation(out=xs[:, sl, :], in_=xbuf[:, tb, :], func=AF.Square,
                                                                   accum_out=ss[:, tb:tb + 1]),
                     reads=[r_x[tb]], writes=[r_xs[sl], r_small[tb]])
                emit_rstd(tb, ci == FIRST_OWN)
                S.op("dve", lambda h, tb=tb: h.scalar_tensor_tensor(out=xbuf[:, tb, :], in0=xbuf[:, tb, :], scalar=rstd[:, tb:tb + 1],
                                                                    in1=gfin[:, :], op0=ALU.mult, op1=ALU.mult),
                     reads=[r_x[tb], r_small[tb], r_c], writes=[r_x[tb]])
                r0 = (ci - FIRST_OWN) * T + tb * 128
                S.dma("pool", f"st{tb}", y[r0:r0 + 128, :], xbuf[:, tb, :], reads=[r_x[tb]])

        load_x(0)
        norm_to_hT(g1c, 0, True)
        for ci in range(nchunks):
            nn = None
            own = ci >= FIRST_OWN
            if ci + 1 < nchunks:
                if not own:
                    load_x(ci + 1)
                nn = norm_gen(g1c, (ci + 1) % 2, ci + 1 <= FIRST_OWN)
            if ci == FIRST_OWN:
                tap("hT", hT[:, :, :], r_h)
            if not own:
                qkv_stage(ci, 2, False)
                if ci == FIRST_OWN - 1:
                    qkv_stage(ci, 0, False)
                    qkv_stage(ci, 1, False)
                if nn is not None:
                    for _ in nn:
                        pass
                continue
            S.fence(r_ff + r_mg + r_gb, r_q + [r_u, r_sg] + r_at)
            S.fence([r_t1], [r_vz, r_vt, r_sgt])
            for g in range(3):
                qkv_stage(ci, g, True)
            if ci == FIRST_OWN:
                tap("qT", qT, r_q)
                for g in range(3):
                    tap(f"kr{g}", kr[g][:, :, :, :], r_k[g])
                    tap(f"vr{g}", vr[g][:, :, :], r_v[g])
            sgu_pre(ci)
            ga_ = attention(ci)
            gs_ = sgu_stage(ci)
            k_ = 0
            for _ in ga_:
                k_ += 1
                if k_ % 9 == 4:
                    next(gs_, None)
            for _ in gs_:
                pass
            if ci == FIRST_OWN:
                tap("attnT", attnT, r_at)
                tap("sguT", sguT, [r_sg])
                tap("uT", uT, [r_u])
            if ci + 1 < nchunks:
                load_x(ci + 1)
            merge_stage(ci)
            if ci == FIRST_OWN:
                tap("mergedT", mergedT, r_mg)
                tap("x1", xbuf2[:, ci % 2], r_xb[ci % 2])
            ffn_stage(ci, nn)

        fin = [(f"st{i}", S.cnt[f"st{i}"]) for i in range(4)] + [("tap", S.cnt["tap"])]
        S.wait_all("sp", fin)
        S.emit()
    return nc


def make_in_maps(x, positions, norm1_g, w_in, sgu_ln_g, sgu_ln_b, w_spatial, b_spatial, w_proj_attn, w_proj_sgu,
                 w_out, norm2_g, w_ffn_gate, w_ffn_up, w_ffn_down, final_g):
    f32 = np.float32
    x = np.asarray(x, f32)
    positions = np.asarray(positions, np.int32)
    shared = {
        "w_in": np.ascontiguousarray(np.asarray(w_in, f32)[0]),
        "wpa": np.ascontiguousarray(np.asarray(w_proj_attn, f32)[0]),
        "wps": np.ascontiguousarray(np.asarray(w_proj_sgu, f32)[0]),
        "wout": np.ascontiguousarray(np.asarray(w_out, f32)[0]),
        "wg": np.ascontiguousarray(np.asarray(w_ffn_gate, f32)[0]),
        "wu": np.ascontiguousarray(np.asarray(w_ffn_up, f32)[0]),
        "wd": np.ascontiguousarray(np.asarray(w_ffn_down, f32)[0]),
        "g1c": np.ascontiguousarray(np.asarray(norm1_g, f32)[0].reshape(8, 128).T),
        "g2c": np.ascontiguousarray(np.asarray(norm2_g, f32)[0].reshape(8, 128).T),
        "gfin": np.ascontiguousarray(np.broadcast_to(np.asarray(final_g, f32)[None, :], (128, D))),
        "lng": np.ascontiguousarray(np.broadcast_to(np.asarray(sgu_ln_g, f32)[0][None, :], (128, 512))),
        "lnb": np.ascontiguousarray(np.broadcast_to(np.asarray(sgu_ln_b, f32)[0][None, :], (128, 512))),
        "bsp": np.ascontiguousarray(np.repeat(np.asarray(b_spatial, f32)[0].reshape(4, 2, 128), 64, axis=1)
                                    .transpose(1, 0, 2).reshape(128, 512)),
        "wsp": np.ascontiguousarray(np.asarray(w_spatial, f32)[0].transpose(2, 0, 1).reshape(128, 1024)),
    }
    in_maps = []
    for core in range(NCORE):
        b, half = core // 2, core % 2
        s0 = half * OWN
        xin = np.zeros((OWN + HALO, D), f32)
        pin = np.zeros((OWN + HALO,), np.int32)
        xin[HALO:] = x[b, s0:s0 + OWN]
        pin[HALO:] = positions[b, s0:s0 + OWN]
        if half == 1:
            xin[:HALO] = x[b, s0 - HALO:s0]
            pin[:HALO] = positions[b, s0 - HALO:s0]
        m = dict(shared)
        m["xin"] = xin
        m["pos"] = np.ascontiguousarray(pin.reshape(48, 128).T)
        m["hbias"] = np.full((128, 1), 0.0 if half == 1 else NEG, f32)
        in_maps.append(m)
    return in_maps


_NC_CACHE = {}


def kernel(x, positions, norm1_g, w_in, sgu_ln_g, sgu_ln_b, w_spatial, b_spatial, w_proj_attn, w_proj_sgu,
           w_out, norm2_g, w_ffn_gate, w_ffn_up, w_ffn_down, final_g):
    in_maps = make_in_maps(x, positions, norm1_g, w_in, sgu_ln_g, sgu_ln_b, w_spatial, b_spatial, w_proj_attn,
                           w_proj_sgu, w_out, norm2_g, w_ffn_gate, w_ffn_up, w_ffn_down, final_g)
    if "nc" not in _NC_CACHE:
        _NC_CACHE["nc"] = build_program()
    nc = _NC_CACHE["nc"]
    res = run_bass_kernel_spmd(nc, in_maps, core_ids=list(range(NCORE)))
    out = np.zeros((4, SEQ, D), np.float32)
    for core in range(NCORE):
        b, half = core // 2, core % 2
        out[b, half * OWN:(half + 1) * OWN] = np.asarray(res.results[core]["y"], np.float32)
    return out
```
